# Optimizing a Trainium2 kernel written in Bass

```python
import math
import jax, jax.numpy as jnp
from jax import lax
import numpy as np

D_MODEL = 1024
BATCH = 8
SEQ = 2048
DEPTH = 2

GRID_W = 64
CTX_LEN = 256
N_EVEN = (DEPTH + 1) // 2
N_ODD = DEPTH // 2
MIX_W = D_MODEL
EPS = 1e-6
DA_HEADS = 4
DA_V_DIM = (MIX_W // 2) // DA_HEADS
DA_QK_DIM = DA_V_DIM // 2
DA_QK_W = DA_HEADS * 2 * DA_QK_DIM
DA_V_W = DA_HEADS * DA_V_DIM
Q_BLOCK = 128
ROPE_THETA = 10000.0
POOL_WINDOWS = (2, 4, 8, 16)
POOL_W = MIX_W // 2
POOL_GDIM = POOL_W // len(POOL_WINDOWS)
EV_IN_W = 2 * DA_QK_W + DA_V_W + POOL_W
HG_HEAD_DIM = 128
HG_HEADS = MIX_W // HG_HEAD_DIM
HG_CHUNK = 64
HG_IN_W = 5 * MIX_W
D_FF = 2816
CONV_W = 3

kernel_name = "hybrid_diffattn_pool_hgrn2_dit_block"


def rmsnorm(x):
    x32 = x.astype(jnp.float32)
    y = x32 * lax.rsqrt(jnp.mean(x32 * x32, axis=-1, keepdims=True) + EPS)
    return y.astype(x.dtype)


def modulate(h, shift, scale):
    return h * (1.0 + scale) + shift


def axial_rope_tables(rows_n):
    row = jnp.repeat(jnp.arange(rows_n, dtype=jnp.float32), GRID_W)
    col = jnp.tile(jnp.arange(GRID_W, dtype=jnp.float32), rows_n)
    n_freq = DA_QK_DIM // 4
    inv = ROPE_THETA ** (-jnp.arange(n_freq, dtype=jnp.float32) / n_freq)
    ang = jnp.stack([row[:, None] * inv, col[:, None] * inv], axis=1)
    return jnp.cos(ang), jnp.sin(ang)


def apply_axial_rope(x, cos, sin):
    xr = x.reshape(x.shape[:-1] + (2, 2, DA_QK_DIM // 4))
    x1, x2 = xr[..., 0, :], xr[..., 1, :]
    c = cos[:, None, None].astype(x.dtype)
    s = sin[:, None, None].astype(x.dtype)
    out = jnp.stack([x1 * c - x2 * s, x2 * c + x1 * s], axis=-2)
    return out.reshape(x.shape)


def diff_attention(q, k, v, lam):
    B, Tq, H, M, dq = q.shape
    dv = v.shape[-1]
    nb = Tq // Q_BLOCK
    qb = q.reshape(B, nb, Q_BLOCK, H, M, dq).transpose(1, 0, 2, 3, 4, 5)
    scale = dq ** -0.5

    def one_block(qblk):
        s = jnp.einsum('bqhmd,bkhmd->bhmqk', qblk, k, preferred_element_type=jnp.float32) * scale
        p = jax.nn.softmax(s, axis=-1)
        a = p[:, :, 0] - lam * p[:, :, 1]
        return jnp.einsum('bhqk,bkhd->bqhd', a.astype(v.dtype), v)

    o = lax.map(one_block, qb)
    return o.transpose(1, 0, 2, 3, 4).reshape(B, Tq, H, dv)


def centred_mean_minus_self(u, w):
    B, T, C = u.shape
    cs = jnp.concatenate([jnp.zeros((B, 1, C), jnp.float32), jnp.cumsum(u.astype(jnp.float32), axis=1)], axis=1)
    t = jnp.arange(T)
    lo = jnp.clip(t - w // 2, 0, T)
    hi = jnp.clip(t + w - w // 2, 0, T)
    cnt = (hi - lo).astype(jnp.float32)
    mean = (cs[:, hi] - cs[:, lo]) / cnt[None, :, None]
    return mean.astype(u.dtype) - u


def pool_mixer(u, pool_w, pool_scale):
    B, T, _ = u.shape
    d = jnp.stack([centred_mean_minus_self(u[..., g * POOL_GDIM:(g + 1) * POOL_GDIM], w)
                   for g, w in enumerate(POOL_WINDOWS)], axis=2)
    y = jnp.einsum('btgc,gcd->btgd', d, pool_w).reshape(B, T, POOL_W)
    return y * pool_scale


def even_mixer(h_ctx, h_lat, w_in, w_out, lq1, lk1, lq2, lk2, subln, pool_w, pool_scale,
               lambda_init, cos, sin, need_ctx):
    def project(h):
        B, T, _ = h.shape
        p = h @ w_in
        q = p[..., :DA_QK_W].reshape(B, T, DA_HEADS, 2, DA_QK_DIM)
        k = p[..., DA_QK_W:2 * DA_QK_W].reshape(B, T, DA_HEADS, 2, DA_QK_DIM)
        v = p[..., 2 * DA_QK_W:2 * DA_QK_W + DA_V_W].reshape(B, T, DA_HEADS, DA_V_DIM)
        u = p[..., 2 * DA_QK_W + DA_V_W:]
        return q, k, v, u

    qc, kc, vc, uc = project(h_ctx)
    ql, kl, vl, ul = project(h_lat)
    ql = apply_axial_rope(ql, cos, sin)
    kl = apply_axial_rope(kl, cos, sin)
    lam = (jnp.exp(jnp.sum(lq1.astype(jnp.float32) * lk1.astype(jnp.float32)))
           - jnp.exp(jnp.sum(lq2.astype(jnp.float32) * lk2.astype(jnp.float32))) + lambda_init)
    gain = subln.reshape(DA_HEADS, DA_V_DIM) * (1.0 - lambda_init)

    def readout(o, u):
        B, T = o.shape[:2]
        o = (rmsnorm(o) * gain).reshape(B, T, DA_V_W)
        return jnp.concatenate([o, pool_mixer(u, pool_w, pool_scale)], axis=-1) @ w_out

    k_all = jnp.concatenate([kc, kl], axis=1)
    v_all = jnp.concatenate([vc, vl], axis=1)
    y_lat = readout(diff_attention(ql, k_all, v_all, lam), ul)
    y_ctx = readout(diff_attention(qc, kc, vc, lam), uc) if need_ctx else None
    return y_ctx, y_lat


def hgrn2_chunk_scan(q, logf, k, v, s0):
    B, T, H, dk = q.shape
    dv = v.shape[-1]
    n = T // HG_CHUNK

    def to_chunks(a):
        return a.reshape(B, n, HG_CHUNK, H, a.shape[-1]).transpose(1, 0, 3, 2, 4)

    mask = jnp.tril(jnp.ones((HG_CHUNK, HG_CHUNK), dtype=bool))[None, None, :, :, None]

    def step(S, xs):
        qc, gc, kc, vc = xs
        b = jnp.cumsum(gc, axis=2)
        diff = b[:, :, :, None, :] - b[:, :, None, :, :]
        decay = jnp.exp(jnp.where(mask, diff, -jnp.inf))
        A = jnp.einsum('bhtc,bhtsc,bhsc->bhts', qc, decay, kc)
        o = jnp.einsum('bhts,bhsd->bhtd', A, vc) + jnp.einsum('bhtc,bhcd->bhtd', qc * jnp.exp(b), S)
        b_last = b[:, :, -1:, :]
        S_new = (jnp.exp(b_last[:, :, 0, :])[..., None] * S
                 + jnp.einsum('bhsc,bhsd->bhcd', kc * jnp.exp(b_last - b), vc))
        return S_new, o

    S_fin, o = lax.scan(step, s0, (to_chunks(q), to_chunks(logf), to_chunks(k), to_chunks(v)))
    o = o.transpose(1, 0, 3, 2, 4).reshape(B, T, H, dv)
    return o, S_fin


def hgrn2_direction(q, z, v, lb, s0, reverse):
    f = lb + (1.0 - lb) * jax.nn.sigmoid(z.astype(jnp.float32))
    k = 1.0 - f
    logf = jnp.log(f)
    xs = (q.astype(jnp.float32), logf, k, v.astype(jnp.float32))
    if reverse:
        xs = tuple(jnp.flip(a, axis=1) for a in xs)
    o, s = hgrn2_chunk_scan(xs[0], xs[1], xs[2], xs[3], s0)
    if reverse:
        o = jnp.flip(o, axis=1)
    return o.astype(v.dtype), s


def odd_mixer(h_ctx, h_lat, w_in, w_out, lb_f, lb_b, norm_g, need_ctx):
    def project(h):
        B, T, _ = h.shape
        p = (h @ w_in).reshape(B, T, 5, HG_HEADS, HG_HEAD_DIM)
        return jax.nn.silu(p[:, :, 0]), p[:, :, 1], p[:, :, 2], p[:, :, 3], p[:, :, 4]

    qc, vc, zfc, zbc, gc = project(h_ctx)
    ql, vl, zfl, zbl, gl = project(h_lat)
    lbf = lb_f.reshape(HG_HEADS, HG_HEAD_DIM)
    lbb = lb_b.reshape(HG_HEADS, HG_HEAD_DIM)
    B = h_lat.shape[0]
    s0 = jnp.zeros((B, HG_HEADS, HG_HEAD_DIM, HG_HEAD_DIM), jnp.float32)
    oc_f, s_f = hgrn2_direction(qc, zfc, vc, lbf, s0, False)
    oc_b, s_b = hgrn2_direction(qc, zbc, vc, lbb, s0, True)
    ol_f, _ = hgrn2_direction(ql, zfl, vl, lbf, s_f, False)
    ol_b, _ = hgrn2_direction(ql, zbl, vl, lbb, s_b, True)
    gain = norm_g.reshape(HG_HEADS, HG_HEAD_DIM)

    def readout(o, g):
        Bq, T = o.shape[:2]
        o = rmsnorm(o) * gain * jax.nn.silu(g)
        return o.reshape(Bq, T, MIX_W) @ w_out

    y_lat = readout(ol_f + ol_b, gl)
    y_ctx = readout(oc_f + oc_b, gc) if need_ctx else None
    return y_ctx, y_lat


def conv_ffn(h, w_up, conv_w, conv_b, w_down):
    u = h @ w_up
    up = jnp.pad(u, ((0, 0), (1, 1), (0, 0)))
    u = up[:, :-2] * conv_w[0] + up[:, 1:-1] * conv_w[1] + up[:, 2:] * conv_w[2] + conv_b
    a, gate = jnp.split(u, 2, axis=-1)
    return (a * jax.nn.silu(gate)) @ w_down


def setup_inputs(seed: int = 0) -> dict:
    key = jax.random.key(seed)
    ks = jax.random.split(key, 26)
    f32 = jnp.float32

    def nrm(k, shape, scale):
        return jax.random.normal(k, shape, f32) * scale

    D = D_MODEL
    return {
        "x": nrm(ks[0], (BATCH, SEQ, D), 1.0),
        "c": nrm(ks[1], (BATCH, D), 1.0),
        "ctx": nrm(ks[2], (BATCH, CTX_LEN, D), 1.0),
        "c_ctx": nrm(ks[3], (D,), 1.0),
        "w_mod": nrm(ks[4], (DEPTH, D, 6 * D), 0.5 * D ** -0.5),
        "b_mod": nrm(ks[5], (DEPTH, 6 * D), 0.02),
        "ev_w_in": nrm(ks[6], (N_EVEN, D, EV_IN_W), D ** -0.5),
        "ev_w_out": nrm(ks[7], (N_EVEN, DA_V_W + POOL_W, D), (DA_V_W + POOL_W) ** -0.5),
        "da_lq1": nrm(ks[8], (N_EVEN, DA_QK_DIM), 0.1),
        "da_lk1": nrm(ks[9], (N_EVEN, DA_QK_DIM), 0.1),
        "da_lq2": nrm(ks[10], (N_EVEN, DA_QK_DIM), 0.1),
        "da_lk2": nrm(ks[11], (N_EVEN, DA_QK_DIM), 0.1),
        "da_subln": 1.0 + nrm(ks[12], (N_EVEN, DA_V_W), 0.02),
        "pool_w": nrm(ks[13], (N_EVEN, len(POOL_WINDOWS), POOL_GDIM, POOL_GDIM), POOL_GDIM ** -0.5),
        "pool_scale": 1.0 + nrm(ks[14], (N_EVEN, POOL_W), 0.02),
        "hg_w_in": nrm(ks[15], (N_ODD, D, HG_IN_W), D ** -0.5),
        "hg_w_out": nrm(ks[16], (N_ODD, MIX_W, D), MIX_W ** -0.5),
        "hg_lb_logits": nrm(ks[17], (2, DEPTH, MIX_W), 0.5),
        "hg_norm": 1.0 + nrm(ks[18], (N_ODD, MIX_W), 0.02),
        "ffn_w_up": nrm(ks[19], (DEPTH, D, 2 * D_FF), D ** -0.5),
        "ffn_conv_w": nrm(ks[20], (DEPTH, CONV_W, 2 * D_FF), CONV_W ** -0.5),
        "ffn_conv_b": nrm(ks[21], (DEPTH, 2 * D_FF), 0.02),
        "ffn_w_down": nrm(ks[22], (DEPTH, D_FF, D), D_FF ** -0.5),
        "final_norm": 1.0 + nrm(ks[23], (D,), 0.02),
    }


def reference(x, c, ctx, c_ctx, w_mod, b_mod, ev_w_in, ev_w_out, da_lq1, da_lk1, da_lq2, da_lk2,
              da_subln, pool_w, pool_scale, hg_w_in, hg_w_out, hg_lb_logits, hg_norm,
              ffn_w_up, ffn_conv_w, ffn_conv_b, ffn_w_down, final_norm):
    B, T, D = x.shape
    ROWS = T // GRID_W
    cos, sin = axial_rope_tables(ROWS)
    lb_p = jax.nn.softmax(hg_lb_logits.astype(jnp.float32), axis=1)
    lb = jnp.cumsum(lb_p, axis=1)
    lb = lb - lb[:, :1]
    xc = ctx
    for i in range(DEPTH):
        need_ctx = i < DEPTH - 1
        m_lat = (jax.nn.silu(c) @ w_mod[i] + b_mod[i]).reshape(B, 6, 1, D)
        m_ctx = (jax.nn.silu(c_ctx) @ w_mod[i] + b_mod[i]).reshape(6, 1, 1, D)
        h_lat = modulate(rmsnorm(x), m_lat[:, 0], m_lat[:, 1])
        h_ctx = modulate(rmsnorm(xc), m_ctx[0], m_ctx[1])
        j = i // 2
        if i % 2 == 0:
            lambda_init = 0.8 - 0.6 * math.exp(-0.3 * i)
            y_ctx, y_lat = even_mixer(h_ctx, h_lat, ev_w_in[j], ev_w_out[j], da_lq1[j], da_lk1[j],
                                      da_lq2[j], da_lk2[j], da_subln[j], pool_w[j], pool_scale[j],
                                      lambda_init, cos, sin, need_ctx)
        else:
            y_ctx, y_lat = odd_mixer(h_ctx, h_lat, hg_w_in[j], hg_w_out[j], lb[0, i], lb[1, i],
                                     hg_norm[j], need_ctx)
        x = x + m_lat[:, 2] * y_lat
        x = x + m_lat[:, 5] * conv_ffn(modulate(rmsnorm(x), m_lat[:, 3], m_lat[:, 4]),
                                       ffn_w_up[i], ffn_conv_w[i], ffn_conv_b[i], ffn_w_down[i])
        if need_ctx:
            xc = xc + m_ctx[2] * y_ctx
            xc = xc + m_ctx[5] * conv_ffn(modulate(rmsnorm(xc), m_ctx[3], m_ctx[4]),
                                          ffn_w_up[i], ffn_conv_w[i], ffn_conv_b[i], ffn_w_down[i])
    return rmsnorm(x) * final_norm
```

```python
import math
from contextlib import ExitStack

import numpy as np
import concourse.bass as bass
import concourse.mybir as mybir
from concourse.bass_utils import run_bass_kernel_spmd

F32 = mybir.dt.float32
BF16 = mybir.dt.bfloat16
AF = mybir.ActivationFunctionType
ALU = mybir.AluOpType

NCORES = 8
D = 1024
KC = 8
NCTX = 256
NLAT = 2048
NT = NCTX + NLAT
DFF = 2816
NPAIR = DFF // 128
EPS = 1e-6
GRID_W = 64
SLOT_ELEMS = 5120
NSLOT = 2

_c = {}
_off = 0


def _cdef(name, n):
    global _off
    _c[name] = (_off, n)
    _off += n


_cdef("cT", 16)
_cdef("bmod", 2 * 96)
_cdef("lqk", 4 * 64)
_cdef("subln", 4)
_cdef("pscale", 4)
_cdef("hgnorm", 8)
_cdef("fnorm", 8)
_cdef("convw", 2 * 44 * 3)
_cdef("convb", 2 * 44)
_cdef("lblog", 2 * 2 * 8)
_cdef("pedge", 4 * 4 * 8)
_cdef("maskf", 64)
_cdef("maskb", 64)
NCONST = _off


def CS(name, a=0, n=None):
    o, sz = _c[name]
    if n is None:
        n = sz - a
    return slice(o + a, o + a + n)


class Prog:
    ROT = 20000

    def __init__(self, nc, es):
        self.nc = nc
        self.es = es
        self.eng = {"pe": nc.tensor, "act": nc.scalar, "dve": nc.vector,
                    "pool": nc.gpsimd, "sp": nc.sync}
        self.sem = {}
        self.cnt = {}
        self.nsem = 0
        for e in self.eng:
            self.sem[e] = self._newsem(e)
            self.cnt[e] = 0
        self.known = {e: {} for e in self.eng}
        self.res = {}
        self.dq = {}
        for q in ("sp", "pool", "act"):
            self.dq[q] = {"sems": [[self._newsem("d" + q), 0] for _ in range(8)], "rr": 0}
        self.out_tokens = []
        self.active = {}
        self.retired = set()

    def _newsem(self, tag):
        self.nsem += 1
        return self.es.enter_context(self.nc.semaphore(f"s_{tag}_{self.nsem}"))

    def _need(self, e, tok):
        sem, val, owner = tok
        if owner == "pe" and e == "pe":
            return
        k = self.known[e]
        if k.get(sem.name, 0) >= val:
            return
        self.eng[e].wait_ge(sem, val)
        k[sem.name] = val

    def _deps(self, e, reads, writes):
        for key in list(reads) + list(writes):
            if key in self.retired:
                raise RuntimeError(f"use of retired key {key}")
        for key in reads:
            r = self.res.get(key)
            if r and r[0] is not None:
                self._need(e, r[0])
        for key in writes:
            r = self.res.get(key)
            if r:
                if r[0] is not None:
                    self._need(e, r[0])
                for t in r[1]:
                    self._need(e, t)

    def _commit(self, tok, reads, writes):
        for key in reads:
            r = self.res.setdefault(key, [None, []])
            r[1] = [t for t in r[1] if t[0].name != tok[0].name] + [tok]
        for key in writes:
            self.res[key] = [tok, []]

    def op(self, e, fn, reads=(), writes=()):
        self._deps(e, reads, writes)
        if self.cnt[e] >= self.ROT:
            self.sem[e] = self._newsem(e)
            self.cnt[e] = 0
        inst = fn(self.eng[e])
        self.cnt[e] += 1
        inst.then_inc(self.sem[e], 1)
        tok = (self.sem[e], self.cnt[e], e)
        self._commit(tok, reads, writes)
        return tok

    def dma(self, q, out, in_, reads=(), writes=()):
        self._deps(q, reads, writes)
        dq = self.dq[q]
        slot = dq["sems"][dq["rr"] % len(dq["sems"])]
        dq["rr"] += 1
        if slot[1] > 0:
            self._need(q, (slot[0], slot[1], "dma"))
        inst = self.eng[q].dma_start(out=out, in_=in_)
        slot[1] += 16
        inst.then_inc(slot[0], 16)
        tok = (slot[0], slot[1], "dma")
        self._commit(tok, reads, writes)
        return tok

    def alias(self, new_keys, old_keys):
        toks = []
        for k in old_keys:
            r = self.res.get(k)
            if r:
                if r[0] is not None:
                    toks.append(r[0])
                toks.extend(r[1])
        for nk in new_keys:
            r = self.res.setdefault(nk, [None, []])
            r[1] = r[1] + toks

    def claim(self, region, lo, hi, key):
        act = self.active.setdefault(region, {})
        if key in act and act[key] == (lo, hi):
            return
        for k, (l, h) in list(act.items()):
            if l < hi and lo < h:
                self.alias([key], [k])
                del act[k]
                self.retired.add(k)
        act[key] = (lo, hi)
        self.retired.discard(key)

    def finish(self, e, toks):
        for t in toks:
            self._need(e, t)


def tok_blocks(lo, hi, step=512):
    out = []
    t = lo
    while t < hi:
        n = min(step, hi - t)
        out.append((t, n))
        t += n
    return out


def build_program(stop="full", debug=False):
    nc = bass.Bass("TRN2", target_bir_lowering=False)

    def din(name, shape, dt=F32):
        return nc.dram_tensor(name, list(shape), dt, kind="ExternalInput").ap()

    xT_d = din("xT", [128, KC, NT])
    const_d = din("const", [128, NCONST])
    constb_d = din("constb", [128, 256 + 512])
    wmod_d = din("wmod", [2, 12, 128, KC * 512])
    rope_d = din("rope", [128, 2 * NLAT])
    whead_d = din("whead", [4, 128, KC * 640])
    wpool_d = din("wpool", [128, KC * 512])
    wout0_d = din("wout0", [KC, 128, D])
    wffn_d = din("wffn", [2, NPAIR, 128, KC * 256])
    wdn_d = din("wdn", [2, KC, 128, NPAIR * 128])
    whg_d = din("whg", [8, 128, KC * 640])
    wout1_d = din("wout1", [KC, 128, D])
    out_d = nc.dram_tensor("outT", [128, KC, NLAT], F32, kind="ExternalOutput").ap()

    with ExitStack() as es:
        P = Prog(nc, es)

        def sb(name, shape, dt):
            return es.enter_context(nc.sbuf_tensor(name, list(shape), dt))

        xT = sb("xT_sb", [128, KC, NT], F32)
        REG = {"HT": sb("HT", [128, 18432], BF16),
               "SCR": sb("SCR", [128, 23040], BF16),
               "ROPE": sb("ROPE", [128, 8192], BF16)}
        WS = [sb(f"WS{i}", [128, SLOT_ELEMS], BF16) for i in range(NSLOT)]
        CONST = sb("CONST", [128, NCONST], F32)
        CB = sb("CB", [128, 256 + 512], BF16)
        MOD = sb("MOD", [128, 2 * 96], F32)
        MOD1 = sb("MOD1", [128, 2 * 96], F32)
        SC = sb("SC", [128, KC * 2], BF16)
        MISC = sb("MISC", [128, 160], F32)
        PS = [es.enter_context(nc.psum_tensor(f"ps{i}", [128, 512], F32)) for i in range(8)]

        ident = CB[:, 0:128]
        ones = CB[:, 128:256]

        def rv(region, off, nelem, dt, key=None, keys=None):
            esz = 2 if dt == BF16 else 4
            assert off % 4 == 0
            v = REG[region][:, off // 2: off // 2 + nelem * esz // 2]
            if dt != BF16:
                v = v.bitcast(dt)
            if key is not None:
                P.claim(region, off, off + nelem * esz, key)
            if keys is not None:
                for (k, so, n) in keys:
                    P.claim(region, off + so * esz, off + (so + n) * esz, k)
            return v

        def xk(c, t0, n):
            return [("xT", c, b) for b in range(t0 // 256, (t0 + n + 255) // 256)]

        P.dma("sp", CONST[:], const_d, writes=["CONST"])
        P.dma("pool", CB[:], constb_d, writes=["CB"])
        for c in range(KC):
            P.dma("sp", xT[:, c, :], xT_d[:, c, :], writes=xk(c, 0, NT))
        P.op("dve", lambda e: e.memset(MISC[:, 0:1], EPS), writes=["EPSB"])
        EPSB = MISC[:, 0:1]

        state = {"slot": 0}

        def wload(src_ap, nelem):
            s = state["slot"] % NSLOT
            state["slot"] += 1
            P.dma("pool", WS[s][:, 0:nelem], src_ap, writes=[("WS", s)])
            return s

        SC3 = SC[:, :].rearrange("p (k two) -> p k two", two=2)
        P.op("act", lambda e: e.activation(out=SC3[:, :, 0], in_=CONST[:, CS("cT", 0, 8)], func=AF.Silu),
             reads=["CONST"], writes=["SC0"])
        P.op("act", lambda e: e.activation(out=SC3[:, :, 1], in_=CONST[:, CS("cT", 8, 8)], func=AF.Silu),
             reads=["CONST"], writes=["SC1"])
        for layer in range(2):
            for piece in range(12):
                s = wload(wmod_d[layer, piece], KC * 512)
                for sub in range(4):
                    jc = piece * 4 + sub
                    col = layer * 96 + jc * 2
                    for kc in range(KC):
                        P.op("pe", lambda e, s=s, kc=kc, sub=sub, col=col: e.matmul(
                            PS[0][:, col:col + 2],
                            lhsT=WS[s][:, kc * 512 + sub * 128: kc * 512 + sub * 128 + 128],
                            rhs=SC[:, kc * 2: kc * 2 + 2],
                            start=(kc == 0), stop=(kc == KC - 1)),
                            reads=[("WS", s), "SC0", "SC1"], writes=[("ps", 0)])
        P.op("dve", lambda e: e.tensor_tensor(out=MOD[:, :], in0=PS[0][:, 0:192], in1=CONST[:, CS("bmod")],
                                               op=ALU.add),
             reads=[("ps", 0), "CONST"], writes=["MOD"])
        P.op("dve", lambda e: e.tensor_scalar_add(out=MOD1[:, :], in0=MOD[:, :], scalar1=1.0),
             reads=["MOD"], writes=["MOD1"])

        def modv(layer, j, chunk, which, plus1=False):
            t = MOD1 if plus1 else MOD
            col = layer * 96 + (j * 8 + chunk) * 2 + which
            return t[:, col:col + 1]

        def nrm_temps(region, off, tag):
            t = {"tag": tag}
            t["sq"] = [rv(region, off + i * 1024, 512, BF16, key=("sq", tag, i)) for i in range(2)]
            t["rs"] = rv(region, off + 2048, 512, F32, key=("rs", tag))
            t["rstd"] = rv(region, off + 4096, 512, F32, key=("rstd", tag))
            t["tmp"] = [rv(region, off + 6144 + i * 2048, 512, F32, key=("ntmp", tag, i)) for i in range(2)]
            return t

        def rms_rstd(T, src_fn, rkeys, nchunk, n, denom, psb):
            tag = T["tag"]
            for c in range(nchunk):
                b = c % 2
                P.op("act", lambda e, c=c, b=b: e.activation(out=T["sq"][b][:, :n], in_=src_fn(c), func=AF.Square),
                     reads=rkeys(c), writes=[("sq", tag, b)])
                P.op("pe", lambda e, c=c, b=b: e.matmul(PS[psb][:, :n], lhsT=ones, rhs=T["sq"][b][:, :n],
                                                        start=(c == 0), stop=(c == nchunk - 1)),
                     reads=[("sq", tag, b), "CB"], writes=[("ps", psb)])
            P.op("act", lambda e: e.activation(out=T["rs"][:, :n], in_=PS[psb][:, :n], func=AF.Sqrt,
                                               scale=1.0 / denom, bias=EPSB),
                 reads=[("ps", psb), "EPSB"], writes=[("rs", tag)])
            P.op("dve", lambda e: e.reciprocal(out=T["rstd"][:, :n], in_=T["rs"][:, :n]),
                 reads=[("rs", tag)], writes=[("rstd", tag)])

        def norm_mod(T, layer, jshift, jscale, t0, n, which, out_fn, out_keys, psb=1):
            tag = T["tag"]
            rms_rstd(T, lambda c: xT[:, c, t0:t0 + n], lambda c: xk(c, t0, n), KC, n, float(D), psb)
            for c in range(KC):
                b = c % 2
                P.op("dve", lambda e, c=c, b=b: e.scalar_tensor_tensor(
                    out=T["tmp"][b][:, :n], in0=xT[:, c, t0:t0 + n], scalar=modv(layer, jscale, c, which, True),
                    in1=T["rstd"][:, :n], op0=ALU.mult, op1=ALU.mult),
                    reads=xk(c, t0, n) + ["MOD1", ("rstd", tag)], writes=[("ntmp", tag, b)])
                P.op("act", lambda e, c=c, b=b: e.activation(
                    out=out_fn(c), in_=T["tmp"][b][:, :n], func=AF.Identity,
                    bias=modv(layer, jshift, c, which), scale=1.0),
                    reads=[("ntmp", tag, b), "MOD"], writes=out_keys(c))

        def ffn(layer, groups):
            W2 = 1026
            cw = lambda c, tap: CONST[:, CS("convw", (layer * 44 + c) * 3 + tap, 1)]
            cb = lambda c: CONST[:, CS("convb", layer * 44 + c, 1)]
            h2 = rv("HT", 0, KC * W2, BF16,
                    keys=[(("h2", c), c * W2, W2) for c in range(KC)]).rearrange("p (k t) -> p k t", k=KC)
            ctmp = [[rv("HT", KC * W2 * 2 + (i * 2 + j) * 1408, 352, F32, key=("ctmp", i, j))
                     for j in range(2)] for i in range(2)]
            T = nrm_temps("HT", KC * W2 * 2 + 4 * 1408, "H")
            stash = rv("HT", KC * W2 * 2 + 4 * 1408 + 10240, KC * 8, BF16,
                       keys=[(("stash", c), c * 8, 8) for c in range(KC)]).rearrange("p (k t) -> p k t", k=KC)
            gbuf = rv("SCR", 0, NPAIR * 1024, BF16,
                      keys=[(("g", c), c * 1024, 1024) for c in range(NPAIR)]
                      ).rearrange("p (c t) -> p c t", c=NPAIR)
            h2keys = [("h2", c) for c in range(KC)]
            for gi, (slo, shi, t0, G, which) in enumerate(groups):
                if t0 - 1 >= slo:
                    norm_mod(T, layer, 3, 4, t0 - 2, 2, which,
                             lambda c, gi=gi: stash[:, c, 2 * gi:2 * gi + 2],
                             lambda c: [("stash", c)])
            for gi, (slo, shi, t0, G, which) in enumerate(groups):
                hi = min(t0 + G + 1, shi)
                if t0 - 1 >= slo:
                    for c in range(KC):
                        P.op("dve", lambda e, c=c, gi=gi: e.tensor_copy(out=h2[:, c, 0:1],
                                                                        in_=stash[:, c, 2 * gi + 1:2 * gi + 2]),
                             reads=[("stash", c)], writes=[("h2", c)])
                else:
                    P.op("dve", lambda e: e.memset(h2[:, :, 0:1], 0.0), writes=h2keys)
                if hi < t0 + G + 1:
                    P.op("dve", lambda e, G=G: e.memset(h2[:, :, G + 1:G + 2], 0.0), writes=h2keys)
                for (b0, bn) in tok_blocks(t0, hi):
                    j0 = b0 - (t0 - 1)
                    norm_mod(T, layer, 3, 4, b0, bn, which,
                             lambda c, j0=j0, bn=bn: h2[:, c, j0:j0 + bn],
                             lambda c: [("h2", c)])
                npiece = (G + 341) // 342
                base = G // npiece
                pieces = []
                o = 0
                for i in range(npiece):
                    n = base + (1 if i < G % npiece else 0)
                    pieces.append((o, n))
                    o += n
                it = 0
                for c in range(NPAIR):
                    s = wload(wffn_d[layer, c], KC * 256)
                    for (o, n) in pieces:
                        bsel = it % 2
                        it += 1
                        pa, pg = 2 + bsel * 2, 3 + bsel * 2
                        for half, pb in ((0, pa), (1, pg)):
                            for kc in range(KC):
                                P.op("pe", lambda e, s=s, kc=kc, half=half, pb=pb, o=o, n=n: e.matmul(
                                    PS[pb][:, 0:n + 2],
                                    lhsT=WS[s][:, kc * 256 + half * 128: kc * 256 + half * 128 + 128],
                                    rhs=h2[:, kc, o:o + n + 2],
                                    start=(kc == 0), stop=(kc == KC - 1)),
                                    reads=[("WS", s), ("h2", kc)], writes=[("ps", pb)])
                        for half, pb in ((0, pa), (1, pg)):
                            cc = c + half * NPAIR
                            tb = ctmp[bsel][half]
                            P.op("act", lambda e, pb=pb, tb=tb, cc=cc, n=n: e.activation(
                                out=tb[:, 0:n], in_=PS[pb][:, 1:n + 1], func=AF.Identity,
                                scale=cw(cc, 1), bias=cb(cc)),
                                reads=[("ps", pb), "CONST"], writes=[("ctmp", bsel, half)])
                            P.op("dve", lambda e, pb=pb, tb=tb, cc=cc, n=n: e.scalar_tensor_tensor(
                                out=tb[:, 0:n], in0=PS[pb][:, 0:n], scalar=cw(cc, 0), in1=tb[:, 0:n],
                                op0=ALU.mult, op1=ALU.add),
                                reads=[("ps", pb), "CONST", ("ctmp", bsel, half)], writes=[("ctmp", bsel, half)])
                            P.op("dve", lambda e, pb=pb, tb=tb, cc=cc, n=n: e.scalar_tensor_tensor(
                                out=tb[:, 0:n], in0=PS[pb][:, 2:n + 2], scalar=cw(cc, 2), in1=tb[:, 0:n],
                                op0=ALU.mult, op1=ALU.add),
                                reads=[("ps", pb), "CONST", ("ctmp", bsel, half)], writes=[("ctmp", bsel, half)])
                        P.op("act", lambda e, bsel=bsel, n=n: e.activation(
                            out=ctmp[bsel][1][:, 0:n], in_=ctmp[bsel][1][:, 0:n], func=AF.Silu),
                            reads=[("ctmp", bsel, 1)], writes=[("ctmp", bsel, 1)])
                        P.op("dve", lambda e, bsel=bsel, n=n, c=c, o=o: e.tensor_tensor(
                            out=gbuf[:, c, o:o + n], in0=ctmp[bsel][0][:, 0:n], in1=ctmp[bsel][1][:, 0:n],
                            op=ALU.mult),
                            reads=[("ctmp", bsel, 0), ("ctmp", bsel, 1)], writes=[("g", c)])
                it = 0
                for m in range(KC):
                    s = wload(wdn_d[layer, m], NPAIR * 128)
                    for (b0, bn) in tok_blocks(0, G):
                        pb = 2 + (it % 4)
                        it += 1
                        for c in range(NPAIR):
                            P.op("pe", lambda e, s=s, c=c, pb=pb, b0=b0, bn=bn: e.matmul(
                                PS[pb][:, 0:bn], lhsT=WS[s][:, c * 128:(c + 1) * 128],
                                rhs=gbuf[:, c, b0:b0 + bn], start=(c == 0), stop=(c == NPAIR - 1)),
                                reads=[("WS", s), ("g", c)], writes=[("ps", pb)])
                        tt = t0 + b0
                        P.op("dve", lambda e, pb=pb, m=m, tt=tt, bn=bn: e.scalar_tensor_tensor(
                            out=xT[:, m, tt:tt + bn], in0=PS[pb][:, 0:bn], scalar=modv(layer, 5, m, which),
                            in1=xT[:, m, tt:tt + bn], op0=ALU.mult, op1=ALU.add),
                            reads=[("ps", pb), "MOD"] + xk(m, tt, bn), writes=xk(m, tt, bn))

        def hT_view():
            return rv("HT", 0, KC * NT, BF16,
                      keys=[(("hT", c), c * NT, NT) for c in range(KC)]).rearrange("p (k t) -> p k t", k=KC)

        ALLBLK = [(0, NCTX, 1)] + [(b0, bn, 0) for (b0, bn) in tok_blocks(NCTX, NT)]

        def compute_hT(layer):
            hT = hT_view()
            T = nrm_temps("SCR", 46080 - 10240, "S")
            for (b0, bn, which) in ALLBLK:
                norm_mod(T, layer, 0, 1, b0, bn, which,
                         lambda c, b0=b0, bn=bn: hT[:, c, b0:b0 + bn], lambda c: [("hT", c)])
            return hT

        hkeys = [("hT", c) for c in range(KC)]
        psrr = {"i": 0}

        def nextps(lo=0, hi=8):
            i = lo + psrr["i"] % (hi - lo)
            psrr["i"] += 1
            return i

        def resid_add(layer, ysrc_fn, ykeys, s, wcol_fn, nk, jgate, blocks=None):
            for m in range(KC):
                for (b0, bn, which) in (blocks or ALLBLK):
                    pb = nextps()
                    for k in range(nk):
                        P.op("pe", lambda e, m=m, k=k, pb=pb, b0=b0, bn=bn: e.matmul(
                            PS[pb][:, 0:bn], lhsT=wcol_fn(k, m), rhs=ysrc_fn(k, b0, bn),
                            start=(k == 0), stop=(k == nk - 1)),
                            reads=[("WS", s)] + ykeys(k), writes=[("ps", pb)])
                    P.op("dve", lambda e, m=m, pb=pb, b0=b0, bn=bn, which=which: e.scalar_tensor_tensor(
                        out=xT[:, m, b0:b0 + bn], in0=PS[pb][:, 0:bn], scalar=modv(layer, jgate, m, which),
                        in1=xT[:, m, b0:b0 + bn], op0=ALU.mult, op1=ALU.add),
                        reads=[("ps", pb), "MOD"] + xk(m, b0, bn), writes=xk(m, b0, bn))

        def mixer0():
            layer = 0
            hT = compute_hT(layer)
            WP = 2336

            def col(t):
                return t + 8 if t < NCTX else t + 24
            U = rv("SCR", 0, WP, F32, key="U")
            T1 = rv("SCR", 9344, WP, F32, key="T1")
            dbuf = rv("SCR", 18688, WP, BF16, key="dbuf")
            ypool = rv("SCR", 23360, 4 * NT, BF16,
                       keys=[(("ypool", g), g * NT, NT) for g in range(4)]).rearrange("p (g t) -> p g t", g=4)
            T2 = rv("ROPE", 0, WP, F32, key="T2")
            s_u = wload(wpool_d, KC * 512)
            for (a, b) in ((0, 8), (264, 280), (2328, 2336)):
                P.op("dve", lambda e, a=a, b=b: e.memset(U[:, a:b], 0.0), writes=["U"])
            tmpE = MISC[:, 8:16]
            for g, w in enumerate((2, 4, 8, 16)):
                for (b0, bn, which) in ALLBLK:
                    pb = nextps()
                    for kc in range(KC):
                        P.op("pe", lambda e, kc=kc, pb=pb, b0=b0, bn=bn, g=g: e.matmul(
                            PS[pb][:, 0:bn], lhsT=WS[s_u][:, kc * 512 + g * 128: kc * 512 + g * 128 + 128],
                            rhs=hT[:, kc, b0:b0 + bn], start=(kc == 0), stop=(kc == KC - 1)),
                            reads=[("WS", s_u), ("hT", kc)], writes=[("ps", pb)])
                    P.op("act", lambda e, pb=pb, b0=b0, bn=bn: e.activation(
                        out=U[:, col(b0):col(b0) + bn], in_=PS[pb][:, 0:bn], func=AF.Identity),
                        reads=[("ps", pb)], writes=["U"])
                P.op("dve", lambda e: e.tensor_tensor(out=T1[:, 1:WP], in0=U[:, 0:WP - 1], in1=U[:, 1:WP], op=ALU.add),
                     reads=["U"], writes=["T1"])
                A, Akey = T1, "T1"
                if w >= 4:
                    P.op("dve", lambda e: e.tensor_tensor(out=T2[:, 2:WP - 1], in0=T1[:, 1:WP - 2], in1=T1[:, 3:WP],
                                                          op=ALU.add), reads=["T1"], writes=["T2"])
                    A, Akey = T2, "T2"
                if w >= 8:
                    P.op("dve", lambda e: e.tensor_tensor(out=T1[:, 4:WP - 3], in0=T2[:, 2:WP - 5], in1=T2[:, 6:WP - 1],
                                                          op=ALU.add), reads=["T2"], writes=["T1"])
                    A, Akey = T1, "T1"
                if w >= 16:
                    P.op("dve", lambda e: e.tensor_tensor(out=T2[:, 8:WP - 7], in0=T1[:, 4:WP - 11], in1=T1[:, 12:WP - 3],
                                                          op=ALU.add), reads=["T1"], writes=["T2"])
                    A, Akey = T2, "T2"
                P.op("dve", lambda e, A=A, w=w: e.scalar_tensor_tensor(
                    out=dbuf[:, 8:2328], in0=A[:, 8:2328], scalar=1.0 / w, in1=U[:, 8:2328],
                    op0=ALU.mult, op1=ALU.subtract), reads=[Akey, "U"], writes=["dbuf"])
                hw = w // 2
                for ei, c0 in enumerate((col(0), col(NCTX - 1) + 1 - hw, col(NCTX), col(NT - 1) + 1 - hw)):
                    tbl = CONST[:, CS("pedge", (g * 4 + ei) * 8, hw)]
                    P.op("dve", lambda e, A=A, c0=c0, tbl=tbl, hw=hw: e.tensor_tensor(
                        out=tmpE[:, 0:hw], in0=A[:, c0:c0 + hw], in1=tbl, op=ALU.mult),
                        reads=[Akey, "CONST"], writes=["tmpE"])
                    P.op("dve", lambda e, c0=c0, hw=hw: e.tensor_tensor(
                        out=dbuf[:, c0:c0 + hw], in0=tmpE[:, 0:hw], in1=U[:, c0:c0 + hw], op=ALU.subtract),
                        reads=["tmpE", "U", "dbuf"], writes=["dbuf"])
                for (b0, bn, which) in ALLBLK:
                    pb = nextps()
                    P.op("pe", lambda e, pb=pb, b0=b0, bn=bn, g=g: e.matmul(
                        PS[pb][:, 0:bn], lhsT=CB[:, 256 + g * 128: 256 + (g + 1) * 128],
                        rhs=dbuf[:, col(b0):col(b0) + bn], start=True, stop=True),
                        reads=["CB", "dbuf"], writes=[("ps", pb)])
                    P.op("act", lambda e, pb=pb, b0=b0, bn=bn, g=g: e.activation(
                        out=ypool[:, g, b0:b0 + bn], in_=PS[pb][:, 0:bn], func=AF.Identity,
                        scale=CONST[:, CS("pscale", g, 1)]),
                        reads=[("ps", pb), "CONST"], writes=[("ypool", g)])
            s_w = state["slot"] % NSLOT
            state["slot"] += 1
            P.dma("pool", WS[s_w][:, 0:4096].rearrange("p (k n) -> p k n", k=4),
                  wout0_d[4:8].rearrange("k p n -> p k n"), writes=[("WS", s_w)])
            resid_add(layer, lambda k, b0, bn: ypool[:, k, b0:b0 + bn], lambda k: [("ypool", k)], s_w,
                      lambda k, m: WS[s_w][:, k * 1024 + m * 128: k * 1024 + m * 128 + 128], 4, 2)

            ropeT = rv("ROPE", 0, 2 * NLAT, F32, key="rope")
            P.dma("sp", ropeT, rope_d, writes=["rope"])
            CTt, STt = ropeT[:, 0:NLAT], ropeT[:, NLAT:2 * NLAT]
            P.op("dve", lambda e: e.tensor_tensor(out=MISC[:, 16:80], in0=CONST[:, CS("lqk", 0, 64)],
                                                  in1=CONST[:, CS("lqk", 64, 64)], op=ALU.mult),
                 reads=["CONST"], writes=["lam_p1"])
            P.op("dve", lambda e: e.tensor_tensor(out=MISC[:, 80:144], in0=CONST[:, CS("lqk", 128, 64)],
                                                  in1=CONST[:, CS("lqk", 192, 64)], op=ALU.mult),
                 reads=["CONST"], writes=["lam_p2"])
            P.op("dve", lambda e: e.reduce_sum(out=MISC[:, 1:2], in_=MISC[:, 16:80], axis=mybir.AxisListType.X),
                 reads=["lam_p1"], writes=["lam_s1"])
            P.op("dve", lambda e: e.reduce_sum(out=MISC[:, 2:3], in_=MISC[:, 80:144], axis=mybir.AxisListType.X),
                 reads=["lam_p2"], writes=["lam_s2"])
            P.op("act", lambda e: e.activation(out=MISC[:, 1:3], in_=MISC[:, 1:3], func=AF.Exp),
                 reads=["lam_s1", "lam_s2"], writes=["lam_e"])
            P.op("dve", lambda e: e.tensor_tensor(out=MISC[:, 3:4], in0=MISC[:, 2:3], in1=MISC[:, 1:2],
                                                  op=ALU.subtract), reads=["lam_e"], writes=["neglam0"])
            lambda_init = 0.8 - 0.6 * math.exp(-0.3 * 0)
            P.op("dve", lambda e: e.tensor_scalar_add(out=MISC[:, 3:4], in0=MISC[:, 3:4], scalar1=-lambda_init),
                 reads=["neglam0"], writes=["neglam"])
            P.op("dve", lambda e: e.tensor_scalar_mul(out=MISC[:, 4:8], in0=CONST[:, CS("subln")],
                                                      scalar1=1.0 - lambda_init),
                 reads=["CONST"], writes=["G08"])
            neglam = MISC[:, 3:4]

            qT = rv("SCR", 0, NT, BF16, key="qT")
            kT = rv("SCR", 4608, NT, BF16, key="kT")
            Vh = rv("SCR", 9216, 18 * 128, BF16, key="Vh").rearrange("p (i d) -> p i d", i=18)
            yh = rv("SCR", 13824, NT, BF16, key="yh")
            E = [[rv("SCR", 18432 + (m * 2 + b) * 1024, 512, BF16, key=("E", m, b)) for b in range(2)]
                 for m in range(2)]
            rt = [rv("SCR", 22528 + i * 2048, 512, F32, key=("rt", i)) for i in range(2)]
            r0 = rv("SCR", 26624, 512, F32, key="r0")
            r1 = rv("SCR", 28672, 512, F32, key="r1")
            t0b = rv("SCR", 30720, 512, F32, key="t0b")
            t1b = rv("SCR", 32768, 512, F32, key="t1b")
            sqh = rv("SCR", 34816, 512, BF16, key="sqh")
            rsh = rv("SCR", 35840, 512, F32, key="rsh")
            rrh = rv("SCR", 37888, 512, F32, key="rrh")
            for h in range(4):
                s = wload(whead_d[h], KC * 640)
                W = lambda kc, c0: WS[s][:, kc * 640 + c0: kc * 640 + c0 + 128]
                for (dst, dkey, c0) in ((qT, "qT", 0), (kT, "kT", 256)):
                    for (b0, bn, which) in ALLBLK:
                        pa = nextps()
                        for kc in range(KC):
                            P.op("pe", lambda e, kc=kc, pa=pa, b0=b0, bn=bn, c0=c0: e.matmul(
                                PS[pa][:, 0:bn], lhsT=W(kc, c0), rhs=hT[:, kc, b0:b0 + bn],
                                start=(kc == 0), stop=(kc == KC - 1)),
                                reads=[("WS", s), ("hT", kc)], writes=[("ps", pa)])
                        if which == 1:
                            P.op("act", lambda e, pa=pa, b0=b0, bn=bn, dst=dst: e.activation(
                                out=dst[:, b0:b0 + bn], in_=PS[pa][:, 0:bn], func=AF.Identity),
                                reads=[("ps", pa)], writes=[dkey])
                            continue
                        pb = nextps()
                        for kc in range(KC):
                            P.op("pe", lambda e, kc=kc, pb=pb, b0=b0, bn=bn, c0=c0: e.matmul(
                                PS[pb][:, 0:bn], lhsT=W(kc, c0 + 128), rhs=hT[:, kc, b0:b0 + bn],
                                start=(kc == 0), stop=(kc == KC - 1)),
                                reads=[("WS", s), ("hT", kc)], writes=[("ps", pb)])
                        l0 = b0 - NCTX
                        P.op("dve", lambda e, pa=pa, bn=bn, l0=l0: e.tensor_tensor(
                            out=rt[0][:, 0:bn], in0=PS[pa][:, 0:bn], in1=CTt[:, l0:l0 + bn], op=ALU.mult),
                            reads=[("ps", pa), "rope"], writes=[("rt", 0)])
                        P.op("dve", lambda e, pb=pb, bn=bn, l0=l0: e.tensor_tensor(
                            out=rt[1][:, 0:bn], in0=PS[pb][:, 0:bn], in1=STt[:, l0:l0 + bn], op=ALU.mult),
                            reads=[("ps", pb), "rope"], writes=[("rt", 1)])
                        P.op("dve", lambda e, b0=b0, bn=bn, dst=dst: e.tensor_tensor(
                            out=dst[:, b0:b0 + bn], in0=rt[0][:, 0:bn], in1=rt[1][:, 0:bn], op=ALU.add),
                            reads=[("rt", 0), ("rt", 1)], writes=[dkey])
                for i4 in range(0, 18, 4):
                    nt_ = min(4, 18 - i4)
                    pb = nextps()
                    for j in range(nt_):
                        i = i4 + j
                        for kc in range(KC):
                            P.op("pe", lambda e, kc=kc, pb=pb, i=i, j=j: e.matmul(
                                PS[pb][:, j * 128:(j + 1) * 128], lhsT=hT[:, kc, i * 128:(i + 1) * 128],
                                rhs=W(kc, 512), start=(kc == 0), stop=(kc == KC - 1)),
                                reads=[("WS", s), ("hT", kc)], writes=[("ps", pb)])
                    P.op("act", lambda e, pb=pb, i4=i4, nt_=nt_: e.activation(
                        out=Vh[:, i4:i4 + nt_, :], in_=PS[pb][:, 0:nt_ * 128].rearrange("p (i d) -> p i d", i=nt_),
                        func=AF.Identity), reads=[("ps", pb)], writes=["Vh"])
                for (q0, n, which) in ALLBLK:
                    nkt = 2 if which == 1 else 18
                    for i in range(nkt):
                        b = i % 2
                        for m in range(2):
                            P.op("pe", lambda e, m=m, b=b, i=i, q0=q0, n=n: e.matmul(
                                PS[4 + m * 2 + b][:, 0:n], lhsT=kT[64 * m:64 * m + 64, i * 128:(i + 1) * 128],
                                rhs=qT[64 * m:64 * m + 64, q0:q0 + n], start=True, stop=True),
                                reads=["kT", "qT"], writes=[("ps", 4 + m * 2 + b)])
                        for m in range(2):
                            P.op("act", lambda e, m=m, b=b, n=n: e.activation(
                                out=E[m][b][:, 0:n], in_=PS[4 + m * 2 + b][:, 0:n], func=AF.Exp, scale=0.125),
                                reads=[("ps", 4 + m * 2 + b)], writes=[("E", m, b)])
                        for m in range(2):
                            P.op("pe", lambda e, m=m, b=b, i=i, n=n: e.matmul(
                                PS[m][:, 0:n], lhsT=Vh[:, i, :], rhs=E[m][b][:, 0:n],
                                start=(i == 0), stop=(i == nkt - 1)),
                                reads=["Vh", ("E", m, b)], writes=[("ps", m)])
                            P.op("pe", lambda e, m=m, b=b, i=i, n=n: e.matmul(
                                PS[2 + m][:, 0:n], lhsT=ones, rhs=E[m][b][:, 0:n],
                                start=(i == 0), stop=(i == nkt - 1)),
                                reads=["CB", ("E", m, b)], writes=[("ps", 2 + m)])
                    P.op("dve", lambda e, n=n: e.reciprocal(out=r0[:, 0:n], in_=PS[2][:, 0:n]),
                         reads=[("ps", 2)], writes=["r0"])
                    P.op("dve", lambda e, n=n: e.reciprocal(out=r1[:, 0:n], in_=PS[3][:, 0:n]),
                         reads=[("ps", 3)], writes=["r1"])
                    P.op("dve", lambda e, n=n: e.tensor_tensor(out=t0b[:, 0:n], in0=PS[0][:, 0:n], in1=r0[:, 0:n],
                                                               op=ALU.mult),
                         reads=[("ps", 0), "r0"], writes=["t0b"])
                    P.op("dve", lambda e, n=n: e.tensor_tensor(out=t1b[:, 0:n], in0=PS[1][:, 0:n], in1=r1[:, 0:n],
                                                               op=ALU.mult),
                         reads=[("ps", 1), "r1"], writes=["t1b"])
                    P.op("dve", lambda e, n=n: e.scalar_tensor_tensor(
                        out=t0b[:, 0:n], in0=t1b[:, 0:n], scalar=neglam, in1=t0b[:, 0:n],
                        op0=ALU.mult, op1=ALU.add), reads=["t1b", "t0b", "neglam"], writes=["t0b"])
                    P.op("act", lambda e, n=n: e.activation(out=sqh[:, 0:n], in_=t0b[:, 0:n], func=AF.Square),
                         reads=["t0b"], writes=["sqh"])
                    P.op("pe", lambda e, n=n: e.matmul(PS[2][:, 0:n], lhsT=ones, rhs=sqh[:, 0:n], start=True, stop=True),
                         reads=["CB", "sqh"], writes=[("ps", 2)])
                    P.op("act", lambda e, n=n: e.activation(out=rsh[:, 0:n], in_=PS[2][:, 0:n], func=AF.Sqrt,
                                                            scale=1.0 / 128.0, bias=EPSB),
                         reads=[("ps", 2), "EPSB"], writes=["rsh"])
                    P.op("dve", lambda e, n=n: e.reciprocal(out=rrh[:, 0:n], in_=rsh[:, 0:n]),
                         reads=["rsh"], writes=["rrh"])
                    P.op("dve", lambda e, n=n, q0=q0, h=h: e.scalar_tensor_tensor(
                        out=yh[:, q0:q0 + n], in0=t0b[:, 0:n], scalar=MISC[:, 4 + h:5 + h], in1=rrh[:, 0:n],
                        op0=ALU.mult, op1=ALU.mult), reads=["t0b", "G08", "rrh"], writes=["yh"])
                s_w = wload(wout0_d[h], D)
                resid_add(layer, lambda k, b0, bn: yh[:, b0:b0 + bn], lambda k: ["yh"], s_w,
                          lambda k, m, s_w=s_w: WS[s_w][:, m * 128:(m + 1) * 128], 1, 2)

        HS = sb("HG_S", [128, 2 * 128], F32)
        HSm = sb("HG_Sm", [128, 2 * 128], BF16)
        ATm = sb("HG_ATm", [128, 2 * 64], BF16)
        HSC = sb("HG_SC", [128, 6 * 36 + 16 + 32], F32)
        LATBLK = [(b0, bn, 0) for (b0, bn) in tok_blocks(NCTX, NT)]

        def mixer1():
            layer = 1
            hT = compute_hT(layer)
            LB = HSC[:, 232:248]
            OML = HSC[:, 248:264]
            lbl = CONST[:, CS("lblog")].rearrange("p (d l h) -> p d l h", d=2, l=2)
            LB3 = LB.rearrange("p (d h) -> p d h", d=2)
            P.op("dve", lambda e: e.tensor_tensor(out=LB3, in0=lbl[:, :, 1, :], in1=lbl[:, :, 0, :], op=ALU.subtract),
                 reads=["CONST"], writes=["LB"])
            P.op("act", lambda e: e.activation(out=LB, in_=LB, func=AF.Sigmoid), reads=["LB"], writes=["LB"])
            P.op("dve", lambda e: e.tensor_scalar(out=OML, in0=LB, scalar1=-1.0, scalar2=1.0, op0=ALU.mult,
                                                  op1=ALU.add), reads=["LB"], writes=["OML"])

            Qd = [rv("SCR", d * 4096, NLAT, BF16, key=("Qd", d)) for d in range(2)]
            Kd = [rv("SCR", 8192 + d * 4608, NT, BF16, key=("Kd", d)) for d in range(2)]
            Ktok = [rv("SCR", 17408 + d * 4608, 18 * 128, BF16, key=("Ktok", d)).rearrange("p (i c) -> p i c", i=18)
                    for d in range(2)]
            Vh = rv("SCR", 26624, 18 * 128, BF16, key="Vh1").rearrange("p (i d) -> p i d", i=18)
            TO = 31232
            OF = rv("ROPE", 0, NLAT, F32, keys=[(("OF", b), b * 512, 512) for b in range(4)])
            SG = rv("ROPE", 8192, NLAT, BF16, key="SG")
            YH = rv("ROPE", 12288, NLAT, BF16, key="YH")
            AL = [HSC[:, d * 108 + 0: d * 108 + 36] for d in range(2)]
            BE = [HSC[:, d * 108 + 36: d * 108 + 72] for d in range(2)]
            GA = [HSC[:, d * 108 + 72: d * 108 + 108] for d in range(2)]
            RP = HSC[:, 216:224]
            PIECES = tok_blocks(0, NT)
            for h in range(8):
                Qp = rv("SCR", TO, 512, F32, key="Qp")
                Fb = rv("SCR", TO + 2048, 512, F32, key="Fb")
                Kk = rv("SCR", TO + 4096, 512, F32, key="Kk")
                D1 = rv("SCR", TO + 6144, 512, F32, key="D1")
                Pb = rv("SCR", TO + 8192, 512, F32, key="Pb")
                Gb = rv("SCR", TO + 10240, 512, F32, key="Gb")
                s = wload(whg_d[h], KC * 640)
                W = lambda kc, c0: WS[s][:, kc * 640 + c0: kc * 640 + c0 + 128]

                def proj(c0, p0, n, pb):
                    for kc in range(KC):
                        P.op("pe", lambda e, kc=kc: e.matmul(
                            PS[pb][:, 0:n], lhsT=W(kc, c0), rhs=hT[:, kc, p0:p0 + n],
                            start=(kc == 0), stop=(kc == KC - 1)),
                            reads=[("WS", s), ("hT", kc)], writes=[("ps", pb)])

                for i4 in range(0, 18, 4):
                    nt_ = min(4, 18 - i4)
                    pb = nextps()
                    for j in range(nt_):
                        i = i4 + j
                        for kc in range(KC):
                            P.op("pe", lambda e, kc=kc, pb=pb, i=i, j=j: e.matmul(
                                PS[pb][:, j * 128:(j + 1) * 128], lhsT=hT[:, kc, i * 128:(i + 1) * 128],
                                rhs=W(kc, 128), start=(kc == 0), stop=(kc == KC - 1)),
                                reads=[("WS", s), ("hT", kc)], writes=[("ps", pb)])
                    P.op("act", lambda e, pb=pb, i4=i4, nt_=nt_: e.activation(
                        out=Vh[:, i4:i4 + nt_, :], in_=PS[pb][:, 0:nt_ * 128].rearrange("p (i d) -> p i d", i=nt_),
                        func=AF.Identity), reads=[("ps", pb)], writes=["Vh1"])
                for (b0, bn, _) in LATBLK:
                    pb = nextps()
                    proj(512, b0, bn, pb)
                    P.op("act", lambda e, pb=pb, b0=b0, bn=bn: e.activation(
                        out=SG[:, b0 - NCTX:b0 - NCTX + bn], in_=PS[pb][:, 0:bn], func=AF.Silu),
                        reads=[("ps", pb)], writes=["SG"])
                for (p0, n) in PIECES:
                    nch = n // 64
                    c0 = p0 // 64
                    l0 = max(p0, NCTX)
                    ln = p0 + n - l0
                    lo = l0 - p0
                    pb = nextps()
                    proj(0, p0, n, pb)
                    P.op("act", lambda e, pb=pb, n=n: e.activation(out=Qp[:, 0:n], in_=PS[pb][:, 0:n], func=AF.Silu),
                         reads=[("ps", pb)], writes=["Qp"])
                    v3 = lambda t, n=n: t[:, 0:n].rearrange("p (c t) -> p c t", t=64)
                    for d in range(2):
                        pb = nextps()
                        proj(256 + d * 128, p0, n, pb)
                        P.op("act", lambda e, pb=pb, n=n: e.activation(out=Fb[:, 0:n], in_=PS[pb][:, 0:n],
                                                                       func=AF.Sigmoid),
                             reads=[("ps", pb)], writes=["Fb"])
                        P.op("dve", lambda e, n=n, d=d, h=h: e.tensor_scalar(
                            out=Fb[:, 0:n], in0=Fb[:, 0:n], scalar1=OML[:, d * 8 + h:d * 8 + h + 1],
                            scalar2=LB[:, d * 8 + h:d * 8 + h + 1], op0=ALU.mult, op1=ALU.add),
                            reads=["Fb", "OML", "LB"], writes=["Fb"])
                        P.op("dve", lambda e, n=n: e.tensor_scalar(
                            out=Kk[:, 0:n], in0=Fb[:, 0:n], scalar1=-1.0, scalar2=1.0, op0=ALU.mult, op1=ALU.add),
                            reads=["Fb"], writes=["Kk"])
                        P.op("dve", lambda e, n=n: e.memset(D1[:, 0:n], 0.0), writes=["D1"])
                        P.op("dve", lambda e: e.tensor_copy(out=v3(D1)[:, :, 0:1], in_=v3(Fb)[:, :, 0:1]),
                             reads=["Fb", "D1"], writes=["D1"])
                        P.op("dve", lambda e: e.memset(v3(Fb)[:, :, 0:1], 0.0), reads=["Fb"], writes=["Fb"])
                        P.op("dve", lambda e, n=n: e.tensor_tensor_scan(
                            out=Pb[:, 0:n], data0=Fb[:, 0:n], data1=D1[:, 0:n], initial=0.0,
                            op0=ALU.mult, op1=ALU.add), reads=["Fb", "D1"], writes=["Pb"])
                        P.op("dve", lambda e, nch=nch: e.reciprocal(
                            out=RP[:, 0:nch].rearrange("p (c o) -> p c o", o=1), in_=v3(Pb)[:, :, 31:32]),
                            reads=["Pb"], writes=["RP"])
                        P.op("dve", lambda e, nch=nch: e.tensor_tensor(
                            out=v3(Gb), in0=v3(Pb),
                            in1=RP[:, 0:nch].rearrange("p (c o) -> p c o", o=1).to_broadcast([128, nch, 64]),
                            op=ALU.mult), reads=["Pb", "RP"], writes=["Gb"])
                        P3 = v3(Pb)
                        G3 = v3(Gb)
                        sc = lambda t, nch=nch, c0=c0: t[:, c0:c0 + nch].rearrange("p (c o) -> p c o", o=1)
                        P.op("dve", lambda e, d=d: e.tensor_copy(out=sc(AL[d]), in_=P3[:, :, 63:64]),
                             reads=["Pb"], writes=[("AL", d)])
                        bsrc = G3[:, :, 63:64] if d == 0 else P3[:, :, 31:32]
                        gsrc = P3[:, :, 31:32] if d == 0 else G3[:, :, 63:64]
                        P.op("dve", lambda e, d=d, bsrc=bsrc: e.tensor_copy(out=sc(BE[d]), in_=bsrc),
                             reads=["Pb", "Gb"], writes=[("BE", d)])
                        P.op("dve", lambda e, d=d, gsrc=gsrc: e.tensor_copy(out=sc(GA[d]), in_=gsrc),
                             reads=["Pb", "Gb"], writes=[("GA", d)])
                        if d == 0:
                            if ln > 0:
                                P.op("dve", lambda e, lo=lo, ln=ln, l0=l0: e.tensor_tensor(
                                    out=Qd[0][:, l0 - NCTX:l0 - NCTX + ln], in0=Qp[:, lo:lo + ln],
                                    in1=Gb[:, lo:lo + ln], op=ALU.mult), reads=["Qp", "Gb"], writes=[("Qd", 0)])
                            P.op("dve", lambda e, n=n: e.reciprocal(out=D1[:, 0:n], in_=Gb[:, 0:n]),
                                 reads=["Gb", "D1"], writes=["D1"])
                            P.op("dve", lambda e, n=n, p0=p0: e.tensor_tensor(
                                out=Kd[0][:, p0:p0 + n], in0=Kk[:, 0:n], in1=D1[:, 0:n], op=ALU.mult),
                                reads=["Kk", "D1"], writes=[("Kd", 0)])
                        else:
                            P.op("dve", lambda e: e.tensor_copy(out=v3(D1)[:, :, 1:64], in_=G3[:, :, 0:63]),
                                 reads=["Gb", "D1"], writes=["D1"])
                            P.op("dve", lambda e, nch=nch: e.tensor_copy(
                                out=v3(D1)[:, :, 0:1], in_=RP[:, 0:nch].rearrange("p (c o) -> p c o", o=1)),
                                reads=["RP", "D1"], writes=["D1"])
                            P.op("dve", lambda e, n=n, p0=p0: e.tensor_tensor(
                                out=Kd[1][:, p0:p0 + n], in0=Kk[:, 0:n], in1=D1[:, 0:n], op=ALU.mult),
                                reads=["Kk", "D1"], writes=[("Kd", 1)])
                            if ln > 0:
                                P.op("dve", lambda e, lo=lo, ln=ln: e.reciprocal(out=Pb[:, lo:lo + ln],
                                                                                 in_=D1[:, lo:lo + ln]),
                                     reads=["D1", "Pb"], writes=["Pb"])
                                P.op("dve", lambda e, lo=lo, ln=ln, l0=l0: e.tensor_tensor(
                                    out=Qd[1][:, l0 - NCTX:l0 - NCTX + ln], in0=Qp[:, lo:lo + ln],
                                    in1=Pb[:, lo:lo + ln], op=ALU.mult), reads=["Qp", "Pb"], writes=[("Qd", 1)])
                for d in range(2):
                    for i4 in range(0, 18, 4):
                        nt_ = min(4, 18 - i4)
                        pb = nextps()
                        psb = PS[pb][:, 0:256].bitcast(BF16)
                        for j in range(nt_):
                            i = i4 + j
                            P.op("pe", lambda e, d=d, i=i, j=j, psb=psb: e.transpose(
                                psb[:, j * 128:(j + 1) * 128], Kd[d][:, i * 128:(i + 1) * 128], ident),
                                reads=[("Kd", d), "CB"], writes=[("ps", pb)])
                        P.op("act", lambda e, d=d, i4=i4, nt_=nt_, psb=psb: e.activation(
                            out=Ktok[d][:, i4:i4 + nt_, :],
                            in_=psb[:, 0:nt_ * 128].rearrange("p (i c) -> p i c", i=nt_), func=AF.Identity),
                            reads=[("ps", pb)], writes=[("Ktok", d)])
                osum = rv("SCR", TO, 512, F32, key="osum")
                rsb = rv("SCR", TO + 2048, 512, F32, key="rsb")
                rrb = rv("SCR", TO + 4096, 512, F32, key="rrb")
                ytmp = rv("SCR", TO + 6144, 512, F32, key="ytmp")
                sqb = rv("SCR", TO + 8192, 512, BF16, key="sqb")
                order = [list(range(36)), [3, 2, 1, 0] + list(range(35, 3, -1))]
                maskc = [CONST[:, CS("maskf")], CONST[:, CS("maskb")]]
                obank = {}
                for step in range(36):
                    for d in range(2):
                        n = order[d][step]
                        i, half = n // 2, n % 2
                        pbase = half * 64
                        t0 = n * 64
                        is_lat = n >= 4
                        first = (step == 0)
                        last = (step == 35)
                        Sd = HS[:, d * 128:(d + 1) * 128]
                        Smd = HSm[:, d * 128:(d + 1) * 128]
                        kvb = 6 + d
                        if not last:
                            P.op("pe", lambda e, d=d, i=i, pbase=pbase, kvb=kvb: e.matmul(
                                PS[kvb][:, 0:128], lhsT=Ktok[d][pbase:pbase + 64, i, :],
                                rhs=Vh[pbase:pbase + 64, i, :], start=True, stop=True),
                                reads=[("Ktok", d), "Vh1"], writes=[("ps", kvb)])
                        if is_lat:
                            lt = t0 - NCTX
                            blk = lt // 512
                            ob = d * 2 + (blk % 2)
                            oc = lt % 512
                            atb = 4 + d
                            P.op("pe", lambda e, d=d, t0=t0, lt=lt, pbase=pbase, atb=atb: e.matmul(
                                PS[atb][pbase:pbase + 64, 0:64], lhsT=Kd[d][:, t0:t0 + 64],
                                rhs=Qd[d][:, lt:lt + 64], start=True, stop=True),
                                reads=[("Kd", d), ("Qd", d)], writes=[("ps", atb)])
                            P.op("dve", lambda e, d=d, pbase=pbase, atb=atb: e.tensor_tensor(
                                out=ATm[pbase:pbase + 64, d * 64:(d + 1) * 64], in0=PS[atb][pbase:pbase + 64, 0:64],
                                in1=maskc[d][pbase:pbase + 64, :], op=ALU.mult),
                                reads=[("ps", atb), "CONST"], writes=[("ATm", d)])
                            P.op("pe", lambda e, d=d, i=i, pbase=pbase, ob=ob, oc=oc: e.matmul(
                                PS[ob][:, oc:oc + 64], lhsT=Vh[pbase:pbase + 64, i, :],
                                rhs=ATm[pbase:pbase + 64, d * 64:(d + 1) * 64], start=True, stop=False),
                                reads=["Vh1", ("ATm", d)], writes=[("ps", ob)])
                            P.op("pe", lambda e, d=d, lt=lt, ob=ob, oc=oc, Smd=Smd: e.matmul(
                                PS[ob][:, oc:oc + 64], lhsT=Smd, rhs=Qd[d][:, lt:lt + 64], start=False, stop=True),
                                reads=[("Sm", d), ("Qd", d)], writes=[("ps", ob)])
                            done = (oc == 448) if d == 0 else (oc == 0)
                            if done:
                                b0 = blk * 512
                                if blk not in obank:
                                    obank[blk] = d
                                    P.op("act", lambda e, ob=ob, b0=b0: e.activation(
                                        out=OF[:, b0:b0 + 512], in_=PS[ob][:, 0:512], func=AF.Identity),
                                        reads=[("ps", ob)], writes=[("OF", blk)])
                                else:
                                    P.op("dve", lambda e, ob=ob, b0=b0: e.tensor_tensor(
                                        out=osum[:, :], in0=PS[ob][:, 0:512], in1=OF[:, b0:b0 + 512], op=ALU.add),
                                        reads=[("ps", ob), ("OF", blk)], writes=["osum"])
                                    P.op("act", lambda e: e.activation(out=sqb[:, :], in_=osum[:, :], func=AF.Square),
                                         reads=["osum"], writes=["sqb"])
                                    P.op("pe", lambda e, ob=ob: e.matmul(PS[ob][:, 0:512], lhsT=ones, rhs=sqb[:, :],
                                                                         start=True, stop=True),
                                         reads=["CB", "sqb"], writes=[("ps", ob)])
                                    P.op("act", lambda e, ob=ob: e.activation(
                                        out=rsb[:, :], in_=PS[ob][:, 0:512], func=AF.Sqrt, scale=1.0 / 128.0, bias=EPSB),
                                        reads=[("ps", ob), "EPSB"], writes=["rsb"])
                                    P.op("dve", lambda e: e.reciprocal(out=rrb[:, :], in_=rsb[:, :]),
                                         reads=["rsb"], writes=["rrb"])
                                    P.op("dve", lambda e, h=h: e.scalar_tensor_tensor(
                                        out=ytmp[:, :], in0=osum[:, :], scalar=CONST[:, CS("hgnorm", h, 1)],
                                        in1=rrb[:, :], op0=ALU.mult, op1=ALU.mult),
                                        reads=["osum", "CONST", "rrb"], writes=["ytmp"])
                                    P.op("dve", lambda e, b0=b0: e.tensor_tensor(
                                        out=YH[:, b0:b0 + 512], in0=ytmp[:, :], in1=SG[:, b0:b0 + 512], op=ALU.mult),
                                        reads=["ytmp", "SG"], writes=["YH"])
                        if last:
                            continue
                        if first:
                            P.op("dve", lambda e, d=d, n=n, kvb=kvb, Sd=Sd: e.tensor_scalar(
                                out=Sd, in0=PS[kvb][:, 0:128], scalar1=BE[d][:, n:n + 1], scalar2=None, op0=ALU.mult),
                                reads=[("ps", kvb), ("BE", d)], writes=[("S", d)])
                        else:
                            P.op("dve", lambda e, d=d, n=n, Sd=Sd: e.tensor_scalar(
                                out=Sd, in0=Sd, scalar1=AL[d][:, n:n + 1], scalar2=None, op0=ALU.mult),
                                reads=[("S", d), ("AL", d)], writes=[("S", d)])
                            P.op("dve", lambda e, d=d, n=n, kvb=kvb, Sd=Sd: e.scalar_tensor_tensor(
                                out=Sd, in0=PS[kvb][:, 0:128], scalar=BE[d][:, n:n + 1], in1=Sd,
                                op0=ALU.mult, op1=ALU.add),
                                reads=[("ps", kvb), ("BE", d), ("S", d)], writes=[("S", d)])
                        nn = order[d][step + 1]
                        if nn >= 4:
                            P.op("act", lambda e, d=d, nn=nn, Sd=Sd, Smd=Smd: e.activation(
                                out=Smd, in_=Sd, func=AF.Identity, scale=GA[d][:, nn:nn + 1]),
                                reads=[("S", d), ("GA", d)], writes=[("Sm", d)])
                s_w = wload(wout1_d[h], D)
                resid_add(layer, lambda k, b0, bn: YH[:, b0 - NCTX:b0 - NCTX + bn], lambda k: ["YH"], s_w,
                          lambda k, m, s_w=s_w: WS[s_w][:, m * 128:(m + 1) * 128], 1, 2, blocks=LATBLK)

        stages = stop.split("+")

        if "mix0" in stages:
            mixer0()
        if "ffn0" in stages:
            ffn(0, [(0, NCTX, 0, NCTX, 1),
                    (NCTX, NT, NCTX, 1024, 0),
                    (NCTX, NT, NCTX + 1024, 1024, 0)])
        if "mix1" in stages:
            mixer1()
        if "ffn1" in stages:
            ffn(1, [(NCTX, NT, NCTX, 1024, 0),
                    (NCTX, NT, NCTX + 1024, 1024, 0)])

        if "rawout" in stages:
            for c in range(KC):
                P.out_tokens.append(P.dma("sp", out_d[:, c, :], xT[:, c, NCTX:NT], reads=xk(c, NCTX, NLAT)))
        else:
            OUTB = [rv("SCR", i * 2048, 512, F32, key=("outb", i)) for i in range(2)]
            T = nrm_temps("SCR", 46080 - 10240, "S")
            it = 0
            for (b0, bn) in tok_blocks(NCTX, NT):
                rms_rstd(T, lambda c, b0=b0, bn=bn: xT[:, c, b0:b0 + bn],
                         lambda c, b0=b0, bn=bn: xk(c, b0, bn), KC, bn, float(D), 1)
                for c in range(KC):
                    b = it % 2
                    it += 1
                    P.op("dve", lambda e, c=c, b=b, b0=b0, bn=bn: e.scalar_tensor_tensor(
                        out=OUTB[b][:, :bn], in0=xT[:, c, b0:b0 + bn], scalar=CONST[:, CS("fnorm", c, 1)],
                        in1=T["rstd"][:, :bn], op0=ALU.mult, op1=ALU.mult),
                        reads=xk(c, b0, bn) + ["CONST", ("rstd", "S")], writes=[("outb", b)])
                    P.out_tokens.append(P.dma("sp", out_d[:, c, b0 - NCTX:b0 - NCTX + bn], OUTB[b][:, :bn],
                                              reads=[("outb", b)]))
        P.finish("sp", P.out_tokens)
    return nc


def _fm(v):
    v = np.asarray(v, np.float32)
    n = v.shape[-1] // 128
    return np.moveaxis(v.reshape(v.shape[:-1] + (n, 128)), -1, 0)


def _wk(w):
    K, N = w.shape
    return np.ascontiguousarray(w.reshape(K // 128, 128, N).transpose(1, 0, 2).reshape(128, (K // 128) * N))


def _rope_tables():
    rows_n = NLAT // GRID_W
    row = np.repeat(np.arange(rows_n, dtype=np.float32), GRID_W)
    col = np.tile(np.arange(GRID_W, dtype=np.float32), rows_n)
    n_freq = 16
    inv = (np.float32(10000.0) ** (-np.arange(n_freq, dtype=np.float32) / n_freq)).astype(np.float32)
    ang = np.stack([row[:, None] * inv, col[:, None] * inv], axis=1).astype(np.float32)
    cos, sin = np.cos(ang).astype(np.float32), np.sin(ang).astype(np.float32)
    CT = np.zeros((128, NLAT), np.float32)
    ST = np.zeros((128, NLAT), np.float32)
    for p in range(128):
        q = p % 64
        axis, half, f = q // 32, (q % 32) // 16, q % 16
        CT[p] = cos[:, axis, f]
        ST[p] = sin[:, axis, f] * (-1.0 if half == 0 else 1.0)
    return np.concatenate([CT, ST], axis=1)


def _swap_cols(w):
    N = w.shape[1]
    idx = np.arange(N)
    blk = idx // 32
    r = idx % 32
    return w[:, blk * 32 + (r + 16) % 32]


def prepare_inputs(inp):
    f32 = np.float32
    g = {k: np.asarray(v, f32) for k, v in inp.items()}
    shared = {}
    wm = g["w_mod"]
    shared["wmod"] = np.stack([np.stack([_wk(wm[l][:, p * 512:(p + 1) * 512]) for p in range(12)]) for l in range(2)])
    shared["rope"] = _rope_tables()
    wi = g["ev_w_in"][0]
    qw, kw, vw, uw = wi[:, 0:512], wi[:, 512:1024], wi[:, 1024:1536], wi[:, 1536:2048]
    qs, ks = _swap_cols(qw), _swap_cols(kw)
    heads = []
    for h in range(4):
        sl = slice(h * 128, (h + 1) * 128)
        heads.append(_wk(np.concatenate([qw[:, sl], qs[:, sl], kw[:, sl], ks[:, sl], vw[:, sl]], axis=1)))
    shared["whead"] = np.stack(heads)
    shared["wpool"] = _wk(uw)
    shared["wout0"] = np.ascontiguousarray(g["ev_w_out"][0].reshape(KC, 128, D))
    wu = g["ffn_w_up"]
    shared["wffn"] = np.stack([np.stack([
        _wk(np.concatenate([wu[l][:, c * 128:(c + 1) * 128], wu[l][:, DFF + c * 128: DFF + (c + 1) * 128]], axis=1))
        for c in range(NPAIR)]) for l in range(2)])
    wd = g["ffn_w_down"]
    shared["wdn"] = np.stack([np.stack([
        np.ascontiguousarray(wd[l][:, m * 128:(m + 1) * 128].reshape(NPAIR, 128, 128).transpose(1, 0, 2)
                             .reshape(128, NPAIR * 128)) for m in range(KC)]) for l in range(2)])
    hw = g["hg_w_in"][0]
    hh = []
    for h in range(8):
        cols = [hw[:, part * 1024 + h * 128: part * 1024 + (h + 1) * 128] for part in range(5)]
        hh.append(_wk(np.concatenate(cols, axis=1)))
    shared["whg"] = np.stack(hh)
    shared["wout1"] = np.ascontiguousarray(g["hg_w_out"][0].reshape(KC, 128, D))
    cb = np.zeros((128, 256 + 512), f32)
    cb[:, 0:128] = np.eye(128, dtype=f32)
    cb[:, 128:256] = 1.0
    cb[:, 256:768] = g["pool_w"][0].transpose(1, 0, 2).reshape(128, 512)
    shared["constb"] = cb

    const = np.zeros((128, NCONST), f32)
    bm = g["b_mod"]
    bmT = np.stack([_fm(bm[l]) for l in range(2)], axis=1)
    const[:, CS("bmod")] = np.repeat(bmT.reshape(128, 96), 2, axis=1).reshape(128, 192)
    lq = np.concatenate([g["da_lq1"][0], g["da_lk1"][0], g["da_lq2"][0], g["da_lk2"][0]])
    const[:, CS("lqk")] = lq[None, :]
    const[:, CS("subln")] = _fm(g["da_subln"][0])
    const[:, CS("pscale")] = _fm(g["pool_scale"][0])
    const[:, CS("hgnorm")] = _fm(g["hg_norm"][0])
    const[:, CS("fnorm")] = _fm(g["final_norm"])
    cwT = _fm(g["ffn_conv_w"])
    const[:, CS("convw")] = cwT.transpose(0, 1, 3, 2).reshape(128, -1)
    const[:, CS("convb")] = _fm(g["ffn_conv_b"]).reshape(128, -1)
    const[:, CS("lblog")] = _fm(g["hg_lb_logits"]).reshape(128, -1)
    pe = np.zeros((4, 4, 8), f32)
    for gi, w in enumerate((2, 4, 8, 16)):
        hw_ = w // 2
        for ei, T in enumerate((NCTX, NCTX, NLAT, NLAT)):
            for i in range(hw_):
                t = i if ei % 2 == 0 else T - hw_ + i
                cnt = min(t + w - hw_, T) - max(t - hw_, 0)
                pe[gi, ei, i] = 1.0 / cnt
    const[:, CS("pedge")] = pe.reshape(1, -1)
    s_idx = np.arange(128)[:, None] % 64
    t_idx = np.arange(64)[None, :]
    const[:, CS("maskf")] = (t_idx >= s_idx).astype(f32)
    const[:, CS("maskb")] = (t_idx <= s_idx).astype(f32)

    cctx = _fm(g["c_ctx"])
    in_maps = []
    for b in range(NCORES):
        cst = const.copy()
        cst[:, CS("cT", 0, 8)] = _fm(g["c"][b])
        cst[:, CS("cT", 8, 8)] = cctx
        tok = np.concatenate([g["ctx"][b], g["x"][b]], axis=0)
        xTh = np.ascontiguousarray(tok.T.reshape(KC, 128, NT).transpose(1, 0, 2))
        m = dict(shared)
        m["const"] = cst
        m["xT"] = xTh
        in_maps.append(m)
    return in_maps


_CACHE = {}


def run(inputs, stop="full", trace=False, cores=None):
    in_maps = prepare_inputs(inputs)
    if cores is not None:
        in_maps = [in_maps[b] for b in cores]
    if stop not in _CACHE:
        _CACHE[stop] = build_program(stop)
    nc = _CACHE[stop]
    res = run_bass_kernel_spmd(nc, in_maps, core_ids=list(range(len(in_maps))), trace=trace)
    outs = []
    for r in res.results:
        o = np.asarray(r["outT"], np.float32)
        outs.append(o.transpose(2, 1, 0).reshape(NLAT, D))
    return np.stack(outs), res


def kernel(**inputs):
    out, _ = run(inputs, stop="mix0+ffn0+mix1+ffn1")
    return out
```

```python
import math
from contextlib import ExitStack

import numpy as np
import concourse.bass as bass
import concourse.mybir as mybir
from concourse.bass_utils import run_bass_kernel_spmd

F32 = mybir.dt.float32
BF16 = mybir.dt.bfloat16
AF = mybir.ActivationFunctionType
ALU = mybir.AluOpType

NCORES = 8
D = 1024
KC = 8
NCTX = 256
NLAT = 2048
NT = NCTX + NLAT
DFF = 2816
NPAIR = DFF // 128
EPS = 1e-6
GRID_W = 64
SLOT_ELEMS = 5120
NSLOT = 2

_c = {}
_off = 0


def _cdef(name, n):
    global _off
    _c[name] = (_off, n)
    _off += n


_cdef("cT", 16)
_cdef("bmod", 2 * 96)
_cdef("lqk", 4 * 64)
_cdef("subln", 4)
_cdef("pscale", 4)
_cdef("hgnorm", 8)
_cdef("fnorm", 8)
_cdef("convw", 2 * 44 * 3)
_cdef("convb", 2 * 44)
_cdef("lblog", 2 * 2 * 8)
_cdef("pedge", 4 * 4 * 8)
_cdef("maskf", 64)
_cdef("maskb", 64)
_cdef("m01", 512)
NCONST = _off


def CS(name, a=0, n=None):
    o, sz = _c[name]
    if n is None:
        n = sz - a
    return slice(o + a, o + a + n)


class Prog:
    ROT = 7000

    def __init__(self, nc, es):
        self.nc = nc
        self.es = es
        self.eng = {"pe": nc.tensor, "act": nc.scalar, "dve": nc.vector,
                    "pool": nc.gpsimd, "sp": nc.sync}
        self.sem = {}
        self.cnt = {}
        self.nsem = 0
        for e in self.eng:
            self.sem[e] = self._newsem(e)
            self.cnt[e] = 0
        self.known = {e: {} for e in self.eng}
        self.res = {}
        self.dq = {}
        for q in ("sp", "pool", "act"):
            self.dq[q] = {"sems": [[self._newsem("d" + q), 0] for _ in range(8)], "rr": 0}
        self.out_tokens = []
        self.active = {}
        self.retired = set()

    def _newsem(self, tag):
        self.nsem += 1
        return self.es.enter_context(self.nc.semaphore(f"s_{tag}_{self.nsem}"))

    def _need(self, e, tok):
        sem, val, owner = tok
        if owner == "pe" and e == "pe":
            return
        k = self.known[e]
        if k.get(sem.name, 0) >= val:
            return
        self.eng[e].wait_ge(sem, val)
        k[sem.name] = val

    def _deps(self, e, reads, writes):
        for key in list(reads) + list(writes):
            if key in self.retired:
                raise RuntimeError(f"use of retired key {key}")
        for key in reads:
            r = self.res.get(key)
            if r and r[0] is not None:
                self._need(e, r[0])
        for key in writes:
            r = self.res.get(key)
            if r:
                if r[0] is not None:
                    self._need(e, r[0])
                for t in r[1]:
                    self._need(e, t)

    def _commit(self, tok, reads, writes):
        for key in reads:
            r = self.res.setdefault(key, [None, []])
            r[1] = [t for t in r[1] if t[0].name != tok[0].name] + [tok]
        for key in writes:
            self.res[key] = [tok, []]

    def op(self, e, fn, reads=(), writes=()):
        self._deps(e, reads, writes)
        if self.cnt[e] >= self.ROT:
            self.sem[e] = self._newsem(e)
            self.cnt[e] = 0
        inst = fn(self.eng[e])
        self.cnt[e] += 1
        inst.then_inc(self.sem[e], 1)
        tok = (self.sem[e], self.cnt[e], e)
        self._commit(tok, reads, writes)
        return tok

    def dma(self, q, out, in_, reads=(), writes=()):
        self._deps(q, reads, writes)
        dq = self.dq[q]
        slot = dq["sems"][dq["rr"] % len(dq["sems"])]
        dq["rr"] += 1
        if slot[1] > 0:
            self._need(q, (slot[0], slot[1], "dma"))
        inst = self.eng[q].dma_start(out=out, in_=in_)
        slot[1] += 16
        inst.then_inc(slot[0], 16)
        tok = (slot[0], slot[1], "dma")
        self._commit(tok, reads, writes)
        return tok

    def alias(self, new_keys, old_keys):
        toks = []
        for k in old_keys:
            r = self.res.get(k)
            if r:
                if r[0] is not None:
                    toks.append(r[0])
                toks.extend(r[1])
        for nk in new_keys:
            r = self.res.setdefault(nk, [None, []])
            r[1] = r[1] + toks

    def claim(self, region, lo, hi, key):
        act = self.active.setdefault(region, {})
        if key in act and act[key] == (lo, hi):
            return
        for k, (l, h) in list(act.items()):
            if l < hi and lo < h:
                self.alias([key], [k])
                del act[k]
                self.retired.add(k)
        act[key] = (lo, hi)
        self.retired.discard(key)

    def finish(self, e, toks):
        for t in toks:
            self._need(e, t)


def tok_blocks(lo, hi, step=512):
    out = []
    t = lo
    while t < hi:
        n = min(step, hi - t)
        out.append((t, n))
        t += n
    return out


def build_program(stop="full", debug=False):
    nc = bass.Bass("TRN2", target_bir_lowering=False)

    def din(name, shape, dt=F32):
        return nc.dram_tensor(name, list(shape), dt, kind="ExternalInput").ap()

    xT_d = din("xT", [128, KC, NT])
    const_d = din("const", [128, NCONST])
    constb_d = din("constb", [128, 256 + 512])
    wmod_d = din("wmod", [2, 12, 128, KC * 512])
    rope_d = din("rope", [128, 2 * NLAT])
    whead_d = din("whead", [4, 128, KC * 640])
    wpool_d = din("wpool", [128, KC * 512])
    wout0_d = din("wout0", [KC, 128, D])
    wffn_d = din("wffn", [2, NPAIR, 128, KC * 256])
    wdn_d = din("wdn", [2, KC, 128, NPAIR * 128])
    whg_d = din("whg", [8, 128, KC * 640])
    wout1_d = din("wout1", [KC, 128, D])
    out_d = nc.dram_tensor("outT", [128, KC, NLAT], F32, kind="ExternalOutput").ap()

    with ExitStack() as es:
        P = Prog(nc, es)

        def sb(name, shape, dt):
            return es.enter_context(nc.sbuf_tensor(name, list(shape), dt))

        xT = sb("xT_sb", [128, KC, NT], F32)
        REG = {"HT": sb("HT", [128, 18432], BF16),
               "SCR": sb("SCR", [128, 23040], BF16),
               "ROPE": sb("ROPE", [128, 8192], BF16)}
        WS = [sb(f"WS{i}", [128, SLOT_ELEMS], BF16) for i in range(NSLOT)]
        CONST = sb("CONST", [128, NCONST], F32)
        CB = sb("CB", [128, 256 + 512], BF16)
        MOD = sb("MOD", [128, 2 * 96], F32)
        MOD1 = sb("MOD1", [128, 2 * 96], F32)
        SC = sb("SC", [128, KC * 2], BF16)
        MISC = sb("MISC", [128, 160], F32)
        PS = [es.enter_context(nc.psum_tensor(f"ps{i}", [128, 512], F32)) for i in range(8)]

        ident = CB[:, 0:128]
        ones = CB[:, 128:256]

        def rv(region, off, nelem, dt, key=None, keys=None):
            esz = 2 if dt == BF16 else 4
            assert off % 4 == 0
            v = REG[region][:, off // 2: off // 2 + nelem * esz // 2]
            if dt != BF16:
                v = v.bitcast(dt)
            if key is not None:
                P.claim(region, off, off + nelem * esz, key)
            if keys is not None:
                for (k, so, n) in keys:
                    P.claim(region, off + so * esz, off + (so + n) * esz, k)
            return v

        def xk(c, t0, n):
            return [("xT", c, b) for b in range(t0 // 256, (t0 + n + 255) // 256)]

        P.dma("sp", CONST[:], const_d, writes=["CONST"])
        P.dma("pool", CB[:], constb_d, writes=["CB"])
        for c in range(KC):
            P.dma("sp", xT[:, c, :], xT_d[:, c, :], writes=xk(c, 0, NT))
        P.op("dve", lambda e: e.memset(MISC[:, 0:1], EPS), writes=["EPSB"])
        EPSB = MISC[:, 0:1]

        state = {"slot": 0}

        def wload(src_ap, nelem):
            s = state["slot"] % NSLOT
            state["slot"] += 1
            P.dma("pool", WS[s][:, 0:nelem], src_ap, writes=[("WS", s)])
            return s

        SC3 = SC[:, :].rearrange("p (k two) -> p k two", two=2)
        P.op("act", lambda e: e.activation(out=SC3[:, :, 0], in_=CONST[:, CS("cT", 0, 8)], func=AF.Silu),
             reads=["CONST"], writes=["SC0"])
        P.op("act", lambda e: e.activation(out=SC3[:, :, 1], in_=CONST[:, CS("cT", 8, 8)], func=AF.Silu),
             reads=["CONST"], writes=["SC1"])
        for layer in range(2):
            for piece in range(12):
                s = wload(wmod_d[layer, piece], KC * 512)
                for sub in range(4):
                    jc = piece * 4 + sub
                    col = layer * 96 + jc * 2
                    for kc in range(KC):
                        P.op("pe", lambda e, s=s, kc=kc, sub=sub, col=col: e.matmul(
                            PS[0][:, col:col + 2],
                            lhsT=WS[s][:, kc * 512 + sub * 128: kc * 512 + sub * 128 + 128],
                            rhs=SC[:, kc * 2: kc * 2 + 2],
                            start=(kc == 0), stop=(kc == KC - 1)),
                            reads=[("WS", s), "SC0", "SC1"], writes=[("ps", 0)])
        P.op("dve", lambda e: e.tensor_tensor(out=MOD[:, :], in0=PS[0][:, 0:192], in1=CONST[:, CS("bmod")],
                                               op=ALU.add),
             reads=[("ps", 0), "CONST"], writes=["MOD"])
        P.op("dve", lambda e: e.tensor_scalar_add(out=MOD1[:, :], in0=MOD[:, :], scalar1=1.0),
             reads=["MOD"], writes=["MOD1"])

        def modv(layer, j, chunk, which, plus1=False):
            t = MOD1 if plus1 else MOD
            col = layer * 96 + (j * 8 + chunk) * 2 + which
            return t[:, col:col + 1]

        def nrm_temps(region, off, tag):
            t = {"tag": tag}
            t["sq"] = [rv(region, off + i * 1024, 512, BF16, key=("sq", tag, i)) for i in range(2)]
            t["rs"] = rv(region, off + 2048, 512, F32, key=("rs", tag))
            t["rstd"] = rv(region, off + 4096, 512, F32, key=("rstd", tag))
            t["tmp"] = [rv(region, off + 6144 + i * 2048, 512, F32, key=("ntmp", tag, i)) for i in range(2)]
            return t

        def rms_rstd(T, src_fn, rkeys, nchunk, n, denom, psb):
            tag = T["tag"]
            for c in range(nchunk):
                b = c % 2
                P.op("act", lambda e, c=c, b=b: e.activation(out=T["sq"][b][:, :n], in_=src_fn(c), func=AF.Square),
                     reads=rkeys(c), writes=[("sq", tag, b)])
                P.op("pe", lambda e, c=c, b=b: e.matmul(PS[psb][:, :n], lhsT=ones, rhs=T["sq"][b][:, :n],
                                                        start=(c == 0), stop=(c == nchunk - 1)),
                     reads=[("sq", tag, b), "CB"], writes=[("ps", psb)])
            P.op("act", lambda e: e.activation(out=T["rs"][:, :n], in_=PS[psb][:, :n], func=AF.Ln,
                                               scale=1.0 / denom, bias=EPSB),
                 reads=[("ps", psb), "EPSB"], writes=[("rs", tag)])
            P.op("act", lambda e: e.activation(out=T["rstd"][:, :n], in_=T["rs"][:, :n], func=AF.Exp, scale=-0.5),
                 reads=[("rs", tag)], writes=[("rstd", tag)])

        def norm_mod(T, layer, jshift, jscale, t0, n, which, out_fn, out_keys, psb=1):
            tag = T["tag"]
            rms_rstd(T, lambda c: xT[:, c, t0:t0 + n], lambda c: xk(c, t0, n), KC, n, float(D), psb)
            for c in range(KC):
                b = c % 2
                P.op("dve", lambda e, c=c, b=b: e.scalar_tensor_tensor(
                    out=T["tmp"][b][:, :n], in0=xT[:, c, t0:t0 + n], scalar=modv(layer, jscale, c, which, True),
                    in1=T["rstd"][:, :n], op0=ALU.mult, op1=ALU.mult),
                    reads=xk(c, t0, n) + ["MOD1", ("rstd", tag)], writes=[("ntmp", tag, b)])
                P.op("act", lambda e, c=c, b=b: e.activation(
                    out=out_fn(c), in_=T["tmp"][b][:, :n], func=AF.Identity,
                    bias=modv(layer, jshift, c, which), scale=1.0),
                    reads=[("ntmp", tag, b), "MOD"], writes=out_keys(c))

        def ffn(layer, groups):
            W2 = 1026
            cw = lambda c, tap: CONST[:, CS("convw", (layer * 44 + c) * 3 + tap, 1)]
            cb = lambda c: CONST[:, CS("convb", layer * 44 + c, 1)]
            h2 = rv("HT", 0, KC * W2, BF16,
                    keys=[(("h2", c), c * W2, W2) for c in range(KC)]).rearrange("p (k t) -> p k t", k=KC)
            ctmp = [[rv("HT", KC * W2 * 2 + (i * 2 + j) * 1408, 352, F32, key=("ctmp", i, j))
                     for j in range(2)] for i in range(2)]
            T = nrm_temps("HT", KC * W2 * 2 + 4 * 1408, "H")
            stash = rv("HT", KC * W2 * 2 + 4 * 1408 + 10240, KC * 8, BF16,
                       keys=[(("stash", c), c * 8, 8) for c in range(KC)]).rearrange("p (k t) -> p k t", k=KC)
            gbuf = rv("SCR", 0, NPAIR * 1024, BF16,
                      keys=[(("g", c), c * 1024, 1024) for c in range(NPAIR)]
                      ).rearrange("p (c t) -> p c t", c=NPAIR)
            h2keys = [("h2", c) for c in range(KC)]
            for gi, (slo, shi, t0, G, which) in enumerate(groups):
                if t0 - 1 >= slo:
                    norm_mod(T, layer, 3, 4, t0 - 2, 2, which,
                             lambda c, gi=gi: stash[:, c, 2 * gi:2 * gi + 2],
                             lambda c: [("stash", c)])
            for gi, (slo, shi, t0, G, which) in enumerate(groups):
                hi = min(t0 + G + 1, shi)
                if t0 - 1 >= slo:
                    for c in range(KC):
                        P.op("dve", lambda e, c=c, gi=gi: e.tensor_copy(out=h2[:, c, 0:1],
                                                                        in_=stash[:, c, 2 * gi + 1:2 * gi + 2]),
                             reads=[("stash", c)], writes=[("h2", c)])
                else:
                    P.op("dve", lambda e: e.memset(h2[:, :, 0:1], 0.0), writes=h2keys)
                if hi < t0 + G + 1:
                    P.op("dve", lambda e, G=G: e.memset(h2[:, :, G + 1:G + 2], 0.0), writes=h2keys)
                for (b0, bn) in tok_blocks(t0, hi):
                    j0 = b0 - (t0 - 1)
                    norm_mod(T, layer, 3, 4, b0, bn, which,
                             lambda c, j0=j0, bn=bn: h2[:, c, j0:j0 + bn],
                             lambda c: [("h2", c)])
                npiece = (G + 341) // 342
                base = G // npiece
                pieces = []
                o = 0
                for i in range(npiece):
                    n = base + (1 if i < G % npiece else 0)
                    pieces.append((o, n))
                    o += n
                it = 0
                for c in range(NPAIR):
                    s = wload(wffn_d[layer, c], KC * 256)
                    for (o, n) in pieces:
                        bsel = it % 2
                        it += 1
                        pa, pg = 2 + bsel * 2, 3 + bsel * 2
                        for half, pb in ((0, pa), (1, pg)):
                            for kc in range(KC):
                                P.op("pe", lambda e, s=s, kc=kc, half=half, pb=pb, o=o, n=n: e.matmul(
                                    PS[pb][:, 0:n + 2],
                                    lhsT=WS[s][:, kc * 256 + half * 128: kc * 256 + half * 128 + 128],
                                    rhs=h2[:, kc, o:o + n + 2],
                                    start=(kc == 0), stop=(kc == KC - 1)),
                                    reads=[("WS", s), ("h2", kc)], writes=[("ps", pb)])
                        for half, pb in ((0, pa), (1, pg)):
                            cc = c + half * NPAIR
                            tb = ctmp[bsel][half]
                            P.op("act", lambda e, pb=pb, tb=tb, cc=cc, n=n: e.activation(
                                out=tb[:, 0:n], in_=PS[pb][:, 1:n + 1], func=AF.Identity,
                                scale=cw(cc, 1), bias=cb(cc)),
                                reads=[("ps", pb), "CONST"], writes=[("ctmp", bsel, half)])
                            P.op("dve", lambda e, pb=pb, tb=tb, cc=cc, n=n: e.scalar_tensor_tensor(
                                out=tb[:, 0:n], in0=PS[pb][:, 0:n], scalar=cw(cc, 0), in1=tb[:, 0:n],
                                op0=ALU.mult, op1=ALU.add),
                                reads=[("ps", pb), "CONST", ("ctmp", bsel, half)], writes=[("ctmp", bsel, half)])
                            P.op("dve", lambda e, pb=pb, tb=tb, cc=cc, n=n: e.scalar_tensor_tensor(
                                out=tb[:, 0:n], in0=PS[pb][:, 2:n + 2], scalar=cw(cc, 2), in1=tb[:, 0:n],
                                op0=ALU.mult, op1=ALU.add),
                                reads=[("ps", pb), "CONST", ("ctmp", bsel, half)], writes=[("ctmp", bsel, half)])
                        P.op("act", lambda e, bsel=bsel, n=n: e.activation(
                            out=ctmp[bsel][1][:, 0:n], in_=ctmp[bsel][1][:, 0:n], func=AF.Silu),
                            reads=[("ctmp", bsel, 1)], writes=[("ctmp", bsel, 1)])
                        P.op("dve", lambda e, bsel=bsel, n=n, c=c, o=o: e.tensor_tensor(
                            out=gbuf[:, c, o:o + n], in0=ctmp[bsel][0][:, 0:n], in1=ctmp[bsel][1][:, 0:n],
                            op=ALU.mult),
                            reads=[("ctmp", bsel, 0), ("ctmp", bsel, 1)], writes=[("g", c)])
                it = 0
                for m in range(KC):
                    s = wload(wdn_d[layer, m], NPAIR * 128)
                    for (b0, bn) in tok_blocks(0, G):
                        pb = 2 + (it % 4)
                        it += 1
                        for c in range(NPAIR):
                            P.op("pe", lambda e, s=s, c=c, pb=pb, b0=b0, bn=bn: e.matmul(
                                PS[pb][:, 0:bn], lhsT=WS[s][:, c * 128:(c + 1) * 128],
                                rhs=gbuf[:, c, b0:b0 + bn], start=(c == 0), stop=(c == NPAIR - 1)),
                                reads=[("WS", s), ("g", c)], writes=[("ps", pb)])
                        tt = t0 + b0
                        P.op("dve", lambda e, pb=pb, m=m, tt=tt, bn=bn: e.scalar_tensor_tensor(
                            out=xT[:, m, tt:tt + bn], in0=PS[pb][:, 0:bn], scalar=modv(layer, 5, m, which),
                            in1=xT[:, m, tt:tt + bn], op0=ALU.mult, op1=ALU.add),
                            reads=[("ps", pb), "MOD"] + xk(m, tt, bn), writes=xk(m, tt, bn))

        def hT_view():
            return rv("HT", 0, KC * NT, BF16,
                      keys=[(("hT", c), c * NT, NT) for c in range(KC)]).rearrange("p (k t) -> p k t", k=KC)

        ALLBLK = [(0, NCTX, 1)] + [(b0, bn, 0) for (b0, bn) in tok_blocks(NCTX, NT)]

        def compute_hT(layer):
            hT = hT_view()
            T = nrm_temps("SCR", 46080 - 10240, "S")
            for (b0, bn, which) in ALLBLK:
                norm_mod(T, layer, 0, 1, b0, bn, which,
                         lambda c, b0=b0, bn=bn: hT[:, c, b0:b0 + bn], lambda c: [("hT", c)])
            return hT

        hkeys = [("hT", c) for c in range(KC)]
        psrr = {"i": 0}

        def nextps(lo=0, hi=8):
            i = lo + psrr["i"] % (hi - lo)
            psrr["i"] += 1
            return i

        def resid_add(layer, ysrc_fn, ykeys, s, wcol_fn, nk, jgate, blocks=None):
            for m in range(KC):
                for (b0, bn, which) in (blocks or ALLBLK):
                    pb = nextps()
                    for k in range(nk):
                        P.op("pe", lambda e, m=m, k=k, pb=pb, b0=b0, bn=bn: e.matmul(
                            PS[pb][:, 0:bn], lhsT=wcol_fn(k, m), rhs=ysrc_fn(k, b0, bn),
                            start=(k == 0), stop=(k == nk - 1)),
                            reads=[("WS", s)] + ykeys(k), writes=[("ps", pb)])
                    P.op("dve", lambda e, m=m, pb=pb, b0=b0, bn=bn, which=which: e.scalar_tensor_tensor(
                        out=xT[:, m, b0:b0 + bn], in0=PS[pb][:, 0:bn], scalar=modv(layer, jgate, m, which),
                        in1=xT[:, m, b0:b0 + bn], op0=ALU.mult, op1=ALU.add),
                        reads=[("ps", pb), "MOD"] + xk(m, b0, bn), writes=xk(m, b0, bn))

        def mixer0():
            layer = 0
            hT = compute_hT(layer)
            WP = 2336

            def col(t):
                return t + 8 if t < NCTX else t + 24
            U = rv("SCR", 0, WP, F32, key="U")
            T1 = rv("SCR", 9344, WP, F32, key="T1")
            dbuf = rv("SCR", 18688, WP, BF16, key="dbuf")
            ypool = rv("SCR", 23360, 4 * NT, BF16,
                       keys=[(("ypool", g), g * NT, NT) for g in range(4)]).rearrange("p (g t) -> p g t", g=4)
            T2 = rv("ROPE", 0, WP, F32, key="T2")
            s_u = wload(wpool_d, KC * 512)
            for (a, b) in ((0, 8), (264, 280), (2328, 2336)):
                P.op("dve", lambda e, a=a, b=b: e.memset(U[:, a:b], 0.0), writes=["U"])
            tmpE = MISC[:, 8:16]
            for g, w in enumerate((2, 4, 8, 16)):
                for (b0, bn, which) in ALLBLK:
                    pb = nextps()
                    for kc in range(KC):
                        P.op("pe", lambda e, kc=kc, pb=pb, b0=b0, bn=bn, g=g: e.matmul(
                            PS[pb][:, 0:bn], lhsT=WS[s_u][:, kc * 512 + g * 128: kc * 512 + g * 128 + 128],
                            rhs=hT[:, kc, b0:b0 + bn], start=(kc == 0), stop=(kc == KC - 1)),
                            reads=[("WS", s_u), ("hT", kc)], writes=[("ps", pb)])
                    P.op("act", lambda e, pb=pb, b0=b0, bn=bn: e.activation(
                        out=U[:, col(b0):col(b0) + bn], in_=PS[pb][:, 0:bn], func=AF.Identity),
                        reads=[("ps", pb)], writes=["U"])
                P.op("dve", lambda e: e.tensor_tensor(out=T1[:, 1:WP], in0=U[:, 0:WP - 1], in1=U[:, 1:WP], op=ALU.add),
                     reads=["U"], writes=["T1"])
                A, Akey = T1, "T1"
                if w >= 4:
                    P.op("dve", lambda e: e.tensor_tensor(out=T2[:, 2:WP - 1], in0=T1[:, 1:WP - 2], in1=T1[:, 3:WP],
                                                          op=ALU.add), reads=["T1"], writes=["T2"])
                    A, Akey = T2, "T2"
                if w >= 8:
                    P.op("dve", lambda e: e.tensor_tensor(out=T1[:, 4:WP - 3], in0=T2[:, 2:WP - 5], in1=T2[:, 6:WP - 1],
                                                          op=ALU.add), reads=["T2"], writes=["T1"])
                    A, Akey = T1, "T1"
                if w >= 16:
                    P.op("dve", lambda e: e.tensor_tensor(out=T2[:, 8:WP - 7], in0=T1[:, 4:WP - 11], in1=T1[:, 12:WP - 3],
                                                          op=ALU.add), reads=["T1"], writes=["T2"])
                    A, Akey = T2, "T2"
                P.op("dve", lambda e, A=A, w=w: e.scalar_tensor_tensor(
                    out=dbuf[:, 8:2328], in0=A[:, 8:2328], scalar=1.0 / w, in1=U[:, 8:2328],
                    op0=ALU.mult, op1=ALU.subtract), reads=[Akey, "U"], writes=["dbuf"])
                hw = w // 2
                for ei, c0 in enumerate((col(0), col(NCTX - 1) + 1 - hw, col(NCTX), col(NT - 1) + 1 - hw)):
                    tbl = CONST[:, CS("pedge", (g * 4 + ei) * 8, hw)]
                    P.op("dve", lambda e, A=A, c0=c0, tbl=tbl, hw=hw: e.tensor_tensor(
                        out=tmpE[:, 0:hw], in0=A[:, c0:c0 + hw], in1=tbl, op=ALU.mult),
                        reads=[Akey, "CONST"], writes=["tmpE"])
                    P.op("dve", lambda e, c0=c0, hw=hw: e.tensor_tensor(
                        out=dbuf[:, c0:c0 + hw], in0=tmpE[:, 0:hw], in1=U[:, c0:c0 + hw], op=ALU.subtract),
                        reads=["tmpE", "U", "dbuf"], writes=["dbuf"])
                for (b0, bn, which) in ALLBLK:
                    pb = nextps()
                    P.op("pe", lambda e, pb=pb, b0=b0, bn=bn, g=g: e.matmul(
                        PS[pb][:, 0:bn], lhsT=CB[:, 256 + g * 128: 256 + (g + 1) * 128],
                        rhs=dbuf[:, col(b0):col(b0) + bn], start=True, stop=True),
                        reads=["CB", "dbuf"], writes=[("ps", pb)])
                    P.op("act", lambda e, pb=pb, b0=b0, bn=bn, g=g: e.activation(
                        out=ypool[:, g, b0:b0 + bn], in_=PS[pb][:, 0:bn], func=AF.Identity,
                        scale=CONST[:, CS("pscale", g, 1)]),
                        reads=[("ps", pb), "CONST"], writes=[("ypool", g)])
            s_w = state["slot"] % NSLOT
            state["slot"] += 1
            P.dma("pool", WS[s_w][:, 0:4096].rearrange("p (k n) -> p k n", k=4),
                  wout0_d[4:8].rearrange("k p n -> p k n"), writes=[("WS", s_w)])
            resid_add(layer, lambda k, b0, bn: ypool[:, k, b0:b0 + bn], lambda k: [("ypool", k)], s_w,
                      lambda k, m: WS[s_w][:, k * 1024 + m * 128: k * 1024 + m * 128 + 128], 4, 2)

            ropeT = rv("ROPE", 0, 2 * NLAT, F32, key="rope")
            P.dma("sp", ropeT, rope_d, writes=["rope"])
            CTt, STt = ropeT[:, 0:NLAT], ropeT[:, NLAT:2 * NLAT]
            P.op("dve", lambda e: e.tensor_tensor(out=MISC[:, 16:80], in0=CONST[:, CS("lqk", 0, 64)],
                                                  in1=CONST[:, CS("lqk", 64, 64)], op=ALU.mult),
                 reads=["CONST"], writes=["lam_p1"])
            P.op("dve", lambda e: e.tensor_tensor(out=MISC[:, 80:144], in0=CONST[:, CS("lqk", 128, 64)],
                                                  in1=CONST[:, CS("lqk", 192, 64)], op=ALU.mult),
                 reads=["CONST"], writes=["lam_p2"])
            P.op("dve", lambda e: e.reduce_sum(out=MISC[:, 1:2], in_=MISC[:, 16:80], axis=mybir.AxisListType.X),
                 reads=["lam_p1"], writes=["lam_s1"])
            P.op("dve", lambda e: e.reduce_sum(out=MISC[:, 2:3], in_=MISC[:, 80:144], axis=mybir.AxisListType.X),
                 reads=["lam_p2"], writes=["lam_s2"])
            P.op("act", lambda e: e.activation(out=MISC[:, 1:3], in_=MISC[:, 1:3], func=AF.Exp),
                 reads=["lam_s1", "lam_s2"], writes=["lam_e"])
            P.op("dve", lambda e: e.tensor_tensor(out=MISC[:, 3:4], in0=MISC[:, 2:3], in1=MISC[:, 1:2],
                                                  op=ALU.subtract), reads=["lam_e"], writes=["neglam0"])
            lambda_init = 0.8 - 0.6 * math.exp(-0.3 * 0)
            P.op("dve", lambda e: e.tensor_scalar_add(out=MISC[:, 3:4], in0=MISC[:, 3:4], scalar1=-lambda_init),
                 reads=["neglam0"], writes=["neglam"])
            P.op("dve", lambda e: e.tensor_scalar_mul(out=MISC[:, 4:8], in0=CONST[:, CS("subln")],
                                                      scalar1=1.0 - lambda_init),
                 reads=["CONST"], writes=["G08"])
            neglam = MISC[:, 3:4]

            qT = rv("SCR", 0, NT, BF16, key="qT")
            kT = rv("SCR", 4608, NT, BF16, key="kT")
            Vh = rv("SCR", 9216, 18 * 128, BF16, key="Vh").rearrange("p (i d) -> p i d", i=18)
            yh = rv("SCR", 13824, NT, BF16, key="yh")
            E = [[rv("SCR", 18432 + (m * 2 + b) * 1024, 512, BF16, key=("E", m, b)) for b in range(2)]
                 for m in range(2)]
            rt = [rv("SCR", 22528 + i * 2048, 512, F32, key=("rt", i)) for i in range(2)]
            r0 = rv("SCR", 26624, 512, F32, key="r0")
            r1 = rv("SCR", 28672, 512, F32, key="r1")
            t0b = rv("SCR", 30720, 512, F32, key="t0b")
            t1b = rv("SCR", 32768, 512, F32, key="t1b")
            sqh = rv("SCR", 34816, 512, BF16, key="sqh")
            rsh = rv("SCR", 35840, 512, F32, key="rsh")
            rrh = rv("SCR", 37888, 512, F32, key="rrh")
            for h in range(4):
                s = wload(whead_d[h], KC * 640)
                W = lambda kc, c0: WS[s][:, kc * 640 + c0: kc * 640 + c0 + 128]
                for (dst, dkey, c0) in ((qT, "qT", 0), (kT, "kT", 256)):
                    for (b0, bn, which) in ALLBLK:
                        pa = nextps()
                        for kc in range(KC):
                            P.op("pe", lambda e, kc=kc, pa=pa, b0=b0, bn=bn, c0=c0: e.matmul(
                                PS[pa][:, 0:bn], lhsT=W(kc, c0), rhs=hT[:, kc, b0:b0 + bn],
                                start=(kc == 0), stop=(kc == KC - 1)),
                                reads=[("WS", s), ("hT", kc)], writes=[("ps", pa)])
                        if which == 1:
                            P.op("act", lambda e, pa=pa, b0=b0, bn=bn, dst=dst: e.activation(
                                out=dst[:, b0:b0 + bn], in_=PS[pa][:, 0:bn], func=AF.Identity),
                                reads=[("ps", pa)], writes=[dkey])
                            continue
                        pb = nextps()
                        for kc in range(KC):
                            P.op("pe", lambda e, kc=kc, pb=pb, b0=b0, bn=bn, c0=c0: e.matmul(
                                PS[pb][:, 0:bn], lhsT=W(kc, c0 + 128), rhs=hT[:, kc, b0:b0 + bn],
                                start=(kc == 0), stop=(kc == KC - 1)),
                                reads=[("WS", s), ("hT", kc)], writes=[("ps", pb)])
                        l0 = b0 - NCTX
                        P.op("dve", lambda e, pa=pa, bn=bn, l0=l0: e.tensor_tensor(
                            out=rt[0][:, 0:bn], in0=PS[pa][:, 0:bn], in1=CTt[:, l0:l0 + bn], op=ALU.mult),
                            reads=[("ps", pa), "rope"], writes=[("rt", 0)])
                        P.op("dve", lambda e, pb=pb, bn=bn, l0=l0: e.tensor_tensor(
                            out=rt[1][:, 0:bn], in0=PS[pb][:, 0:bn], in1=STt[:, l0:l0 + bn], op=ALU.mult),
                            reads=[("ps", pb), "rope"], writes=[("rt", 1)])
                        P.op("dve", lambda e, b0=b0, bn=bn, dst=dst: e.tensor_tensor(
                            out=dst[:, b0:b0 + bn], in0=rt[0][:, 0:bn], in1=rt[1][:, 0:bn], op=ALU.add),
                            reads=[("rt", 0), ("rt", 1)], writes=[dkey])
                for i4 in range(0, 18, 4):
                    nt_ = min(4, 18 - i4)
                    pb = nextps()
                    for j in range(nt_):
                        i = i4 + j
                        for kc in range(KC):
                            P.op("pe", lambda e, kc=kc, pb=pb, i=i, j=j: e.matmul(
                                PS[pb][:, j * 128:(j + 1) * 128], lhsT=hT[:, kc, i * 128:(i + 1) * 128],
                                rhs=W(kc, 512), start=(kc == 0), stop=(kc == KC - 1)),
                                reads=[("WS", s), ("hT", kc)], writes=[("ps", pb)])
                    P.op("act", lambda e, pb=pb, i4=i4, nt_=nt_: e.activation(
                        out=Vh[:, i4:i4 + nt_, :], in_=PS[pb][:, 0:nt_ * 128].rearrange("p (i d) -> p i d", i=nt_),
                        func=AF.Identity), reads=[("ps", pb)], writes=["Vh"])
                for (q0, n, which) in ALLBLK:
                    nkt = 2 if which == 1 else 18
                    for i in range(nkt):
                        b = i % 2
                        for m in range(2):
                            P.op("pe", lambda e, m=m, b=b, i=i, q0=q0, n=n: e.matmul(
                                PS[4 + m * 2 + b][:, 0:n], lhsT=kT[64 * m:64 * m + 64, i * 128:(i + 1) * 128],
                                rhs=qT[64 * m:64 * m + 64, q0:q0 + n], start=True, stop=True),
                                reads=["kT", "qT"], writes=[("ps", 4 + m * 2 + b)])
                        for m in range(2):
                            P.op("act", lambda e, m=m, b=b, n=n: e.activation(
                                out=E[m][b][:, 0:n], in_=PS[4 + m * 2 + b][:, 0:n], func=AF.Exp, scale=0.125),
                                reads=[("ps", 4 + m * 2 + b)], writes=[("E", m, b)])
                        for m in range(2):
                            P.op("pe", lambda e, m=m, b=b, i=i, n=n: e.matmul(
                                PS[m][:, 0:n], lhsT=Vh[:, i, :], rhs=E[m][b][:, 0:n],
                                start=(i == 0), stop=(i == nkt - 1)),
                                reads=["Vh", ("E", m, b)], writes=[("ps", m)])
                            P.op("pe", lambda e, m=m, b=b, i=i, n=n: e.matmul(
                                PS[2 + m][:, 0:n], lhsT=ones, rhs=E[m][b][:, 0:n],
                                start=(i == 0), stop=(i == nkt - 1)),
                                reads=["CB", ("E", m, b)], writes=[("ps", 2 + m)])
                    P.op("act", lambda e, n=n: e.activation(out=r0[:, 0:n], in_=PS[2][:, 0:n], func=AF.Ln),
                         reads=[("ps", 2)], writes=["r0"])
                    P.op("act", lambda e, n=n: e.activation(out=r1[:, 0:n], in_=PS[3][:, 0:n], func=AF.Ln),
                         reads=[("ps", 3)], writes=["r1"])
                    P.op("act", lambda e, n=n: e.activation(out=r0[:, 0:n], in_=r0[:, 0:n], func=AF.Exp, scale=-1.0),
                         reads=["r0"], writes=["r0"])
                    P.op("act", lambda e, n=n: e.activation(out=r1[:, 0:n], in_=r1[:, 0:n], func=AF.Exp, scale=-1.0),
                         reads=["r1"], writes=["r1"])
                    P.op("dve", lambda e, n=n: e.tensor_tensor(out=t0b[:, 0:n], in0=PS[0][:, 0:n], in1=r0[:, 0:n],
                                                               op=ALU.mult),
                         reads=[("ps", 0), "r0"], writes=["t0b"])
                    P.op("dve", lambda e, n=n: e.tensor_tensor(out=t1b[:, 0:n], in0=PS[1][:, 0:n], in1=r1[:, 0:n],
                                                               op=ALU.mult),
                         reads=[("ps", 1), "r1"], writes=["t1b"])
                    P.op("dve", lambda e, n=n: e.scalar_tensor_tensor(
                        out=t0b[:, 0:n], in0=t1b[:, 0:n], scalar=neglam, in1=t0b[:, 0:n],
                        op0=ALU.mult, op1=ALU.add), reads=["t1b", "t0b", "neglam"], writes=["t0b"])
                    P.op("act", lambda e, n=n: e.activation(out=sqh[:, 0:n], in_=t0b[:, 0:n], func=AF.Square),
                         reads=["t0b"], writes=["sqh"])
                    P.op("pe", lambda e, n=n: e.matmul(PS[2][:, 0:n], lhsT=ones, rhs=sqh[:, 0:n], start=True, stop=True),
                         reads=["CB", "sqh"], writes=[("ps", 2)])
                    P.op("act", lambda e, n=n: e.activation(out=rsh[:, 0:n], in_=PS[2][:, 0:n], func=AF.Ln,
                                                            scale=1.0 / 128.0, bias=EPSB),
                         reads=[("ps", 2), "EPSB"], writes=["rsh"])
                    P.op("act", lambda e, n=n: e.activation(out=rrh[:, 0:n], in_=rsh[:, 0:n], func=AF.Exp, scale=-0.5),
                         reads=["rsh"], writes=["rrh"])
                    P.op("dve", lambda e, n=n, q0=q0, h=h: e.scalar_tensor_tensor(
                        out=yh[:, q0:q0 + n], in0=t0b[:, 0:n], scalar=MISC[:, 4 + h:5 + h], in1=rrh[:, 0:n],
                        op0=ALU.mult, op1=ALU.mult), reads=["t0b", "G08", "rrh"], writes=["yh"])
                s_w = wload(wout0_d[h], D)
                resid_add(layer, lambda k, b0, bn: yh[:, b0:b0 + bn], lambda k: ["yh"], s_w,
                          lambda k, m, s_w=s_w: WS[s_w][:, m * 128:(m + 1) * 128], 1, 2)

        HS = sb("HG_S", [128, 2 * 128], F32)
        HSm = sb("HG_Sm", [128, 2 * 128], BF16)
        ATm = sb("HG_ATm", [128, 2 * 64], BF16)
        HSC = sb("HG_SC", [128, 6 * 36 + 16 + 32], F32)
        LATBLK = [(b0, bn, 0) for (b0, bn) in tok_blocks(NCTX, NT)]

        def mixer1():
            layer = 1
            hT = compute_hT(layer)
            LB = HSC[:, 232:248]
            OML = HSC[:, 248:264]
            lbl = CONST[:, CS("lblog")].rearrange("p (d l h) -> p d l h", d=2, l=2)
            LB3 = LB.rearrange("p (d h) -> p d h", d=2)
            P.op("dve", lambda e: e.tensor_tensor(out=LB3, in0=lbl[:, :, 1, :], in1=lbl[:, :, 0, :], op=ALU.subtract),
                 reads=["CONST"], writes=["LB"])
            P.op("act", lambda e: e.activation(out=LB, in_=LB, func=AF.Sigmoid), reads=["LB"], writes=["LB"])
            P.op("dve", lambda e: e.tensor_scalar(out=OML, in0=LB, scalar1=-1.0, scalar2=1.0, op0=ALU.mult,
                                                  op1=ALU.add), reads=["LB"], writes=["OML"])

            Qd = [rv("SCR", d * 4096, NLAT, BF16, key=("Qd", d)) for d in range(2)]
            Kd = [rv("SCR", 8192 + d * 4608, NT, BF16, key=("Kd", d)) for d in range(2)]
            Ktok = [rv("SCR", 17408 + d * 4608, 18 * 128, BF16, key=("Ktok", d)).rearrange("p (i c) -> p i c", i=18)
                    for d in range(2)]
            Vh = rv("SCR", 26624, 18 * 128, BF16, key="Vh1").rearrange("p (i d) -> p i d", i=18)
            TO = 31232
            OF = rv("ROPE", 0, NLAT, F32, keys=[(("OF", b), b * 512, 512) for b in range(4)])
            SG = rv("ROPE", 8192, NLAT, BF16, key="SG")
            YH = rv("ROPE", 12288, NLAT, BF16, key="YH")
            AL = [HSC[:, d * 108 + 0: d * 108 + 36] for d in range(2)]
            BE = [HSC[:, d * 108 + 36: d * 108 + 72] for d in range(2)]
            GA = [HSC[:, d * 108 + 72: d * 108 + 108] for d in range(2)]
            RP = HSC[:, 216:224]
            PIECES = tok_blocks(0, NT)
            for h in range(8):
                Qp = rv("SCR", TO, 512, F32, key="Qp")
                Fb = [rv("SCR", TO + 2048 + d * 2048, 512, F32, key=("Fb", d)) for d in range(2)]
                Kk = rv("SCR", TO + 6144, 512, F32, key="Kk")
                Bc = rv("SCR", TO + 8192, 512, F32, key="Bc")
                Gl = rv("SCR", TO + 10240, 512, F32, key="Gl")
                s = wload(whg_d[h], KC * 640)
                W = lambda kc, c0: WS[s][:, kc * 640 + c0: kc * 640 + c0 + 128]

                def proj(c0, p0, n, pb):
                    for kc in range(KC):
                        P.op("pe", lambda e, kc=kc: e.matmul(
                            PS[pb][:, 0:n], lhsT=W(kc, c0), rhs=hT[:, kc, p0:p0 + n],
                            start=(kc == 0), stop=(kc == KC - 1)),
                            reads=[("WS", s), ("hT", kc)], writes=[("ps", pb)])

                for i4 in range(0, 18, 4):
                    nt_ = min(4, 18 - i4)
                    pb = nextps()
                    for j in range(nt_):
                        i = i4 + j
                        for kc in range(KC):
                            P.op("pe", lambda e, kc=kc, pb=pb, i=i, j=j: e.matmul(
                                PS[pb][:, j * 128:(j + 1) * 128], lhsT=hT[:, kc, i * 128:(i + 1) * 128],
                                rhs=W(kc, 128), start=(kc == 0), stop=(kc == KC - 1)),
                                reads=[("WS", s), ("hT", kc)], writes=[("ps", pb)])
                    P.op("act", lambda e, pb=pb, i4=i4, nt_=nt_: e.activation(
                        out=Vh[:, i4:i4 + nt_, :], in_=PS[pb][:, 0:nt_ * 128].rearrange("p (i d) -> p i d", i=nt_),
                        func=AF.Identity), reads=[("ps", pb)], writes=["Vh1"])
                for (b0, bn, _) in LATBLK:
                    pb = nextps()
                    proj(512, b0, bn, pb)
                    P.op("act", lambda e, pb=pb, b0=b0, bn=bn: e.activation(
                        out=SG[:, b0 - NCTX:b0 - NCTX + bn], in_=PS[pb][:, 0:bn], func=AF.Silu),
                        reads=[("ps", pb)], writes=["SG"])
                for (p0, n) in PIECES:
                    nch = n // 64
                    c0 = p0 // 64
                    l0 = max(p0, NCTX)
                    ln = p0 + n - l0
                    lo = l0 - p0
                    v3 = lambda t, n=n: t[:, 0:n].rearrange("p (c t) -> p c t", t=64)
                    sc = lambda t, nch=nch, c0=c0: t[:, c0:c0 + nch].rearrange("p (c o) -> p c o", o=1)
                    pb = nextps()
                    proj(0, p0, n, pb)
                    P.op("act", lambda e, pb=pb, n=n: e.activation(out=Qp[:, 0:n], in_=PS[pb][:, 0:n], func=AF.Silu),
                         reads=[("ps", pb)], writes=["Qp"])
                    for d in range(2):
                        pb = nextps()
                        proj(256 + d * 128, p0, n, pb)
                        P.op("act", lambda e, pb=pb, n=n, d=d: e.activation(out=Fb[d][:, 0:n], in_=PS[pb][:, 0:n],
                                                                            func=AF.Sigmoid),
                             reads=[("ps", pb)], writes=[("Fb", d)])
                    for d in range(2):
                        F = Fb[d]
                        P.op("dve", lambda e, n=n, d=d, h=h, F=F: e.tensor_scalar(
                            out=F[:, 0:n], in0=F[:, 0:n], scalar1=OML[:, d * 8 + h:d * 8 + h + 1],
                            scalar2=LB[:, d * 8 + h:d * 8 + h + 1], op0=ALU.mult, op1=ALU.add),
                            reads=[("Fb", d), "OML", "LB"], writes=[("Fb", d)])
                        P.op("dve", lambda e, n=n, F=F: e.tensor_scalar(
                            out=Kk[:, 0:n], in0=F[:, 0:n], scalar1=-1.0, scalar2=1.0, op0=ALU.mult, op1=ALU.add),
                            reads=[("Fb", d)], writes=["Kk"])
                        P.op("act", lambda e, n=n, F=F: e.activation(out=F[:, 0:n], in_=F[:, 0:n], func=AF.Ln),
                             reads=[("Fb", d)], writes=[("Fb", d)])
                        P.op("dve", lambda e, n=n, F=F: e.tensor_tensor_scan(
                            out=Bc[:, 0:n], data0=CONST[:, CS("m01", 0, n)], data1=F[:, 0:n], initial=0.0,
                            op0=ALU.mult, op1=ALU.add), reads=[("Fb", d), "CONST"], writes=["Bc"])
                        B3 = v3(Bc)
                        G3 = v3(Gl)
                        P.op("dve", lambda e, nch=nch, B3=B3, G3=G3: e.tensor_tensor(
                            out=G3, in0=B3, in1=B3[:, :, 31:32].to_broadcast([128, nch, 64]), op=ALU.subtract),
                            reads=["Bc"], writes=["Gl"])
                        P.op("dve", lambda e, d=d, B3=B3: e.tensor_copy(out=sc(AL[d]), in_=B3[:, :, 63:64]),
                             reads=["Bc"], writes=[("AL", d)])
                        bsrc = G3[:, :, 63:64] if d == 0 else B3[:, :, 31:32]
                        gsrc = B3[:, :, 31:32] if d == 0 else G3[:, :, 63:64]
                        P.op("dve", lambda e, d=d, bsrc=bsrc: e.tensor_copy(out=sc(BE[d]), in_=bsrc),
                             reads=["Bc", "Gl"], writes=[("BE", d)])
                        P.op("dve", lambda e, d=d, gsrc=gsrc: e.tensor_copy(out=sc(GA[d]), in_=gsrc),
                             reads=["Bc", "Gl"], writes=[("GA", d)])
                        if d == 0:
                            P.op("act", lambda e, n=n: e.activation(out=Bc[:, 0:n], in_=Gl[:, 0:n], func=AF.Exp),
                                 reads=["Gl", "Bc"], writes=["Bc"])
                            P.op("act", lambda e, n=n, F=F: e.activation(out=F[:, 0:n], in_=Gl[:, 0:n], func=AF.Exp,
                                                                         scale=-1.0),
                                 reads=["Gl", ("Fb", d)], writes=[("Fb", d)])
                            if ln > 0:
                                P.op("dve", lambda e, lo=lo, ln=ln, l0=l0: e.tensor_tensor(
                                    out=Qd[0][:, l0 - NCTX:l0 - NCTX + ln], in0=Qp[:, lo:lo + ln],
                                    in1=Bc[:, lo:lo + ln], op=ALU.mult), reads=["Qp", "Bc"], writes=[("Qd", 0)])
                            P.op("dve", lambda e, n=n, p0=p0, F=F: e.tensor_tensor(
                                out=Kd[0][:, p0:p0 + n], in0=Kk[:, 0:n], in1=F[:, 0:n], op=ALU.mult),
                                reads=["Kk", ("Fb", d)], writes=[("Kd", 0)])
                        else:
                            F3 = v3(F)
                            P.op("dve", lambda e, F3=F3, G3=G3: e.tensor_copy(out=F3[:, :, 1:64], in_=G3[:, :, 0:63]),
                                 reads=["Gl", ("Fb", d)], writes=[("Fb", d)])
                            P.op("dve", lambda e, F3=F3, B3=B3: e.tensor_scalar(
                                out=F3[:, :, 0:1], in0=B3[:, :, 31:32], scalar1=-1.0, scalar2=None, op0=ALU.mult),
                                reads=["Bc", ("Fb", d)], writes=[("Fb", d)])
                            P.op("act", lambda e, n=n, F=F: e.activation(out=Bc[:, 0:n], in_=F[:, 0:n], func=AF.Exp),
                                 reads=[("Fb", d), "Bc"], writes=["Bc"])
                            P.op("dve", lambda e, n=n, p0=p0: e.tensor_tensor(
                                out=Kd[1][:, p0:p0 + n], in0=Kk[:, 0:n], in1=Bc[:, 0:n], op=ALU.mult),
                                reads=["Kk", "Bc"], writes=[("Kd", 1)])
                            if ln > 0:
                                P.op("act", lambda e, lo=lo, ln=ln, F=F: e.activation(
                                    out=Gl[:, lo:lo + ln], in_=F[:, lo:lo + ln], func=AF.Exp, scale=-1.0),
                                    reads=[("Fb", d), "Gl"], writes=["Gl"])
                                P.op("dve", lambda e, lo=lo, ln=ln, l0=l0: e.tensor_tensor(
                                    out=Qd[1][:, l0 - NCTX:l0 - NCTX + ln], in0=Qp[:, lo:lo + ln],
                                    in1=Gl[:, lo:lo + ln], op=ALU.mult), reads=["Qp", "Gl"], writes=[("Qd", 1)])
                for d in range(2):
                    P.op("act", lambda e, d=d: e.activation(out=HSC[:, d * 108:(d + 1) * 108],
                                                            in_=HSC[:, d * 108:(d + 1) * 108], func=AF.Exp),
                         reads=[("AL", d), ("BE", d), ("GA", d)], writes=[("AL", d), ("BE", d), ("GA", d)])
                for d in range(2):
                    for i4 in range(0, 18, 4):
                        nt_ = min(4, 18 - i4)
                        pb = nextps()
                        psb = PS[pb][:, 0:256].bitcast(BF16)
                        for j in range(nt_):
                            i = i4 + j
                            P.op("pe", lambda e, d=d, i=i, j=j, psb=psb: e.transpose(
                                psb[:, j * 128:(j + 1) * 128], Kd[d][:, i * 128:(i + 1) * 128], ident),
                                reads=[("Kd", d), "CB"], writes=[("ps", pb)])
                        P.op("act", lambda e, d=d, i4=i4, nt_=nt_, psb=psb: e.activation(
                            out=Ktok[d][:, i4:i4 + nt_, :],
                            in_=psb[:, 0:nt_ * 128].rearrange("p (i c) -> p i c", i=nt_), func=AF.Identity),
                            reads=[("ps", pb)], writes=[("Ktok", d)])
                osum = rv("SCR", TO, 512, F32, key="osum")
                rsb = rv("SCR", TO + 2048, 512, F32, key="rsb")
                rrb = rv("SCR", TO + 4096, 512, F32, key="rrb")
                ytmp = rv("SCR", TO + 6144, 512, F32, key="ytmp")
                sqb = rv("SCR", TO + 8192, 512, BF16, key="sqb")
                order = [list(range(36)), [3, 2, 1, 0] + list(range(35, 3, -1))]
                maskc = [CONST[:, CS("maskf")], CONST[:, CS("maskb")]]
                obank = {}
                for step in range(36):
                    for d in range(2):
                        n = order[d][step]
                        i, half = n // 2, n % 2
                        pbase = half * 64
                        t0 = n * 64
                        is_lat = n >= 4
                        first = (step == 0)
                        last = (step == 35)
                        Sd = HS[:, d * 128:(d + 1) * 128]
                        Smd = HSm[:, d * 128:(d + 1) * 128]
                        kvb = 6 + d
                        if not last:
                            P.op("pe", lambda e, d=d, i=i, pbase=pbase, kvb=kvb: e.matmul(
                                PS[kvb][:, 0:128], lhsT=Ktok[d][pbase:pbase + 64, i, :],
                                rhs=Vh[pbase:pbase + 64, i, :], start=True, stop=True),
                                reads=[("Ktok", d), "Vh1"], writes=[("ps", kvb)])
                        if is_lat:
                            lt = t0 - NCTX
                            blk = lt // 512
                            ob = d * 2 + (blk % 2)
                            oc = lt % 512
                            atb = 4 + d
                            P.op("pe", lambda e, d=d, t0=t0, lt=lt, pbase=pbase, atb=atb: e.matmul(
                                PS[atb][pbase:pbase + 64, 0:64], lhsT=Kd[d][:, t0:t0 + 64],
                                rhs=Qd[d][:, lt:lt + 64], start=True, stop=True),
                                reads=[("Kd", d), ("Qd", d)], writes=[("ps", atb)])
                            P.op("dve", lambda e, d=d, pbase=pbase, atb=atb: e.tensor_tensor(
                                out=ATm[pbase:pbase + 64, d * 64:(d + 1) * 64], in0=PS[atb][pbase:pbase + 64, 0:64],
                                in1=maskc[d][pbase:pbase + 64, :], op=ALU.mult),
                                reads=[("ps", atb), "CONST"], writes=[("ATm", d)])
                            P.op("pe", lambda e, d=d, i=i, pbase=pbase, ob=ob, oc=oc: e.matmul(
                                PS[ob][:, oc:oc + 64], lhsT=Vh[pbase:pbase + 64, i, :],
                                rhs=ATm[pbase:pbase + 64, d * 64:(d + 1) * 64], start=True, stop=False),
                                reads=["Vh1", ("ATm", d)], writes=[("ps", ob)])
                            P.op("pe", lambda e, d=d, lt=lt, ob=ob, oc=oc, Smd=Smd: e.matmul(
                                PS[ob][:, oc:oc + 64], lhsT=Smd, rhs=Qd[d][:, lt:lt + 64], start=False, stop=True),
                                reads=[("Sm", d), ("Qd", d)], writes=[("ps", ob)])
                            done = (oc == 448) if d == 0 else (oc == 0)
                            if done:
                                b0 = blk * 512
                                if blk not in obank:
                                    obank[blk] = d
                                    P.op("act", lambda e, ob=ob, b0=b0: e.activation(
                                        out=OF[:, b0:b0 + 512], in_=PS[ob][:, 0:512], func=AF.Identity),
                                        reads=[("ps", ob)], writes=[("OF", blk)])
                                else:
                                    P.op("dve", lambda e, ob=ob, b0=b0: e.tensor_tensor(
                                        out=osum[:, :], in0=PS[ob][:, 0:512], in1=OF[:, b0:b0 + 512], op=ALU.add),
                                        reads=[("ps", ob), ("OF", blk)], writes=["osum"])
                                    P.op("act", lambda e: e.activation(out=sqb[:, :], in_=osum[:, :], func=AF.Square),
                                         reads=["osum"], writes=["sqb"])
                                    P.op("pe", lambda e, ob=ob: e.matmul(PS[ob][:, 0:512], lhsT=ones, rhs=sqb[:, :],
                                                                         start=True, stop=True),
                                         reads=["CB", "sqb"], writes=[("ps", ob)])
                                    P.op("act", lambda e, ob=ob: e.activation(
                                        out=rsb[:, :], in_=PS[ob][:, 0:512], func=AF.Ln, scale=1.0 / 128.0, bias=EPSB),
                                        reads=[("ps", ob), "EPSB"], writes=["rsb"])
                                    P.op("act", lambda e: e.activation(out=rrb[:, :], in_=rsb[:, :], func=AF.Exp,
                                                                       scale=-0.5),
                                         reads=["rsb"], writes=["rrb"])
                                    P.op("dve", lambda e, h=h: e.scalar_tensor_tensor(
                                        out=ytmp[:, :], in0=osum[:, :], scalar=CONST[:, CS("hgnorm", h, 1)],
                                        in1=rrb[:, :], op0=ALU.mult, op1=ALU.mult),
                                        reads=["osum", "CONST", "rrb"], writes=["ytmp"])
                                    P.op("dve", lambda e, b0=b0: e.tensor_tensor(
                                        out=YH[:, b0:b0 + 512], in0=ytmp[:, :], in1=SG[:, b0:b0 + 512], op=ALU.mult),
                                        reads=["ytmp", "SG"], writes=["YH"])
                        if last:
                            continue
                        if first:
                            P.op("dve", lambda e, d=d, n=n, kvb=kvb, Sd=Sd: e.tensor_scalar(
                                out=Sd, in0=PS[kvb][:, 0:128], scalar1=BE[d][:, n:n + 1], scalar2=None, op0=ALU.mult),
                                reads=[("ps", kvb), ("BE", d)], writes=[("S", d)])
                        else:
                            P.op("dve", lambda e, d=d, n=n, Sd=Sd: e.tensor_scalar(
                                out=Sd, in0=Sd, scalar1=AL[d][:, n:n + 1], scalar2=None, op0=ALU.mult),
                                reads=[("S", d), ("AL", d)], writes=[("S", d)])
                            P.op("dve", lambda e, d=d, n=n, kvb=kvb, Sd=Sd: e.scalar_tensor_tensor(
                                out=Sd, in0=PS[kvb][:, 0:128], scalar=BE[d][:, n:n + 1], in1=Sd,
                                op0=ALU.mult, op1=ALU.add),
                                reads=[("ps", kvb), ("BE", d), ("S", d)], writes=[("S", d)])
                        nn = order[d][step + 1]
                        if nn >= 4:
                            P.op("act", lambda e, d=d, nn=nn, Sd=Sd, Smd=Smd: e.activation(
                                out=Smd, in_=Sd, func=AF.Identity, scale=GA[d][:, nn:nn + 1]),
                                reads=[("S", d), ("GA", d)], writes=[("Sm", d)])
                s_w = wload(wout1_d[h], D)
                resid_add(layer, lambda k, b0, bn: YH[:, b0 - NCTX:b0 - NCTX + bn], lambda k: ["YH"], s_w,
                          lambda k, m, s_w=s_w: WS[s_w][:, m * 128:(m + 1) * 128], 1, 2, blocks=LATBLK)

        stages = stop.split("+")

        if "mix0" in stages:
            mixer0()
        if "ffn0" in stages:
            ffn(0, [(0, NCTX, 0, NCTX, 1),
                    (NCTX, NT, NCTX, 1024, 0),
                    (NCTX, NT, NCTX + 1024, 1024, 0)])
        if "mix1" in stages:
            mixer1()
        if "ffn1" in stages:
            ffn(1, [(NCTX, NT, NCTX, 1024, 0),
                    (NCTX, NT, NCTX + 1024, 1024, 0)])

        if "rawout" in stages:
            for c in range(KC):
                P.out_tokens.append(P.dma("sp", out_d[:, c, :], xT[:, c, NCTX:NT], reads=xk(c, NCTX, NLAT)))
        else:
            OUTB = [rv("SCR", i * 2048, 512, F32, key=("outb", i)) for i in range(2)]
            T = nrm_temps("SCR", 46080 - 10240, "S")
            it = 0
            for (b0, bn) in tok_blocks(NCTX, NT):
                rms_rstd(T, lambda c, b0=b0, bn=bn: xT[:, c, b0:b0 + bn],
                         lambda c, b0=b0, bn=bn: xk(c, b0, bn), KC, bn, float(D), 1)
                for c in range(KC):
                    b = it % 2
                    it += 1
                    P.op("dve", lambda e, c=c, b=b, b0=b0, bn=bn: e.scalar_tensor_tensor(
                        out=OUTB[b][:, :bn], in0=xT[:, c, b0:b0 + bn], scalar=CONST[:, CS("fnorm", c, 1)],
                        in1=T["rstd"][:, :bn], op0=ALU.mult, op1=ALU.mult),
                        reads=xk(c, b0, bn) + ["CONST", ("rstd", "S")], writes=[("outb", b)])
                    P.out_tokens.append(P.dma("sp", out_d[:, c, b0 - NCTX:b0 - NCTX + bn], OUTB[b][:, :bn],
                                              reads=[("outb", b)]))
        P.finish("sp", P.out_tokens)
    return nc


def _fm(v):
    v = np.asarray(v, np.float32)
    n = v.shape[-1] // 128
    return np.moveaxis(v.reshape(v.shape[:-1] + (n, 128)), -1, 0)


def _wk(w):
    K, N = w.shape
    return np.ascontiguousarray(w.reshape(K // 128, 128, N).transpose(1, 0, 2).reshape(128, (K // 128) * N))


def _rope_tables():
    rows_n = NLAT // GRID_W
    row = np.repeat(np.arange(rows_n, dtype=np.float32), GRID_W)
    col = np.tile(np.arange(GRID_W, dtype=np.float32), rows_n)
    n_freq = 16
    inv = (np.float32(10000.0) ** (-np.arange(n_freq, dtype=np.float32) / n_freq)).astype(np.float32)
    ang = np.stack([row[:, None] * inv, col[:, None] * inv], axis=1).astype(np.float32)
    cos, sin = np.cos(ang).astype(np.float32), np.sin(ang).astype(np.float32)
    CT = np.zeros((128, NLAT), np.float32)
    ST = np.zeros((128, NLAT), np.float32)
    for p in range(128):
        q = p % 64
        axis, half, f = q // 32, (q % 32) // 16, q % 16
        CT[p] = cos[:, axis, f]
        ST[p] = sin[:, axis, f] * (-1.0 if half == 0 else 1.0)
    return np.concatenate([CT, ST], axis=1)


def _swap_cols(w):
    N = w.shape[1]
    idx = np.arange(N)
    blk = idx // 32
    r = idx % 32
    return w[:, blk * 32 + (r + 16) % 32]


def prepare_inputs(inp):
    f32 = np.float32
    g = {k: np.asarray(v, f32) for k, v in inp.items()}
    shared = {}
    wm = g["w_mod"]
    shared["wmod"] = np.stack([np.stack([_wk(wm[l][:, p * 512:(p + 1) * 512]) for p in range(12)]) for l in range(2)])
    shared["rope"] = _rope_tables()
    wi = g["ev_w_in"][0]
    qw, kw, vw, uw = wi[:, 0:512], wi[:, 512:1024], wi[:, 1024:1536], wi[:, 1536:2048]
    qs, ks = _swap_cols(qw), _swap_cols(kw)
    heads = []
    for h in range(4):
        sl = slice(h * 128, (h + 1) * 128)
        heads.append(_wk(np.concatenate([qw[:, sl], qs[:, sl], kw[:, sl], ks[:, sl], vw[:, sl]], axis=1)))
    shared["whead"] = np.stack(heads)
    shared["wpool"] = _wk(uw)
    shared["wout0"] = np.ascontiguousarray(g["ev_w_out"][0].reshape(KC, 128, D))
    wu = g["ffn_w_up"]
    shared["wffn"] = np.stack([np.stack([
        _wk(np.concatenate([wu[l][:, c * 128:(c + 1) * 128], wu[l][:, DFF + c * 128: DFF + (c + 1) * 128]], axis=1))
        for c in range(NPAIR)]) for l in range(2)])
    wd = g["ffn_w_down"]
    shared["wdn"] = np.stack([np.stack([
        np.ascontiguousarray(wd[l][:, m * 128:(m + 1) * 128].reshape(NPAIR, 128, 128).transpose(1, 0, 2)
                             .reshape(128, NPAIR * 128)) for m in range(KC)]) for l in range(2)])
    hw = g["hg_w_in"][0]
    hh = []
    for h in range(8):
        cols = [hw[:, part * 1024 + h * 128: part * 1024 + (h + 1) * 128] for part in range(5)]
        hh.append(_wk(np.concatenate(cols, axis=1)))
    shared["whg"] = np.stack(hh)
    shared["wout1"] = np.ascontiguousarray(g["hg_w_out"][0].reshape(KC, 128, D))
    cb = np.zeros((128, 256 + 512), f32)
    cb[:, 0:128] = np.eye(128, dtype=f32)
    cb[:, 128:256] = 1.0
    cb[:, 256:768] = g["pool_w"][0].transpose(1, 0, 2).reshape(128, 512)
    shared["constb"] = cb

    const = np.zeros((128, NCONST), f32)
    bm = g["b_mod"]
    bmT = np.stack([_fm(bm[l]) for l in range(2)], axis=1)
    const[:, CS("bmod")] = np.repeat(bmT.reshape(128, 96), 2, axis=1).reshape(128, 192)
    lq = np.concatenate([g["da_lq1"][0], g["da_lk1"][0], g["da_lq2"][0], g["da_lk2"][0]])
    const[:, CS("lqk")] = lq[None, :]
    const[:, CS("subln")] = _fm(g["da_subln"][0])
    const[:, CS("pscale")] = _fm(g["pool_scale"][0])
    const[:, CS("hgnorm")] = _fm(g["hg_norm"][0])
    const[:, CS("fnorm")] = _fm(g["final_norm"])
    cwT = _fm(g["ffn_conv_w"])
    const[:, CS("convw")] = cwT.transpose(0, 1, 3, 2).reshape(128, -1)
    const[:, CS("convb")] = _fm(g["ffn_conv_b"]).reshape(128, -1)
    const[:, CS("lblog")] = _fm(g["hg_lb_logits"]).reshape(128, -1)
    pe = np.zeros((4, 4, 8), f32)
    for gi, w in enumerate((2, 4, 8, 16)):
        hw_ = w // 2
        for ei, T in enumerate((NCTX, NCTX, NLAT, NLAT)):
            for i in range(hw_):
                t = i if ei % 2 == 0 else T - hw_ + i
                cnt = min(t + w - hw_, T) - max(t - hw_, 0)
                pe[gi, ei, i] = 1.0 / cnt
    const[:, CS("pedge")] = pe.reshape(1, -1)
    s_idx = np.arange(128)[:, None] % 64
    t_idx = np.arange(64)[None, :]
    const[:, CS("maskf")] = (t_idx >= s_idx).astype(f32)
    const[:, CS("maskb")] = (t_idx <= s_idx).astype(f32)
    const[:, CS("m01")] = (np.arange(512) % 64 != 0).astype(f32)[None, :]

    cctx = _fm(g["c_ctx"])
    in_maps = []
    for b in range(NCORES):
        cst = const.copy()
        cst[:, CS("cT", 0, 8)] = _fm(g["c"][b])
        cst[:, CS("cT", 8, 8)] = cctx
        tok = np.concatenate([g["ctx"][b], g["x"][b]], axis=0)
        xTh = np.ascontiguousarray(tok.T.reshape(KC, 128, NT).transpose(1, 0, 2))
        m = dict(shared)
        m["const"] = cst
        m["xT"] = xTh
        in_maps.append(m)
    return in_maps


_CACHE = {}


def run(inputs, stop="full", trace=False, cores=None):
    in_maps = prepare_inputs(inputs)
    if cores is not None:
        in_maps = [in_maps[b] for b in cores]
    if stop not in _CACHE:
        _CACHE[stop] = build_program(stop)
    nc = _CACHE[stop]
    res = run_bass_kernel_spmd(nc, in_maps, core_ids=list(range(len(in_maps))), trace=trace)
    outs = []
    for r in res.results:
        o = np.asarray(r["outT"], np.float32)
        outs.append(o.transpose(2, 1, 0).reshape(NLAT, D))
    return np.stack(outs), res


def kernel(**inputs):
    out, _ = run(inputs, stop="mix0+ffn0+mix1+ffn1")
    return out
```

```python
import math
from contextlib import ExitStack

import numpy as np
import concourse.bass as bass
import concourse.mybir as mybir
from concourse.bass_utils import run_bass_kernel_spmd

F32 = mybir.dt.float32
BF16 = mybir.dt.bfloat16
AF = mybir.ActivationFunctionType
ALU = mybir.AluOpType

NCORES = 8
D = 1024
KC = 8
NCTX = 256
NLAT = 2048
NT = NCTX + NLAT
DFF = 2816
NPAIR = DFF // 128
EPS = 1e-6
GRID_W = 64
SLOT_ELEMS = 5120
NSLOT = 2

_c = {}
_off = 0


def _cdef(name, n):
    global _off
    _c[name] = (_off, n)
    _off += n


_cdef("cT", 16)
_cdef("bmod", 2 * 96)
_cdef("lqk", 4 * 64)
_cdef("subln", 4)
_cdef("pscale", 4)
_cdef("hgnorm", 8)
_cdef("fnorm", 8)
_cdef("convw", 2 * 44 * 3)
_cdef("convb", 2 * 44)
_cdef("lblog", 2 * 2 * 8)
_cdef("pedge", 4 * 4 * 8)
_cdef("maskf", 64)
_cdef("maskb", 64)
_cdef("m01", 512)
NCONST = _off


def CS(name, a=0, n=None):
    o, sz = _c[name]
    if n is None:
        n = sz - a
    return slice(o + a, o + a + n)


class Prog:
    ROT = 7000

    def __init__(self, nc, es):
        self.nc = nc
        self.es = es
        self.eng = {"pe": nc.tensor, "act": nc.scalar, "dve": nc.vector,
                    "pool": nc.gpsimd, "sp": nc.sync}
        self.sem = {}
        self.cnt = {}
        self.nsem = 0
        for e in self.eng:
            self.sem[e] = self._newsem(e)
            self.cnt[e] = 0
        self.known = {e: {} for e in self.eng}
        self.res = {}
        self.dq = {}
        for q in ("sp", "pool", "act"):
            self.dq[q] = {"sems": [[self._newsem("d" + q), 0] for _ in range(8)], "rr": 0}
        self.out_tokens = []
        self.active = {}
        self.retired = set()

    def _newsem(self, tag):
        self.nsem += 1
        return self.es.enter_context(self.nc.semaphore(f"s_{tag}_{self.nsem}"))

    def _need(self, e, tok):
        sem, val, owner = tok
        if owner == "pe" and e == "pe":
            return
        k = self.known[e]
        if k.get(sem.name, 0) >= val:
            return
        self.eng[e].wait_ge(sem, val)
        k[sem.name] = val

    def _deps(self, e, reads, writes):
        for key in list(reads) + list(writes):
            if key in self.retired:
                raise RuntimeError(f"use of retired key {key}")
        for key in reads:
            r = self.res.get(key)
            if r and r[0] is not None:
                self._need(e, r[0])
        for key in writes:
            r = self.res.get(key)
            if r:
                if r[0] is not None:
                    self._need(e, r[0])
                for t in r[1]:
                    self._need(e, t)

    def _commit(self, tok, reads, writes):
        for key in reads:
            r = self.res.setdefault(key, [None, []])
            r[1] = [t for t in r[1] if t[0].name != tok[0].name] + [tok]
        for key in writes:
            self.res[key] = [tok, []]

    def op(self, e, fn, reads=(), writes=()):
        self._deps(e, reads, writes)
        if self.cnt[e] >= self.ROT:
            self.sem[e] = self._newsem(e)
            self.cnt[e] = 0
        inst = fn(self.eng[e])
        self.cnt[e] += 1
        inst.then_inc(self.sem[e], 1)
        tok = (self.sem[e], self.cnt[e], e)
        self._commit(tok, reads, writes)
        return tok

    def dma(self, q, out, in_, reads=(), writes=()):
        self._deps(q, reads, writes)
        dq = self.dq[q]
        slot = dq["sems"][dq["rr"] % len(dq["sems"])]
        dq["rr"] += 1
        if slot[1] > 0:
            self._need(q, (slot[0], slot[1], "dma"))
        inst = self.eng[q].dma_start(out=out, in_=in_)
        slot[1] += 16
        inst.then_inc(slot[0], 16)
        tok = (slot[0], slot[1], "dma")
        self._commit(tok, reads, writes)
        return tok

    def alias(self, new_keys, old_keys):
        toks = []
        for k in old_keys:
            r = self.res.get(k)
            if r:
                if r[0] is not None:
                    toks.append(r[0])
                toks.extend(r[1])
        for nk in new_keys:
            r = self.res.setdefault(nk, [None, []])
            r[1] = r[1] + toks

    def claim(self, region, lo, hi, key):
        act = self.active.setdefault(region, {})
        if key in act and act[key] == (lo, hi):
            return
        for k, (l, h) in list(act.items()):
            if l < hi and lo < h:
                self.alias([key], [k])
                del act[k]
                self.retired.add(k)
        act[key] = (lo, hi)
        self.retired.discard(key)

    def finish(self, e, toks):
        for t in toks:
            self._need(e, t)


def tok_blocks(lo, hi, step=512):
    out = []
    t = lo
    while t < hi:
        n = min(step, hi - t)
        out.append((t, n))
        t += n
    return out


def build_program(stop="full", debug=False):
    nc = bass.Bass("TRN2", target_bir_lowering=False)

    def din(name, shape, dt=F32):
        return nc.dram_tensor(name, list(shape), dt, kind="ExternalInput").ap()

    xT_d = din("xT", [128, KC, NT])
    const_d = din("const", [128, NCONST])
    constb_d = din("constb", [128, 256 + 512])
    wmod_d = din("wmod", [2, 12, 128, KC * 512])
    rope_d = din("rope", [128, 2 * NLAT])
    whead_d = din("whead", [4, 128, KC * 640])
    wpool_d = din("wpool", [128, KC * 512])
    wout0_d = din("wout0", [KC, 128, D])
    wffn_d = din("wffn", [2, NPAIR, 128, KC * 256])
    wdn_d = din("wdn", [2, KC, 128, NPAIR * 128])
    whg_d = din("whg", [8, 128, KC * 640])
    wout1_d = din("wout1", [KC, 128, D])
    out_d = nc.dram_tensor("outT", [128, KC, NLAT], F32, kind="ExternalOutput").ap()

    with ExitStack() as es:
        P = Prog(nc, es)

        def sb(name, shape, dt):
            return es.enter_context(nc.sbuf_tensor(name, list(shape), dt))

        xT = sb("xT_sb", [128, KC, NT], F32)
        REG = {"HT": sb("HT", [128, 18432], BF16),
               "SCR": sb("SCR", [128, 23040], BF16),
               "ROPE": sb("ROPE", [128, 8192], BF16)}
        WS = [sb(f"WS{i}", [128, SLOT_ELEMS], BF16) for i in range(NSLOT)]
        CONST = sb("CONST", [128, NCONST], F32)
        CB = sb("CB", [128, 256 + 512], BF16)
        MOD = sb("MOD", [128, 2 * 96], F32)
        MOD1 = sb("MOD1", [128, 2 * 96], F32)
        SC = sb("SC", [128, KC * 2], BF16)
        MISC = sb("MISC", [128, 160], F32)
        PS = [es.enter_context(nc.psum_tensor(f"ps{i}", [128, 512], F32)) for i in range(8)]

        ident = CB[:, 0:128]
        ones = CB[:, 128:256]

        def rv(region, off, nelem, dt, key=None, keys=None):
            esz = 2 if dt == BF16 else 4
            assert off % 4 == 0
            v = REG[region][:, off // 2: off // 2 + nelem * esz // 2]
            if dt != BF16:
                v = v.bitcast(dt)
            if key is not None:
                P.claim(region, off, off + nelem * esz, key)
            if keys is not None:
                for (k, so, n) in keys:
                    P.claim(region, off + so * esz, off + (so + n) * esz, k)
            return v

        def xk(c, t0, n):
            return [("xT", c, b) for b in range(t0 // 256, (t0 + n + 255) // 256)]

        P.dma("sp", CONST[:], const_d, writes=["CONST"])
        P.dma("pool", CB[:], constb_d, writes=["CB"])
        for c in range(KC):
            P.dma("sp", xT[:, c, :], xT_d[:, c, :], writes=xk(c, 0, NT))
        P.op("dve", lambda e: e.memset(MISC[:, 0:1], EPS), writes=["EPSB"])
        EPSB = MISC[:, 0:1]

        state = {"slot": 0}

        def wload(src_ap, nelem):
            s = state["slot"] % NSLOT
            state["slot"] += 1
            P.dma("pool", WS[s][:, 0:nelem], src_ap, writes=[("WS", s)])
            return s

        SC3 = SC[:, :].rearrange("p (k two) -> p k two", two=2)
        P.op("act", lambda e: e.activation(out=SC3[:, :, 0], in_=CONST[:, CS("cT", 0, 8)], func=AF.Silu),
             reads=["CONST"], writes=["SC0"])
        P.op("act", lambda e: e.activation(out=SC3[:, :, 1], in_=CONST[:, CS("cT", 8, 8)], func=AF.Silu),
             reads=["CONST"], writes=["SC1"])
        for layer in range(2):
            for piece in range(12):
                s = wload(wmod_d[layer, piece], KC * 512)
                for sub in range(4):
                    jc = piece * 4 + sub
                    col = layer * 96 + jc * 2
                    for kc in range(KC):
                        P.op("pe", lambda e, s=s, kc=kc, sub=sub, col=col: e.matmul(
                            PS[0][:, col:col + 2],
                            lhsT=WS[s][:, kc * 512 + sub * 128: kc * 512 + sub * 128 + 128],
                            rhs=SC[:, kc * 2: kc * 2 + 2],
                            start=(kc == 0), stop=(kc == KC - 1)),
                            reads=[("WS", s), "SC0", "SC1"], writes=[("ps", 0)])
        P.op("dve", lambda e: e.tensor_tensor(out=MOD[:, :], in0=PS[0][:, 0:192], in1=CONST[:, CS("bmod")],
                                               op=ALU.add),
             reads=[("ps", 0), "CONST"], writes=["MOD"])
        P.op("dve", lambda e: e.tensor_scalar_add(out=MOD1[:, :], in0=MOD[:, :], scalar1=1.0),
             reads=["MOD"], writes=["MOD1"])

        def modv(layer, j, chunk, which, plus1=False):
            t = MOD1 if plus1 else MOD
            col = layer * 96 + (j * 8 + chunk) * 2 + which
            return t[:, col:col + 1]

        def nrm_temps(region, off, tag):
            t = {"tag": tag}
            t["sq"] = [rv(region, off + i * 1024, 512, BF16, key=("sq", tag, i)) for i in range(2)]
            t["rs"] = rv(region, off + 2048, 512, F32, key=("rs", tag))
            t["rstd"] = rv(region, off + 4096, 512, F32, key=("rstd", tag))
            t["tmp"] = [rv(region, off + 6144 + i * 2048, 512, F32, key=("ntmp", tag, i)) for i in range(2)]
            return t

        def rms_rstd(T, src_fn, rkeys, nchunk, n, denom, psb):
            tag = T["tag"]
            for c in range(nchunk):
                b = c % 2
                P.op("act", lambda e, c=c, b=b: e.activation(out=T["sq"][b][:, :n], in_=src_fn(c), func=AF.Square),
                     reads=rkeys(c), writes=[("sq", tag, b)])
                P.op("pe", lambda e, c=c, b=b: e.matmul(PS[psb][:, :n], lhsT=ones, rhs=T["sq"][b][:, :n],
                                                        start=(c == 0), stop=(c == nchunk - 1)),
                     reads=[("sq", tag, b), "CB"], writes=[("ps", psb)])
            P.op("act", lambda e: e.activation(out=T["rs"][:, :n], in_=PS[psb][:, :n], func=AF.Ln,
                                               scale=1.0 / denom, bias=EPSB),
                 reads=[("ps", psb), "EPSB"], writes=[("rs", tag)])
            P.op("act", lambda e: e.activation(out=T["rstd"][:, :n], in_=T["rs"][:, :n], func=AF.Exp, scale=-0.5),
                 reads=[("rs", tag)], writes=[("rstd", tag)])

        def norm_mod(T, layer, jshift, jscale, t0, n, which, out_fn, out_keys, psb=1):
            tag = T["tag"]
            rms_rstd(T, lambda c: xT[:, c, t0:t0 + n], lambda c: xk(c, t0, n), KC, n, float(D), psb)
            for c in range(KC):
                b = c % 2
                P.op("dve", lambda e, c=c, b=b: e.scalar_tensor_tensor(
                    out=T["tmp"][b][:, :n], in0=xT[:, c, t0:t0 + n], scalar=modv(layer, jscale, c, which, True),
                    in1=T["rstd"][:, :n], op0=ALU.mult, op1=ALU.mult),
                    reads=xk(c, t0, n) + ["MOD1", ("rstd", tag)], writes=[("ntmp", tag, b)])
                P.op("act", lambda e, c=c, b=b: e.activation(
                    out=out_fn(c), in_=T["tmp"][b][:, :n], func=AF.Identity,
                    bias=modv(layer, jshift, c, which), scale=1.0),
                    reads=[("ntmp", tag, b), "MOD"], writes=out_keys(c))

        def ffn(layer, groups):
            W2 = 1026
            cw = lambda c, tap: CONST[:, CS("convw", (layer * 44 + c) * 3 + tap, 1)]
            cb = lambda c: CONST[:, CS("convb", layer * 44 + c, 1)]
            h2 = rv("HT", 0, KC * W2, BF16,
                    keys=[(("h2", c), c * W2, W2) for c in range(KC)]).rearrange("p (k t) -> p k t", k=KC)
            ctmp = [[rv("HT", KC * W2 * 2 + (i * 2 + j) * 1408, 352, F32, key=("ctmp", i, j))
                     for j in range(2)] for i in range(2)]
            T = nrm_temps("HT", KC * W2 * 2 + 4 * 1408, "H")
            stash = rv("HT", KC * W2 * 2 + 4 * 1408 + 10240, KC * 8, BF16,
                       keys=[(("stash", c), c * 8, 8) for c in range(KC)]).rearrange("p (k t) -> p k t", k=KC)
            gbuf = rv("SCR", 0, NPAIR * 1024, BF16,
                      keys=[(("g", c), c * 1024, 1024) for c in range(NPAIR)]
                      ).rearrange("p (c t) -> p c t", c=NPAIR)
            h2keys = [("h2", c) for c in range(KC)]
            for gi, (slo, shi, t0, G, which) in enumerate(groups):
                if t0 - 1 >= slo:
                    norm_mod(T, layer, 3, 4, t0 - 2, 2, which,
                             lambda c, gi=gi: stash[:, c, 2 * gi:2 * gi + 2],
                             lambda c: [("stash", c)])
            for gi, (slo, shi, t0, G, which) in enumerate(groups):
                hi = min(t0 + G + 1, shi)
                if t0 - 1 >= slo:
                    for c in range(KC):
                        P.op("dve", lambda e, c=c, gi=gi: e.tensor_copy(out=h2[:, c, 0:1],
                                                                        in_=stash[:, c, 2 * gi + 1:2 * gi + 2]),
                             reads=[("stash", c)], writes=[("h2", c)])
                else:
                    P.op("dve", lambda e: e.memset(h2[:, :, 0:1], 0.0), writes=h2keys)
                if hi < t0 + G + 1:
                    P.op("dve", lambda e, G=G: e.memset(h2[:, :, G + 1:G + 2], 0.0), writes=h2keys)
                for (b0, bn) in tok_blocks(t0, hi):
                    j0 = b0 - (t0 - 1)
                    norm_mod(T, layer, 3, 4, b0, bn, which,
                             lambda c, j0=j0, bn=bn: h2[:, c, j0:j0 + bn],
                             lambda c: [("h2", c)])
                npiece = (G + 341) // 342
                base = G // npiece
                pieces = []
                o = 0
                for i in range(npiece):
                    n = base + (1 if i < G % npiece else 0)
                    pieces.append((o, n))
                    o += n
                it = 0
                for c in range(NPAIR):
                    s = wload(wffn_d[layer, c], KC * 256)
                    for (o, n) in pieces:
                        bsel = it % 2
                        it += 1
                        pa, pg = 2 + bsel * 2, 3 + bsel * 2
                        for half, pb in ((0, pa), (1, pg)):
                            for kc in range(KC):
                                P.op("pe", lambda e, s=s, kc=kc, half=half, pb=pb, o=o, n=n: e.matmul(
                                    PS[pb][:, 0:n + 2],
                                    lhsT=WS[s][:, kc * 256 + half * 128: kc * 256 + half * 128 + 128],
                                    rhs=h2[:, kc, o:o + n + 2],
                                    start=(kc == 0), stop=(kc == KC - 1)),
                                    reads=[("WS", s), ("h2", kc)], writes=[("ps", pb)])
                        for half, pb in ((0, pa), (1, pg)):
                            cc = c + half * NPAIR
                            tb = ctmp[bsel][half]
                            P.op("act", lambda e, pb=pb, tb=tb, cc=cc, n=n: e.activation(
                                out=tb[:, 0:n], in_=PS[pb][:, 1:n + 1], func=AF.Identity,
                                scale=cw(cc, 1), bias=cb(cc)),
                                reads=[("ps", pb), "CONST"], writes=[("ctmp", bsel, half)])
                            P.op("dve", lambda e, pb=pb, tb=tb, cc=cc, n=n: e.scalar_tensor_tensor(
                                out=tb[:, 0:n], in0=PS[pb][:, 0:n], scalar=cw(cc, 0), in1=tb[:, 0:n],
                                op0=ALU.mult, op1=ALU.add),
                                reads=[("ps", pb), "CONST", ("ctmp", bsel, half)], writes=[("ctmp", bsel, half)])
                            P.op("dve", lambda e, pb=pb, tb=tb, cc=cc, n=n: e.scalar_tensor_tensor(
                                out=tb[:, 0:n], in0=PS[pb][:, 2:n + 2], scalar=cw(cc, 2), in1=tb[:, 0:n],
                                op0=ALU.mult, op1=ALU.add),
                                reads=[("ps", pb), "CONST", ("ctmp", bsel, half)], writes=[("ctmp", bsel, half)])
                        P.op("act", lambda e, bsel=bsel, n=n: e.activation(
                            out=ctmp[bsel][1][:, 0:n], in_=ctmp[bsel][1][:, 0:n], func=AF.Silu),
                            reads=[("ctmp", bsel, 1)], writes=[("ctmp", bsel, 1)])
                        P.op("dve", lambda e, bsel=bsel, n=n, c=c, o=o: e.tensor_tensor(
                            out=gbuf[:, c, o:o + n], in0=ctmp[bsel][0][:, 0:n], in1=ctmp[bsel][1][:, 0:n],
                            op=ALU.mult),
                            reads=[("ctmp", bsel, 0), ("ctmp", bsel, 1)], writes=[("g", c)])
                it = 0
                for m in range(KC):
                    s = wload(wdn_d[layer, m], NPAIR * 128)
                    for (b0, bn) in tok_blocks(0, G):
                        pb = 2 + (it % 4)
                        it += 1
                        for c in range(NPAIR):
                            P.op("pe", lambda e, s=s, c=c, pb=pb, b0=b0, bn=bn: e.matmul(
                                PS[pb][:, 0:bn], lhsT=WS[s][:, c * 128:(c + 1) * 128],
                                rhs=gbuf[:, c, b0:b0 + bn], start=(c == 0), stop=(c == NPAIR - 1)),
                                reads=[("WS", s), ("g", c)], writes=[("ps", pb)])
                        tt = t0 + b0
                        P.op("dve", lambda e, pb=pb, m=m, tt=tt, bn=bn: e.scalar_tensor_tensor(
                            out=xT[:, m, tt:tt + bn], in0=PS[pb][:, 0:bn], scalar=modv(layer, 5, m, which),
                            in1=xT[:, m, tt:tt + bn], op0=ALU.mult, op1=ALU.add),
                            reads=[("ps", pb), "MOD"] + xk(m, tt, bn), writes=xk(m, tt, bn))

        def hT_view():
            return rv("HT", 0, KC * NT, BF16,
                      keys=[(("hT", c), c * NT, NT) for c in range(KC)]).rearrange("p (k t) -> p k t", k=KC)

        ALLBLK = [(0, NCTX, 1)] + [(b0, bn, 0) for (b0, bn) in tok_blocks(NCTX, NT)]

        def compute_hT(layer):
            hT = hT_view()
            T = nrm_temps("SCR", 46080 - 10240, "S")
            for (b0, bn, which) in ALLBLK:
                norm_mod(T, layer, 0, 1, b0, bn, which,
                         lambda c, b0=b0, bn=bn: hT[:, c, b0:b0 + bn], lambda c: [("hT", c)])
            return hT

        hkeys = [("hT", c) for c in range(KC)]
        psrr = {"i": 0}

        def nextps(lo=0, hi=8):
            i = lo + psrr["i"] % (hi - lo)
            psrr["i"] += 1
            return i

        def resid_add(layer, ysrc_fn, ykeys, s, wcol_fn, nk, jgate, blocks=None):
            for m in range(KC):
                for (b0, bn, which) in (blocks or ALLBLK):
                    pb = nextps()
                    for k in range(nk):
                        P.op("pe", lambda e, m=m, k=k, pb=pb, b0=b0, bn=bn: e.matmul(
                            PS[pb][:, 0:bn], lhsT=wcol_fn(k, m), rhs=ysrc_fn(k, b0, bn),
                            start=(k == 0), stop=(k == nk - 1)),
                            reads=[("WS", s)] + ykeys(k), writes=[("ps", pb)])
                    P.op("dve", lambda e, m=m, pb=pb, b0=b0, bn=bn, which=which: e.scalar_tensor_tensor(
                        out=xT[:, m, b0:b0 + bn], in0=PS[pb][:, 0:bn], scalar=modv(layer, jgate, m, which),
                        in1=xT[:, m, b0:b0 + bn], op0=ALU.mult, op1=ALU.add),
                        reads=[("ps", pb), "MOD"] + xk(m, b0, bn), writes=xk(m, b0, bn))

        def mixer0():
            layer = 0
            hT = compute_hT(layer)
            WP = 2336

            def col(t):
                return t + 8 if t < NCTX else t + 24
            U = rv("SCR", 0, WP, F32, key="U")
            T1 = rv("SCR", 9344, WP, F32, key="T1")
            dbuf = rv("SCR", 18688, WP, BF16, key="dbuf")
            ypool = rv("SCR", 23360, 4 * NT, BF16,
                       keys=[(("ypool", g), g * NT, NT) for g in range(4)]).rearrange("p (g t) -> p g t", g=4)
            T2 = rv("ROPE", 0, WP, F32, key="T2")
            s_u = wload(wpool_d, KC * 512)
            for (a, b) in ((0, 8), (264, 280), (2328, 2336)):
                P.op("dve", lambda e, a=a, b=b: e.memset(U[:, a:b], 0.0), writes=["U"])
            tmpE = MISC[:, 8:16]
            for g, w in enumerate((2, 4, 8, 16)):
                for (b0, bn, which) in ALLBLK:
                    pb = nextps()
                    for kc in range(KC):
                        P.op("pe", lambda e, kc=kc, pb=pb, b0=b0, bn=bn, g=g: e.matmul(
                            PS[pb][:, 0:bn], lhsT=WS[s_u][:, kc * 512 + g * 128: kc * 512 + g * 128 + 128],
                            rhs=hT[:, kc, b0:b0 + bn], start=(kc == 0), stop=(kc == KC - 1)),
                            reads=[("WS", s_u), ("hT", kc)], writes=[("ps", pb)])
                    P.op("act", lambda e, pb=pb, b0=b0, bn=bn: e.activation(
                        out=U[:, col(b0):col(b0) + bn], in_=PS[pb][:, 0:bn], func=AF.Identity),
                        reads=[("ps", pb)], writes=["U"])
                P.op("dve", lambda e: e.tensor_tensor(out=T1[:, 1:WP], in0=U[:, 0:WP - 1], in1=U[:, 1:WP], op=ALU.add),
                     reads=["U"], writes=["T1"])
                A, Akey = T1, "T1"
                if w >= 4:
                    P.op("dve", lambda e: e.tensor_tensor(out=T2[:, 2:WP - 1], in0=T1[:, 1:WP - 2], in1=T1[:, 3:WP],
                                                          op=ALU.add), reads=["T1"], writes=["T2"])
                    A, Akey = T2, "T2"
                if w >= 8:
                    P.op("dve", lambda e: e.tensor_tensor(out=T1[:, 4:WP - 3], in0=T2[:, 2:WP - 5], in1=T2[:, 6:WP - 1],
                                                          op=ALU.add), reads=["T2"], writes=["T1"])
                    A, Akey = T1, "T1"
                if w >= 16:
                    P.op("dve", lambda e: e.tensor_tensor(out=T2[:, 8:WP - 7], in0=T1[:, 4:WP - 11], in1=T1[:, 12:WP - 3],
                                                          op=ALU.add), reads=["T1"], writes=["T2"])
                    A, Akey = T2, "T2"
                P.op("dve", lambda e, A=A, w=w: e.scalar_tensor_tensor(
                    out=dbuf[:, 8:2328], in0=A[:, 8:2328], scalar=1.0 / w, in1=U[:, 8:2328],
                    op0=ALU.mult, op1=ALU.subtract), reads=[Akey, "U"], writes=["dbuf"])
                hw = w // 2
                for ei, c0 in enumerate((col(0), col(NCTX - 1) + 1 - hw, col(NCTX), col(NT - 1) + 1 - hw)):
                    tbl = CONST[:, CS("pedge", (g * 4 + ei) * 8, hw)]
                    P.op("dve", lambda e, A=A, c0=c0, tbl=tbl, hw=hw: e.tensor_tensor(
                        out=tmpE[:, 0:hw], in0=A[:, c0:c0 + hw], in1=tbl, op=ALU.mult),
                        reads=[Akey, "CONST"], writes=["tmpE"])
                    P.op("dve", lambda e, c0=c0, hw=hw: e.tensor_tensor(
                        out=dbuf[:, c0:c0 + hw], in0=tmpE[:, 0:hw], in1=U[:, c0:c0 + hw], op=ALU.subtract),
                        reads=["tmpE", "U", "dbuf"], writes=["dbuf"])
                for (b0, bn, which) in ALLBLK:
                    pb = nextps()
                    P.op("pe", lambda e, pb=pb, b0=b0, bn=bn, g=g: e.matmul(
                        PS[pb][:, 0:bn], lhsT=CB[:, 256 + g * 128: 256 + (g + 1) * 128],
                        rhs=dbuf[:, col(b0):col(b0) + bn], start=True, stop=True),
                        reads=["CB", "dbuf"], writes=[("ps", pb)])
                    P.op("act", lambda e, pb=pb, b0=b0, bn=bn, g=g: e.activation(
                        out=ypool[:, g, b0:b0 + bn], in_=PS[pb][:, 0:bn], func=AF.Identity,
                        scale=CONST[:, CS("pscale", g, 1)]),
                        reads=[("ps", pb), "CONST"], writes=[("ypool", g)])
            s_w = state["slot"] % NSLOT
            state["slot"] += 1
            P.dma("pool", WS[s_w][:, 0:4096].rearrange("p (k n) -> p k n", k=4),
                  wout0_d[4:8].rearrange("k p n -> p k n"), writes=[("WS", s_w)])
            resid_add(layer, lambda k, b0, bn: ypool[:, k, b0:b0 + bn], lambda k: [("ypool", k)], s_w,
                      lambda k, m: WS[s_w][:, k * 1024 + m * 128: k * 1024 + m * 128 + 128], 4, 2)

            ropeT = rv("ROPE", 0, 2 * NLAT, F32, key="rope")
            P.dma("sp", ropeT, rope_d, writes=["rope"])
            CTt, STt = ropeT[:, 0:NLAT], ropeT[:, NLAT:2 * NLAT]
            P.op("dve", lambda e: e.tensor_tensor(out=MISC[:, 16:80], in0=CONST[:, CS("lqk", 0, 64)],
                                                  in1=CONST[:, CS("lqk", 64, 64)], op=ALU.mult),
                 reads=["CONST"], writes=["lam_p1"])
            P.op("dve", lambda e: e.tensor_tensor(out=MISC[:, 80:144], in0=CONST[:, CS("lqk", 128, 64)],
                                                  in1=CONST[:, CS("lqk", 192, 64)], op=ALU.mult),
                 reads=["CONST"], writes=["lam_p2"])
            P.op("dve", lambda e: e.reduce_sum(out=MISC[:, 1:2], in_=MISC[:, 16:80], axis=mybir.AxisListType.X),
                 reads=["lam_p1"], writes=["lam_s1"])
            P.op("dve", lambda e: e.reduce_sum(out=MISC[:, 2:3], in_=MISC[:, 80:144], axis=mybir.AxisListType.X),
                 reads=["lam_p2"], writes=["lam_s2"])
            P.op("act", lambda e: e.activation(out=MISC[:, 1:3], in_=MISC[:, 1:3], func=AF.Exp),
                 reads=["lam_s1", "lam_s2"], writes=["lam_e"])
            P.op("dve", lambda e: e.tensor_tensor(out=MISC[:, 3:4], in0=MISC[:, 2:3], in1=MISC[:, 1:2],
                                                  op=ALU.subtract), reads=["lam_e"], writes=["neglam0"])
            lambda_init = 0.8 - 0.6 * math.exp(-0.3 * 0)
            P.op("dve", lambda e: e.tensor_scalar_add(out=MISC[:, 3:4], in0=MISC[:, 3:4], scalar1=-lambda_init),
                 reads=["neglam0"], writes=["neglam"])
            P.op("dve", lambda e: e.tensor_scalar_mul(out=MISC[:, 4:8], in0=CONST[:, CS("subln")],
                                                      scalar1=1.0 - lambda_init),
                 reads=["CONST"], writes=["G08"])
            neglam = MISC[:, 3:4]

            qT = rv("SCR", 0, NT, BF16, key="qT")
            kT = rv("SCR", 4608, NT, BF16, key="kT")
            Vh = rv("SCR", 9216, 18 * 128, BF16, key="Vh").rearrange("p (i d) -> p i d", i=18)
            yh = rv("SCR", 13824, NT, BF16, key="yh")
            E = [[rv("SCR", 18432 + (m * 2 + b) * 1024, 512, BF16, key=("E", m, b)) for b in range(2)]
                 for m in range(2)]
            rt = [rv("SCR", 22528 + i * 2048, 512, F32, key=("rt", i)) for i in range(2)]
            r0 = rv("SCR", 26624, 512, F32, key="r0")
            r1 = rv("SCR", 28672, 512, F32, key="r1")
            t0b = rv("SCR", 30720, 512, F32, key="t0b")
            t1b = rv("SCR", 32768, 512, F32, key="t1b")
            sqh = rv("SCR", 34816, 512, BF16, key="sqh")
            rsh = rv("SCR", 35840, 512, F32, key="rsh")
            rrh = rv("SCR", 37888, 512, F32, key="rrh")
            for h in range(4):
                s = wload(whead_d[h], KC * 640)
                W = lambda kc, c0: WS[s][:, kc * 640 + c0: kc * 640 + c0 + 128]
                for (dst, dkey, c0) in ((qT, "qT", 0), (kT, "kT", 256)):
                    for (b0, bn, which) in ALLBLK:
                        pa = nextps()
                        for kc in range(KC):
                            P.op("pe", lambda e, kc=kc, pa=pa, b0=b0, bn=bn, c0=c0: e.matmul(
                                PS[pa][:, 0:bn], lhsT=W(kc, c0), rhs=hT[:, kc, b0:b0 + bn],
                                start=(kc == 0), stop=(kc == KC - 1)),
                                reads=[("WS", s), ("hT", kc)], writes=[("ps", pa)])
                        if which == 1:
                            P.op("act", lambda e, pa=pa, b0=b0, bn=bn, dst=dst: e.activation(
                                out=dst[:, b0:b0 + bn], in_=PS[pa][:, 0:bn], func=AF.Identity),
                                reads=[("ps", pa)], writes=[dkey])
                            continue
                        pb = nextps()
                        for kc in range(KC):
                            P.op("pe", lambda e, kc=kc, pb=pb, b0=b0, bn=bn, c0=c0: e.matmul(
                                PS[pb][:, 0:bn], lhsT=W(kc, c0 + 128), rhs=hT[:, kc, b0:b0 + bn],
                                start=(kc == 0), stop=(kc == KC - 1)),
                                reads=[("WS", s), ("hT", kc)], writes=[("ps", pb)])
                        l0 = b0 - NCTX
                        P.op("dve", lambda e, pa=pa, bn=bn, l0=l0: e.tensor_tensor(
                            out=rt[0][:, 0:bn], in0=PS[pa][:, 0:bn], in1=CTt[:, l0:l0 + bn], op=ALU.mult),
                            reads=[("ps", pa), "rope"], writes=[("rt", 0)])
                        P.op("dve", lambda e, pb=pb, bn=bn, l0=l0: e.tensor_tensor(
                            out=rt[1][:, 0:bn], in0=PS[pb][:, 0:bn], in1=STt[:, l0:l0 + bn], op=ALU.mult),
                            reads=[("ps", pb), "rope"], writes=[("rt", 1)])
                        P.op("dve", lambda e, b0=b0, bn=bn, dst=dst: e.tensor_tensor(
                            out=dst[:, b0:b0 + bn], in0=rt[0][:, 0:bn], in1=rt[1][:, 0:bn], op=ALU.add),
                            reads=[("rt", 0), ("rt", 1)], writes=[dkey])
                for i4 in range(0, 18, 4):
                    nt_ = min(4, 18 - i4)
                    pb = nextps()
                    for j in range(nt_):
                        i = i4 + j
                        for kc in range(KC):
                            P.op("pe", lambda e, kc=kc, pb=pb, i=i, j=j: e.matmul(
                                PS[pb][:, j * 128:(j + 1) * 128], lhsT=hT[:, kc, i * 128:(i + 1) * 128],
                                rhs=W(kc, 512), start=(kc == 0), stop=(kc == KC - 1)),
                                reads=[("WS", s), ("hT", kc)], writes=[("ps", pb)])
                    P.op("act", lambda e, pb=pb, i4=i4, nt_=nt_: e.activation(
                        out=Vh[:, i4:i4 + nt_, :], in_=PS[pb][:, 0:nt_ * 128].rearrange("p (i d) -> p i d", i=nt_),
                        func=AF.Identity), reads=[("ps", pb)], writes=["Vh"])
                for (q0, n, which) in ALLBLK:
                    nkt = 2 if which == 1 else 18

                    def qk(i):
                        b = i % 2
                        for m in range(2):
                            P.op("pe", lambda e, m=m, b=b, i=i: e.matmul(
                                PS[4 + m * 2 + b][:, 0:n], lhsT=kT[64 * m:64 * m + 64, i * 128:(i + 1) * 128],
                                rhs=qT[64 * m:64 * m + 64, q0:q0 + n], start=True, stop=True),
                                reads=["kT", "qT"], writes=[("ps", 4 + m * 2 + b)])
                    qk(0)
                    for i in range(nkt):
                        b = i % 2
                        if i + 1 < nkt:
                            qk(i + 1)
                        for m in range(2):
                            P.op("act", lambda e, m=m, b=b, n=n: e.activation(
                                out=E[m][b][:, 0:n], in_=PS[4 + m * 2 + b][:, 0:n], func=AF.Exp, scale=0.125),
                                reads=[("ps", 4 + m * 2 + b)], writes=[("E", m, b)])
                        for m in range(2):
                            P.op("pe", lambda e, m=m, b=b, i=i, n=n: e.matmul(
                                PS[m][:, 0:n], lhsT=Vh[:, i, :], rhs=E[m][b][:, 0:n],
                                start=(i == 0), stop=(i == nkt - 1)),
                                reads=["Vh", ("E", m, b)], writes=[("ps", m)])
                            P.op("pe", lambda e, m=m, b=b, i=i, n=n: e.matmul(
                                PS[2 + m][:, 0:n], lhsT=ones, rhs=E[m][b][:, 0:n],
                                start=(i == 0), stop=(i == nkt - 1)),
                                reads=["CB", ("E", m, b)], writes=[("ps", 2 + m)])
                    P.op("act", lambda e, n=n: e.activation(out=r0[:, 0:n], in_=PS[2][:, 0:n], func=AF.Ln),
                         reads=[("ps", 2)], writes=["r0"])
                    P.op("act", lambda e, n=n: e.activation(out=r1[:, 0:n], in_=PS[3][:, 0:n], func=AF.Ln),
                         reads=[("ps", 3)], writes=["r1"])
                    P.op("act", lambda e, n=n: e.activation(out=r0[:, 0:n], in_=r0[:, 0:n], func=AF.Exp, scale=-1.0),
                         reads=["r0"], writes=["r0"])
                    P.op("act", lambda e, n=n: e.activation(out=r1[:, 0:n], in_=r1[:, 0:n], func=AF.Exp, scale=-1.0),
                         reads=["r1"], writes=["r1"])
                    P.op("dve", lambda e, n=n: e.tensor_tensor(out=t0b[:, 0:n], in0=PS[0][:, 0:n], in1=r0[:, 0:n],
                                                               op=ALU.mult),
                         reads=[("ps", 0), "r0"], writes=["t0b"])
                    P.op("dve", lambda e, n=n: e.tensor_tensor(out=t1b[:, 0:n], in0=PS[1][:, 0:n], in1=r1[:, 0:n],
                                                               op=ALU.mult),
                         reads=[("ps", 1), "r1"], writes=["t1b"])
                    P.op("dve", lambda e, n=n: e.scalar_tensor_tensor(
                        out=t0b[:, 0:n], in0=t1b[:, 0:n], scalar=neglam, in1=t0b[:, 0:n],
                        op0=ALU.mult, op1=ALU.add), reads=["t1b", "t0b", "neglam"], writes=["t0b"])
                    P.op("act", lambda e, n=n: e.activation(out=sqh[:, 0:n], in_=t0b[:, 0:n], func=AF.Square),
                         reads=["t0b"], writes=["sqh"])
                    P.op("pe", lambda e, n=n: e.matmul(PS[2][:, 0:n], lhsT=ones, rhs=sqh[:, 0:n], start=True, stop=True),
                         reads=["CB", "sqh"], writes=[("ps", 2)])
                    P.op("act", lambda e, n=n: e.activation(out=rsh[:, 0:n], in_=PS[2][:, 0:n], func=AF.Ln,
                                                            scale=1.0 / 128.0, bias=EPSB),
                         reads=[("ps", 2), "EPSB"], writes=["rsh"])
                    P.op("act", lambda e, n=n: e.activation(out=rrh[:, 0:n], in_=rsh[:, 0:n], func=AF.Exp, scale=-0.5),
                         reads=["rsh"], writes=["rrh"])
                    P.op("dve", lambda e, n=n, q0=q0, h=h: e.scalar_tensor_tensor(
                        out=yh[:, q0:q0 + n], in0=t0b[:, 0:n], scalar=MISC[:, 4 + h:5 + h], in1=rrh[:, 0:n],
                        op0=ALU.mult, op1=ALU.mult), reads=["t0b", "G08", "rrh"], writes=["yh"])
                s_w = wload(wout0_d[h], D)
                resid_add(layer, lambda k, b0, bn: yh[:, b0:b0 + bn], lambda k: ["yh"], s_w,
                          lambda k, m, s_w=s_w: WS[s_w][:, m * 128:(m + 1) * 128], 1, 2)

        HS = sb("HG_S", [128, 2 * 128], F32)
        HSm = sb("HG_Sm", [128, 2 * 128], BF16)
        ATm = sb("HG_ATm", [128, 2 * 64], BF16)
        HSC = sb("HG_SC", [128, 6 * 36 + 16 + 32], F32)
        LATBLK = [(b0, bn, 0) for (b0, bn) in tok_blocks(NCTX, NT)]

        def mixer1():
            stages = stop.split("+")
            layer = 1
            hT = compute_hT(layer)
            LB = HSC[:, 232:248]
            OML = HSC[:, 248:264]
            lbl = CONST[:, CS("lblog")].rearrange("p (d l h) -> p d l h", d=2, l=2)
            LB3 = LB.rearrange("p (d h) -> p d h", d=2)
            P.op("dve", lambda e: e.tensor_tensor(out=LB3, in0=lbl[:, :, 1, :], in1=lbl[:, :, 0, :], op=ALU.subtract),
                 reads=["CONST"], writes=["LB"])
            P.op("act", lambda e: e.activation(out=LB, in_=LB, func=AF.Sigmoid), reads=["LB"], writes=["LB"])
            P.op("dve", lambda e: e.tensor_scalar(out=OML, in0=LB, scalar1=-1.0, scalar2=1.0, op0=ALU.mult,
                                                  op1=ALU.add), reads=["LB"], writes=["OML"])

            Qd = [rv("SCR", d * 4096, NLAT, BF16, key=("Qd", d)) for d in range(2)]
            Kd = [rv("SCR", 8192 + d * 4608, NT, BF16, key=("Kd", d)) for d in range(2)]
            Ktok = [rv("SCR", 17408 + d * 4608, 18 * 128, BF16, key=("Ktok", d)).rearrange("p (i c) -> p i c", i=18)
                    for d in range(2)]
            Vh = rv("SCR", 26624, 18 * 128, BF16, key="Vh1").rearrange("p (i d) -> p i d", i=18)
            TO = 31232
            OF = rv("ROPE", 0, NLAT, F32, keys=[(("OF", b), b * 512, 512) for b in range(4)])
            SG = rv("ROPE", 8192, NLAT, BF16, key="SG")
            YH = rv("ROPE", 12288, NLAT, BF16, key="YH")
            AL = [HSC[:, d * 108 + 0: d * 108 + 36] for d in range(2)]
            BE = [HSC[:, d * 108 + 36: d * 108 + 72] for d in range(2)]
            GA = [HSC[:, d * 108 + 72: d * 108 + 108] for d in range(2)]
            RP = HSC[:, 216:224]
            PIECES = tok_blocks(0, NT)
            for h in range(8):
                Qp = rv("SCR", TO, 512, F32, key="Qp")
                Fb = [rv("SCR", TO + 2048 + d * 2048, 512, F32, key=("Fb", d)) for d in range(2)]
                Kk = rv("SCR", TO + 6144, 512, F32, key="Kk")
                Bc = rv("SCR", TO + 8192, 512, F32, key="Bc")
                Gl = rv("SCR", TO + 10240, 512, F32, key="Gl")
                s = wload(whg_d[h], KC * 640)
                W = lambda kc, c0: WS[s][:, kc * 640 + c0: kc * 640 + c0 + 128]

                def proj(c0, p0, n, pb):
                    for kc in range(KC):
                        P.op("pe", lambda e, kc=kc: e.matmul(
                            PS[pb][:, 0:n], lhsT=W(kc, c0), rhs=hT[:, kc, p0:p0 + n],
                            start=(kc == 0), stop=(kc == KC - 1)),
                            reads=[("WS", s), ("hT", kc)], writes=[("ps", pb)])

                for i4 in range(0, 18, 4):
                    nt_ = min(4, 18 - i4)
                    pb = nextps()
                    for j in range(nt_):
                        i = i4 + j
                        for kc in range(KC):
                            P.op("pe", lambda e, kc=kc, pb=pb, i=i, j=j: e.matmul(
                                PS[pb][:, j * 128:(j + 1) * 128], lhsT=hT[:, kc, i * 128:(i + 1) * 128],
                                rhs=W(kc, 128), start=(kc == 0), stop=(kc == KC - 1)),
                                reads=[("WS", s), ("hT", kc)], writes=[("ps", pb)])
                    P.op("act", lambda e, pb=pb, i4=i4, nt_=nt_: e.activation(
                        out=Vh[:, i4:i4 + nt_, :], in_=PS[pb][:, 0:nt_ * 128].rearrange("p (i d) -> p i d", i=nt_),
                        func=AF.Identity), reads=[("ps", pb)], writes=["Vh1"])
                for (b0, bn, _) in LATBLK:
                    pb = nextps()
                    proj(512, b0, bn, pb)
                    P.op("act", lambda e, pb=pb, b0=b0, bn=bn: e.activation(
                        out=SG[:, b0 - NCTX:b0 - NCTX + bn], in_=PS[pb][:, 0:bn], func=AF.Silu),
                        reads=[("ps", pb)], writes=["SG"])
                for (p0, n) in PIECES:
                    nch = n // 64
                    c0 = p0 // 64
                    l0 = max(p0, NCTX)
                    ln = p0 + n - l0
                    lo = l0 - p0
                    v3 = lambda t, n=n: t[:, 0:n].rearrange("p (c t) -> p c t", t=64)
                    sc = lambda t, nch=nch, c0=c0: t[:, c0:c0 + nch].rearrange("p (c o) -> p c o", o=1)
                    pb = nextps()
                    proj(0, p0, n, pb)
                    P.op("act", lambda e, pb=pb, n=n: e.activation(out=Qp[:, 0:n], in_=PS[pb][:, 0:n], func=AF.Silu),
                         reads=[("ps", pb)], writes=["Qp"])
                    for d in range(2):
                        pb = nextps()
                        proj(256 + d * 128, p0, n, pb)
                        P.op("act", lambda e, pb=pb, n=n, d=d: e.activation(out=Fb[d][:, 0:n], in_=PS[pb][:, 0:n],
                                                                            func=AF.Sigmoid),
                             reads=[("ps", pb)], writes=[("Fb", d)])
                    for d in range(2):
                        F = Fb[d]
                        P.op("dve", lambda e, n=n, d=d, h=h, F=F: e.tensor_scalar(
                            out=F[:, 0:n], in0=F[:, 0:n], scalar1=OML[:, d * 8 + h:d * 8 + h + 1],
                            scalar2=LB[:, d * 8 + h:d * 8 + h + 1], op0=ALU.mult, op1=ALU.add),
                            reads=[("Fb", d), "OML", "LB"], writes=[("Fb", d)])
                        P.op("dve", lambda e, n=n, F=F: e.tensor_scalar(
                            out=Kk[:, 0:n], in0=F[:, 0:n], scalar1=-1.0, scalar2=1.0, op0=ALU.mult, op1=ALU.add),
                            reads=[("Fb", d)], writes=["Kk"])
                        P.op("act", lambda e, n=n, F=F: e.activation(out=F[:, 0:n], in_=F[:, 0:n], func=AF.Ln),
                             reads=[("Fb", d)], writes=[("Fb", d)])
                        P.op("dve", lambda e, n=n, F=F: e.tensor_tensor_scan(
                            out=Bc[:, 0:n], data0=CONST[:, CS("m01", 0, n)], data1=F[:, 0:n], initial=0.0,
                            op0=ALU.mult, op1=ALU.add), reads=[("Fb", d), "CONST"], writes=["Bc"])
                        B3 = v3(Bc)
                        G3 = v3(Gl)
                        P.op("dve", lambda e, nch=nch, B3=B3, G3=G3: e.tensor_tensor(
                            out=G3, in0=B3, in1=B3[:, :, 31:32].to_broadcast([128, nch, 64]), op=ALU.subtract),
                            reads=["Bc"], writes=["Gl"])
                        P.op("dve", lambda e, d=d, B3=B3: e.tensor_copy(out=sc(AL[d]), in_=B3[:, :, 63:64]),
                             reads=["Bc"], writes=[("AL", d)])
                        bsrc = G3[:, :, 63:64] if d == 0 else B3[:, :, 31:32]
                        gsrc = B3[:, :, 31:32] if d == 0 else G3[:, :, 63:64]
                        P.op("dve", lambda e, d=d, bsrc=bsrc: e.tensor_copy(out=sc(BE[d]), in_=bsrc),
                             reads=["Bc", "Gl"], writes=[("BE", d)])
                        P.op("dve", lambda e, d=d, gsrc=gsrc: e.tensor_copy(out=sc(GA[d]), in_=gsrc),
                             reads=["Bc", "Gl"], writes=[("GA", d)])
                        if d == 0:
                            P.op("act", lambda e, n=n: e.activation(out=Bc[:, 0:n], in_=Gl[:, 0:n], func=AF.Exp),
                                 reads=["Gl", "Bc"], writes=["Bc"])
                            P.op("act", lambda e, n=n, F=F: e.activation(out=F[:, 0:n], in_=Gl[:, 0:n], func=AF.Exp,
                                                                         scale=-1.0),
                                 reads=["Gl", ("Fb", d)], writes=[("Fb", d)])
                            if ln > 0:
                                P.op("dve", lambda e, lo=lo, ln=ln, l0=l0: e.tensor_tensor(
                                    out=Qd[0][:, l0 - NCTX:l0 - NCTX + ln], in0=Qp[:, lo:lo + ln],
                                    in1=Bc[:, lo:lo + ln], op=ALU.mult), reads=["Qp", "Bc"], writes=[("Qd", 0)])
                            P.op("dve", lambda e, n=n, p0=p0, F=F: e.tensor_tensor(
                                out=Kd[0][:, p0:p0 + n], in0=Kk[:, 0:n], in1=F[:, 0:n], op=ALU.mult),
                                reads=["Kk", ("Fb", d)], writes=[("Kd", 0)])
                        else:
                            F3 = v3(F)
                            P.op("dve", lambda e, F3=F3, G3=G3: e.tensor_copy(out=F3[:, :, 1:64], in_=G3[:, :, 0:63]),
                                 reads=["Gl", ("Fb", d)], writes=[("Fb", d)])
                            P.op("dve", lambda e, F3=F3, B3=B3: e.tensor_scalar(
                                out=F3[:, :, 0:1], in0=B3[:, :, 31:32], scalar1=-1.0, scalar2=None, op0=ALU.mult),
                                reads=["Bc", ("Fb", d)], writes=[("Fb", d)])
                            P.op("act", lambda e, n=n, F=F: e.activation(out=Bc[:, 0:n], in_=F[:, 0:n], func=AF.Exp),
                                 reads=[("Fb", d), "Bc"], writes=["Bc"])
                            P.op("dve", lambda e, n=n, p0=p0: e.tensor_tensor(
                                out=Kd[1][:, p0:p0 + n], in0=Kk[:, 0:n], in1=Bc[:, 0:n], op=ALU.mult),
                                reads=["Kk", "Bc"], writes=[("Kd", 1)])
                            if ln > 0:
                                P.op("act", lambda e, lo=lo, ln=ln, F=F: e.activation(
                                    out=Gl[:, lo:lo + ln], in_=F[:, lo:lo + ln], func=AF.Exp, scale=-1.0),
                                    reads=[("Fb", d), "Gl"], writes=["Gl"])
                                P.op("dve", lambda e, lo=lo, ln=ln, l0=l0: e.tensor_tensor(
                                    out=Qd[1][:, l0 - NCTX:l0 - NCTX + ln], in0=Qp[:, lo:lo + ln],
                                    in1=Gl[:, lo:lo + ln], op=ALU.mult), reads=["Qp", "Gl"], writes=[("Qd", 1)])
                for d in range(2):
                    P.op("act", lambda e, d=d: e.activation(out=HSC[:, d * 108:(d + 1) * 108],
                                                            in_=HSC[:, d * 108:(d + 1) * 108], func=AF.Exp),
                         reads=[("AL", d), ("BE", d), ("GA", d)], writes=[("AL", d), ("BE", d), ("GA", d)])
                for d in range(2):
                    for i4 in range(0, 18, 4):
                        nt_ = min(4, 18 - i4)
                        pb = nextps()
                        psb = PS[pb][:, 0:256].bitcast(BF16)
                        for j in range(nt_):
                            i = i4 + j
                            P.op("pe", lambda e, d=d, i=i, j=j, psb=psb: e.transpose(
                                psb[:, j * 128:(j + 1) * 128], Kd[d][:, i * 128:(i + 1) * 128], ident),
                                reads=[("Kd", d), "CB"], writes=[("ps", pb)])
                        P.op("act", lambda e, d=d, i4=i4, nt_=nt_, psb=psb: e.activation(
                            out=Ktok[d][:, i4:i4 + nt_, :],
                            in_=psb[:, 0:nt_ * 128].rearrange("p (i c) -> p i c", i=nt_), func=AF.Identity),
                            reads=[("ps", pb)], writes=[("Ktok", d)])
                osum = rv("SCR", TO, 512, F32, key="osum")
                rsb = rv("SCR", TO + 2048, 512, F32, key="rsb")
                rrb = rv("SCR", TO + 4096, 512, F32, key="rrb")
                ytmp = rv("SCR", TO + 6144, 512, F32, key="ytmp")
                sqb = rv("SCR", TO + 8192, 512, BF16, key="sqb")
                order = [list(range(36)), [3, 2, 1, 0] + list(range(35, 3, -1))]
                maskc = [CONST[:, CS("maskf")], CONST[:, CS("maskb")]]
                obank = {}
                def chunk_ctx(step, d):
                    n = order[d][step]
                    i, half = n // 2, n % 2
                    c = dict(n=n, i=i, pbase=half * 64, t0=n * 64, is_lat=(n >= 4), first=(step == 0),
                             last=(step == 35), Sd=HS[:, d * 128:(d + 1) * 128], Smd=HSm[:, d * 128:(d + 1) * 128],
                             kvb=6 + d, atb=4 + d, d=d, step=step)
                    if c["is_lat"]:
                        lt = c["t0"] - NCTX
                        c.update(lt=lt, blk=lt // 512, ob=d * 2 + ((lt // 512) % 2), oc=lt % 512)
                    return c

                def partA(c):
                    d, i, pbase, kvb = c["d"], c["i"], c["pbase"], c["kvb"]
                    if not c["last"]:
                        P.op("pe", lambda e: e.matmul(
                            PS[kvb][:, 0:128], lhsT=Ktok[d][pbase:pbase + 64, i, :],
                            rhs=Vh[pbase:pbase + 64, i, :], start=True, stop=True),
                            reads=[("Ktok", d), "Vh1"], writes=[("ps", kvb)])
                    if c["is_lat"]:
                        t0, lt, atb = c["t0"], c["lt"], c["atb"]
                        P.op("pe", lambda e: e.matmul(
                            PS[atb][pbase:pbase + 64, 0:64], lhsT=Kd[d][:, t0:t0 + 64],
                            rhs=Qd[d][:, lt:lt + 64], start=True, stop=True),
                            reads=[("Kd", d), ("Qd", d)], writes=[("ps", atb)])

                def partB(c):
                    if not c["is_lat"]:
                        return
                    d, pbase, atb = c["d"], c["pbase"], c["atb"]
                    P.op("dve", lambda e: e.tensor_tensor(
                        out=ATm[pbase:pbase + 64, d * 64:(d + 1) * 64], in0=PS[atb][pbase:pbase + 64, 0:64],
                        in1=maskc[d][pbase:pbase + 64, :], op=ALU.mult),
                        reads=[("ps", atb), "CONST"], writes=[("ATm", d)])

                def partC(c):
                    if not c["is_lat"]:
                        return
                    d, i, pbase, lt, ob, oc, blk, Smd = (c["d"], c["i"], c["pbase"], c["lt"], c["ob"], c["oc"],
                                                       c["blk"], c["Smd"])
                    P.op("pe", lambda e: e.matmul(
                        PS[ob][:, oc:oc + 64], lhsT=Vh[pbase:pbase + 64, i, :],
                        rhs=ATm[pbase:pbase + 64, d * 64:(d + 1) * 64], start=True, stop=False),
                        reads=["Vh1", ("ATm", d)], writes=[("ps", ob)])
                    P.op("pe", lambda e: e.matmul(
                        PS[ob][:, oc:oc + 64], lhsT=Smd, rhs=Qd[d][:, lt:lt + 64], start=False, stop=True),
                        reads=[("Sm", d), ("Qd", d)], writes=[("ps", ob)])
                    done = (oc == 448) if d == 0 else (oc == 0)
                    if not done:
                        return
                    b0 = blk * 512
                    if blk not in obank:
                        obank[blk] = d
                        P.op("act", lambda e: e.activation(
                            out=OF[:, b0:b0 + 512], in_=PS[ob][:, 0:512], func=AF.Identity),
                            reads=[("ps", ob)], writes=[("OF", blk)])
                        return
                    P.op("dve", lambda e: e.tensor_tensor(
                        out=osum[:, :], in0=PS[ob][:, 0:512], in1=OF[:, b0:b0 + 512], op=ALU.add),
                        reads=[("ps", ob), ("OF", blk)], writes=["osum"])
                    P.op("act", lambda e: e.activation(out=sqb[:, :], in_=osum[:, :], func=AF.Square),
                         reads=["osum"], writes=["sqb"])
                    P.op("pe", lambda e: e.matmul(PS[ob][:, 0:512], lhsT=ones, rhs=sqb[:, :], start=True, stop=True),
                         reads=["CB", "sqb"], writes=[("ps", ob)])
                    P.op("act", lambda e: e.activation(
                        out=rsb[:, :], in_=PS[ob][:, 0:512], func=AF.Ln, scale=1.0 / 128.0, bias=EPSB),
                        reads=[("ps", ob), "EPSB"], writes=["rsb"])
                    P.op("act", lambda e: e.activation(out=rrb[:, :], in_=rsb[:, :], func=AF.Exp, scale=-0.5),
                         reads=["rsb"], writes=["rrb"])
                    P.op("dve", lambda e: e.scalar_tensor_tensor(
                        out=ytmp[:, :], in0=osum[:, :], scalar=CONST[:, CS("hgnorm", h, 1)],
                        in1=rrb[:, :], op0=ALU.mult, op1=ALU.mult),
                        reads=["osum", "CONST", "rrb"], writes=["ytmp"])
                    P.op("dve", lambda e: e.tensor_tensor(
                        out=YH[:, b0:b0 + 512], in0=ytmp[:, :], in1=SG[:, b0:b0 + 512], op=ALU.mult),
                        reads=["ytmp", "SG"], writes=["YH"])

                def partD(c):
                    if c["last"]:
                        return
                    d, n, kvb, Sd, Smd, step = c["d"], c["n"], c["kvb"], c["Sd"], c["Smd"], c["step"]
                    if c["first"]:
                        P.op("dve", lambda e: e.tensor_scalar(
                            out=Sd, in0=PS[kvb][:, 0:128], scalar1=BE[d][:, n:n + 1], scalar2=None, op0=ALU.mult),
                            reads=[("ps", kvb), ("BE", d)], writes=[("S", d)])
                    else:
                        P.op("dve", lambda e: e.tensor_scalar(
                            out=Sd, in0=Sd, scalar1=AL[d][:, n:n + 1], scalar2=None, op0=ALU.mult),
                            reads=[("S", d), ("AL", d)], writes=[("S", d)])
                        P.op("dve", lambda e: e.scalar_tensor_tensor(
                            out=Sd, in0=PS[kvb][:, 0:128], scalar=BE[d][:, n:n + 1], in1=Sd,
                            op0=ALU.mult, op1=ALU.add),
                            reads=[("ps", kvb), ("BE", d), ("S", d)], writes=[("S", d)])
                    nn = order[d][step + 1]
                    if nn >= 4:
                        P.op("act", lambda e: e.activation(
                            out=Smd, in_=Sd, func=AF.Identity, scale=GA[d][:, nn:nn + 1]),
                            reads=[("S", d), ("GA", d)], writes=[("Sm", d)])

                for step in range(36 if "nochunk" not in stages else 0):
                    cs = [chunk_ctx(step, d) for d in range(2)]
                    for part in (partA, partB, partC, partD):
                        for c in cs:
                            part(c)
                if "nochunk" in stages:
                    P.op("act", lambda e: e.activation(out=YH[:, :], in_=SG[:, :], func=AF.Identity),
                         reads=["SG"], writes=["YH"])
                s_w = wload(wout1_d[h], D)
                resid_add(layer, lambda k, b0, bn: YH[:, b0 - NCTX:b0 - NCTX + bn], lambda k: ["YH"], s_w,
                          lambda k, m, s_w=s_w: WS[s_w][:, m * 128:(m + 1) * 128], 1, 2, blocks=LATBLK)

        stages = stop.split("+")

        if "mix0" in stages:
            mixer0()
        if "ffn0" in stages:
            ffn(0, [(0, NCTX, 0, NCTX, 1),
                    (NCTX, NT, NCTX, 1024, 0),
                    (NCTX, NT, NCTX + 1024, 1024, 0)])
        if "mix1" in stages:
            mixer1()
        if "ffn1" in stages:
            ffn(1, [(NCTX, NT, NCTX, 1024, 0),
                    (NCTX, NT, NCTX + 1024, 1024, 0)])

        if "rawout" in stages:
            for c in range(KC):
                P.out_tokens.append(P.dma("sp", out_d[:, c, :], xT[:, c, NCTX:NT], reads=xk(c, NCTX, NLAT)))
        else:
            OUTB = [rv("SCR", i * 2048, 512, F32, key=("outb", i)) for i in range(2)]
            T = nrm_temps("SCR", 46080 - 10240, "S")
            it = 0
            for (b0, bn) in tok_blocks(NCTX, NT):
                rms_rstd(T, lambda c, b0=b0, bn=bn: xT[:, c, b0:b0 + bn],
                         lambda c, b0=b0, bn=bn: xk(c, b0, bn), KC, bn, float(D), 1)
                for c in range(KC):
                    b = it % 2
                    it += 1
                    P.op("dve", lambda e, c=c, b=b, b0=b0, bn=bn: e.scalar_tensor_tensor(
                        out=OUTB[b][:, :bn], in0=xT[:, c, b0:b0 + bn], scalar=CONST[:, CS("fnorm", c, 1)],
                        in1=T["rstd"][:, :bn], op0=ALU.mult, op1=ALU.mult),
                        reads=xk(c, b0, bn) + ["CONST", ("rstd", "S")], writes=[("outb", b)])
                    P.out_tokens.append(P.dma("sp", out_d[:, c, b0 - NCTX:b0 - NCTX + bn], OUTB[b][:, :bn],
                                              reads=[("outb", b)]))
        P.finish("sp", P.out_tokens)
    return nc


def _fm(v):
    v = np.asarray(v, np.float32)
    n = v.shape[-1] // 128
    return np.moveaxis(v.reshape(v.shape[:-1] + (n, 128)), -1, 0)


def _wk(w):
    K, N = w.shape
    return np.ascontiguousarray(w.reshape(K // 128, 128, N).transpose(1, 0, 2).reshape(128, (K // 128) * N))


def _rope_tables():
    rows_n = NLAT // GRID_W
    row = np.repeat(np.arange(rows_n, dtype=np.float32), GRID_W)
    col = np.tile(np.arange(GRID_W, dtype=np.float32), rows_n)
    n_freq = 16
    inv = (np.float32(10000.0) ** (-np.arange(n_freq, dtype=np.float32) / n_freq)).astype(np.float32)
    ang = np.stack([row[:, None] * inv, col[:, None] * inv], axis=1).astype(np.float32)
    cos, sin = np.cos(ang).astype(np.float32), np.sin(ang).astype(np.float32)
    CT = np.zeros((128, NLAT), np.float32)
    ST = np.zeros((128, NLAT), np.float32)
    for p in range(128):
        q = p % 64
        axis, half, f = q // 32, (q % 32) // 16, q % 16
        CT[p] = cos[:, axis, f]
        ST[p] = sin[:, axis, f] * (-1.0 if half == 0 else 1.0)
    return np.concatenate([CT, ST], axis=1)


def _swap_cols(w):
    N = w.shape[1]
    idx = np.arange(N)
    blk = idx // 32
    r = idx % 32
    return w[:, blk * 32 + (r + 16) % 32]


def prepare_inputs(inp):
    f32 = np.float32
    g = {k: np.asarray(v, f32) for k, v in inp.items()}
    shared = {}
    wm = g["w_mod"]
    shared["wmod"] = np.stack([np.stack([_wk(wm[l][:, p * 512:(p + 1) * 512]) for p in range(12)]) for l in range(2)])
    shared["rope"] = _rope_tables()
    wi = g["ev_w_in"][0]
    qw, kw, vw, uw = wi[:, 0:512], wi[:, 512:1024], wi[:, 1024:1536], wi[:, 1536:2048]
    qs, ks = _swap_cols(qw), _swap_cols(kw)
    heads = []
    for h in range(4):
        sl = slice(h * 128, (h + 1) * 128)
        heads.append(_wk(np.concatenate([qw[:, sl], qs[:, sl], kw[:, sl], ks[:, sl], vw[:, sl]], axis=1)))
    shared["whead"] = np.stack(heads)
    shared["wpool"] = _wk(uw)
    shared["wout0"] = np.ascontiguousarray(g["ev_w_out"][0].reshape(KC, 128, D))
    wu = g["ffn_w_up"]
    shared["wffn"] = np.stack([np.stack([
        _wk(np.concatenate([wu[l][:, c * 128:(c + 1) * 128], wu[l][:, DFF + c * 128: DFF + (c + 1) * 128]], axis=1))
        for c in range(NPAIR)]) for l in range(2)])
    wd = g["ffn_w_down"]
    shared["wdn"] = np.stack([np.stack([
        np.ascontiguousarray(wd[l][:, m * 128:(m + 1) * 128].reshape(NPAIR, 128, 128).transpose(1, 0, 2)
                             .reshape(128, NPAIR * 128)) for m in range(KC)]) for l in range(2)])
    hw = g["hg_w_in"][0]
    hh = []
    for h in range(8):
        cols = [hw[:, part * 1024 + h * 128: part * 1024 + (h + 1) * 128] for part in range(5)]
        hh.append(_wk(np.concatenate(cols, axis=1)))
    shared["whg"] = np.stack(hh)
    shared["wout1"] = np.ascontiguousarray(g["hg_w_out"][0].reshape(KC, 128, D))
    cb = np.zeros((128, 256 + 512), f32)
    cb[:, 0:128] = np.eye(128, dtype=f32)
    cb[:, 128:256] = 1.0
    cb[:, 256:768] = g["pool_w"][0].transpose(1, 0, 2).reshape(128, 512)
    shared["constb"] = cb

    const = np.zeros((128, NCONST), f32)
    bm = g["b_mod"]
    bmT = np.stack([_fm(bm[l]) for l in range(2)], axis=1)
    const[:, CS("bmod")] = np.repeat(bmT.reshape(128, 96), 2, axis=1).reshape(128, 192)
    lq = np.concatenate([g["da_lq1"][0], g["da_lk1"][0], g["da_lq2"][0], g["da_lk2"][0]])
    const[:, CS("lqk")] = lq[None, :]
    const[:, CS("subln")] = _fm(g["da_subln"][0])
    const[:, CS("pscale")] = _fm(g["pool_scale"][0])
    const[:, CS("hgnorm")] = _fm(g["hg_norm"][0])
    const[:, CS("fnorm")] = _fm(g["final_norm"])
    cwT = _fm(g["ffn_conv_w"])
    const[:, CS("convw")] = cwT.transpose(0, 1, 3, 2).reshape(128, -1)
    const[:, CS("convb")] = _fm(g["ffn_conv_b"]).reshape(128, -1)
    const[:, CS("lblog")] = _fm(g["hg_lb_logits"]).reshape(128, -1)
    pe = np.zeros((4, 4, 8), f32)
    for gi, w in enumerate((2, 4, 8, 16)):
        hw_ = w // 2
        for ei, T in enumerate((NCTX, NCTX, NLAT, NLAT)):
            for i in range(hw_):
                t = i if ei % 2 == 0 else T - hw_ + i
                cnt = min(t + w - hw_, T) - max(t - hw_, 0)
                pe[gi, ei, i] = 1.0 / cnt
    const[:, CS("pedge")] = pe.reshape(1, -1)
    s_idx = np.arange(128)[:, None] % 64
    t_idx = np.arange(64)[None, :]
    const[:, CS("maskf")] = (t_idx >= s_idx).astype(f32)
    const[:, CS("maskb")] = (t_idx <= s_idx).astype(f32)
    const[:, CS("m01")] = (np.arange(512) % 64 != 0).astype(f32)[None, :]

    cctx = _fm(g["c_ctx"])
    in_maps = []
    for b in range(NCORES):
        cst = const.copy()
        cst[:, CS("cT", 0, 8)] = _fm(g["c"][b])
        cst[:, CS("cT", 8, 8)] = cctx
        tok = np.concatenate([g["ctx"][b], g["x"][b]], axis=0)
        xTh = np.ascontiguousarray(tok.T.reshape(KC, 128, NT).transpose(1, 0, 2))
        m = dict(shared)
        m["const"] = cst
        m["xT"] = xTh
        in_maps.append(m)
    return in_maps


_CACHE = {}


def run(inputs, stop="full", trace=False, cores=None):
    in_maps = prepare_inputs(inputs)
    if cores is not None:
        in_maps = [in_maps[b] for b in cores]
    if stop not in _CACHE:
        _CACHE[stop] = build_program(stop)
    nc = _CACHE[stop]
    res = run_bass_kernel_spmd(nc, in_maps, core_ids=list(range(len(in_maps))), trace=trace)
    outs = []
    for r in res.results:
        o = np.asarray(r["outT"], np.float32)
        outs.append(o.transpose(2, 1, 0).reshape(NLAT, D))
    return np.stack(outs), res


def kernel(**inputs):
    out, _ = run(inputs, stop="mix0+ffn0+mix1+ffn1")
    return out
```

```python
import math
from contextlib import ExitStack

import numpy as np
import concourse.bass as bass
import concourse.mybir as mybir
from concourse.bass_utils import run_bass_kernel_spmd

F32 = mybir.dt.float32
BF16 = mybir.dt.bfloat16
AF = mybir.ActivationFunctionType
ALU = mybir.AluOpType

NCORES = 8
D = 1024
KC = 8
NCTX = 256
NLAT = 2048
NT = NCTX + NLAT
DFF = 2816
NPAIR = DFF // 128
EPS = 1e-6
GRID_W = 64
SLOT_ELEMS = 5120
NSLOT = 2
SCR_BYTES = 50688

_c = {}
_off = 0


def _cdef(name, n):
    global _off
    _c[name] = (_off, n)
    _off += n


_cdef("cT", 16)
_cdef("bmod", 2 * 96)
_cdef("lqk", 4 * 64)
_cdef("subln", 4)
_cdef("pscale", 4)
_cdef("hgnorm", 8)
_cdef("fnorm", 8)
_cdef("convw", 2 * 44 * 3)
_cdef("convb", 2 * 44)
_cdef("lblog", 2 * 2 * 8)
_cdef("pedge", 4 * 4 * 8)
_cdef("maskf", 64)
_cdef("maskb", 64)
_cdef("m01", 512)
NCONST = _off


def CS(name, a=0, n=None):
    o, sz = _c[name]
    if n is None:
        n = sz - a
    return slice(o + a, o + a + n)


class Prog:
    ROT = 7000

    def __init__(self, nc, es):
        self.nc = nc
        self.es = es
        self.eng = {"pe": nc.tensor, "act": nc.scalar, "dve": nc.vector,
                    "pool": nc.gpsimd, "sp": nc.sync}
        self.sem = {}
        self.cnt = {}
        self.nsem = 0
        for e in self.eng:
            self.sem[e] = self._newsem(e)
            self.cnt[e] = 0
        self.known = {e: {} for e in self.eng}
        self.res = {}
        self.dq = {}
        for q in ("sp", "pool", "act"):
            self.dq[q] = {"sems": [[self._newsem("d" + q), 0] for _ in range(8)], "rr": 0}
        self.out_tokens = []
        self.active = {}
        self.retired = set()

    def _newsem(self, tag):
        self.nsem += 1
        return self.es.enter_context(self.nc.semaphore(f"s_{tag}_{self.nsem}"))

    def _need(self, e, tok):
        sem, val, owner = tok
        if owner == "pe" and e == "pe":
            return
        k = self.known[e]
        if k.get(sem.name, 0) >= val:
            return
        self.eng[e].wait_ge(sem, val)
        k[sem.name] = val

    def _deps(self, e, reads, writes):
        for key in list(reads) + list(writes):
            if key in self.retired:
                raise RuntimeError(f"use of retired key {key}")
        for key in reads:
            r = self.res.get(key)
            if r and r[0] is not None:
                self._need(e, r[0])
        for key in writes:
            r = self.res.get(key)
            if r:
                if r[0] is not None:
                    self._need(e, r[0])
                for t in r[1]:
                    self._need(e, t)

    def _commit(self, tok, reads, writes):
        for key in reads:
            r = self.res.setdefault(key, [None, []])
            r[1] = [t for t in r[1] if t[0].name != tok[0].name] + [tok]
        for key in writes:
            self.res[key] = [tok, []]

    def op(self, e, fn, reads=(), writes=()):
        self._deps(e, reads, writes)
        if self.cnt[e] >= self.ROT:
            self.sem[e] = self._newsem(e)
            self.cnt[e] = 0
        inst = fn(self.eng[e])
        self.cnt[e] += 1
        inst.then_inc(self.sem[e], 1)
        tok = (self.sem[e], self.cnt[e], e)
        self._commit(tok, reads, writes)
        return tok

    def dma(self, q, out, in_, reads=(), writes=()):
        self._deps(q, reads, writes)
        dq = self.dq[q]
        slot = dq["sems"][dq["rr"] % len(dq["sems"])]
        dq["rr"] += 1
        if slot[1] > 0:
            self._need(q, (slot[0], slot[1], "dma"))
        inst = self.eng[q].dma_start(out=out, in_=in_)
        slot[1] += 16
        inst.then_inc(slot[0], 16)
        tok = (slot[0], slot[1], "dma")
        self._commit(tok, reads, writes)
        return tok

    def alias(self, new_keys, old_keys):
        toks = []
        for k in old_keys:
            r = self.res.get(k)
            if r:
                if r[0] is not None:
                    toks.append(r[0])
                toks.extend(r[1])
        for nk in new_keys:
            r = self.res.setdefault(nk, [None, []])
            r[1] = r[1] + toks

    def claim(self, region, lo, hi, key):
        act = self.active.setdefault(region, {})
        if key in act and act[key] == (lo, hi):
            return
        for k, (l, h) in list(act.items()):
            if l < hi and lo < h:
                self.alias([key], [k])
                del act[k]
                self.retired.add(k)
        act[key] = (lo, hi)
        self.retired.discard(key)

    def finish(self, e, toks):
        for t in toks:
            self._need(e, t)


def tok_blocks(lo, hi, step=512):
    out = []
    t = lo
    while t < hi:
        n = min(step, hi - t)
        out.append((t, n))
        t += n
    return out


def build_program(stop="full", debug=False):
    nc = bass.Bass("TRN2", target_bir_lowering=False)

    def din(name, shape, dt=F32):
        return nc.dram_tensor(name, list(shape), dt, kind="ExternalInput").ap()

    xT_d = din("xT", [128, KC, NT])
    const_d = din("const", [128, NCONST])
    constb_d = din("constb", [128, 256 + 512])
    wmod_d = din("wmod", [2, 12, 128, KC * 512])
    rope_d = din("rope", [128, 2 * NLAT])
    whead_d = din("whead", [4, 128, KC * 640])
    wpool_d = din("wpool", [128, KC * 512])
    wout0_d = din("wout0", [KC, 128, D])
    wffn_d = din("wffn", [2, NPAIR, 128, KC * 256])
    wdn_d = din("wdn", [2, KC, 128, NPAIR * 128])
    whg_d = din("whg", [8, 128, KC * 640])
    wout1_d = din("wout1", [KC, 128, D])
    out_d = nc.dram_tensor("outT", [128, KC, NLAT], F32, kind="ExternalOutput").ap()

    with ExitStack() as es:
        P = Prog(nc, es)

        def sb(name, shape, dt):
            return es.enter_context(nc.sbuf_tensor(name, list(shape), dt))

        xT = sb("xT_sb", [128, KC, NT], F32)
        REG = {"HT": sb("HT", [128, 18432], BF16),
               "SCR": sb("SCR", [128, SCR_BYTES // 2], BF16),
               "ROPE": sb("ROPE", [128, 8192], BF16)}
        WS = [sb(f"WS{i}", [128, SLOT_ELEMS], BF16) for i in range(NSLOT)]
        CONST = sb("CONST", [128, NCONST], F32)
        CB = sb("CB", [128, 256 + 512], BF16)
        MOD = sb("MOD", [128, 2 * 96], F32)
        MOD1 = sb("MOD1", [128, 2 * 96], F32)
        SC = sb("SC", [128, KC * 2], BF16)
        MISC = sb("MISC", [128, 160], F32)
        PS = [es.enter_context(nc.psum_tensor(f"ps{i}", [128, 512], F32)) for i in range(8)]

        ident = CB[:, 0:128]
        ones = CB[:, 128:256]

        def rv(region, off, nelem, dt, key=None, keys=None):
            esz = 2 if dt == BF16 else 4
            assert off % 4 == 0
            v = REG[region][:, off // 2: off // 2 + nelem * esz // 2]
            if dt != BF16:
                v = v.bitcast(dt)
            if key is not None:
                P.claim(region, off, off + nelem * esz, key)
            if keys is not None:
                for (k, so, n) in keys:
                    P.claim(region, off + so * esz, off + (so + n) * esz, k)
            return v

        def xk(c, t0, n):
            return [("xT", c, b) for b in range(t0 // 256, (t0 + n + 255) // 256)]

        P.dma("sp", CONST[:], const_d, writes=["CONST"])
        P.dma("pool", CB[:], constb_d, writes=["CB"])
        for c in range(KC):
            P.dma("sp", xT[:, c, :], xT_d[:, c, :], writes=xk(c, 0, NT))
        P.op("dve", lambda e: e.memset(MISC[:, 0:1], EPS), writes=["EPSB"])
        EPSB = MISC[:, 0:1]

        state = {"slot": 0}

        def wload(src_ap, nelem):
            s = state["slot"] % NSLOT
            state["slot"] += 1
            P.dma("pool", WS[s][:, 0:nelem], src_ap, writes=[("WS", s)])
            return s

        SC3 = SC[:, :].rearrange("p (k two) -> p k two", two=2)
        P.op("act", lambda e: e.activation(out=SC3[:, :, 0], in_=CONST[:, CS("cT", 0, 8)], func=AF.Silu),
             reads=["CONST"], writes=["SC0"])
        P.op("act", lambda e: e.activation(out=SC3[:, :, 1], in_=CONST[:, CS("cT", 8, 8)], func=AF.Silu),
             reads=["CONST"], writes=["SC1"])
        for layer in range(2):
            for piece in range(12):
                s = wload(wmod_d[layer, piece], KC * 512)
                for sub in range(4):
                    jc = piece * 4 + sub
                    col = layer * 96 + jc * 2
                    for kc in range(KC):
                        P.op("pe", lambda e, s=s, kc=kc, sub=sub, col=col: e.matmul(
                            PS[0][:, col:col + 2],
                            lhsT=WS[s][:, kc * 512 + sub * 128: kc * 512 + sub * 128 + 128],
                            rhs=SC[:, kc * 2: kc * 2 + 2],
                            start=(kc == 0), stop=(kc == KC - 1)),
                            reads=[("WS", s), "SC0", "SC1"], writes=[("ps", 0)])
        P.op("dve", lambda e: e.tensor_tensor(out=MOD[:, :], in0=PS[0][:, 0:192], in1=CONST[:, CS("bmod")],
                                               op=ALU.add),
             reads=[("ps", 0), "CONST"], writes=["MOD"])
        P.op("dve", lambda e: e.tensor_scalar_add(out=MOD1[:, :], in0=MOD[:, :], scalar1=1.0),
             reads=["MOD"], writes=["MOD1"])

        def modv(layer, j, chunk, which, plus1=False):
            t = MOD1 if plus1 else MOD
            col = layer * 96 + (j * 8 + chunk) * 2 + which
            return t[:, col:col + 1]

        def nrm_temps(region, off, tag):
            t = {"tag": tag}
            t["sq"] = [rv(region, off + i * 1024, 512, BF16, key=("sq", tag, i)) for i in range(2)]
            t["rs"] = rv(region, off + 2048, 512, F32, key=("rs", tag))
            t["rstd"] = rv(region, off + 4096, 512, F32, key=("rstd", tag))
            t["tmp"] = [rv(region, off + 6144 + i * 2048, 512, F32, key=("ntmp", tag, i)) for i in range(2)]
            return t

        def rms_rstd(T, src_fn, rkeys, nchunk, n, denom, psb):
            tag = T["tag"]
            for c in range(nchunk):
                b = c % 2
                P.op("act", lambda e, c=c, b=b: e.activation(out=T["sq"][b][:, :n], in_=src_fn(c), func=AF.Square),
                     reads=rkeys(c), writes=[("sq", tag, b)])
                P.op("pe", lambda e, c=c, b=b: e.matmul(PS[psb][:, :n], lhsT=ones, rhs=T["sq"][b][:, :n],
                                                        start=(c == 0), stop=(c == nchunk - 1)),
                     reads=[("sq", tag, b), "CB"], writes=[("ps", psb)])
            P.op("act", lambda e: e.activation(out=T["rs"][:, :n], in_=PS[psb][:, :n], func=AF.Ln,
                                               scale=1.0 / denom, bias=EPSB),
                 reads=[("ps", psb), "EPSB"], writes=[("rs", tag)])
            P.op("act", lambda e: e.activation(out=T["rstd"][:, :n], in_=T["rs"][:, :n], func=AF.Exp, scale=-0.5),
                 reads=[("rs", tag)], writes=[("rstd", tag)])

        def norm_mod(T, layer, jshift, jscale, t0, n, which, out_fn, out_keys, psb=1):
            tag = T["tag"]
            rms_rstd(T, lambda c: xT[:, c, t0:t0 + n], lambda c: xk(c, t0, n), KC, n, float(D), psb)
            for c in range(KC):
                b = c % 2
                P.op("dve", lambda e, c=c, b=b: e.scalar_tensor_tensor(
                    out=T["tmp"][b][:, :n], in0=xT[:, c, t0:t0 + n], scalar=modv(layer, jscale, c, which, True),
                    in1=T["rstd"][:, :n], op0=ALU.mult, op1=ALU.mult),
                    reads=xk(c, t0, n) + ["MOD1", ("rstd", tag)], writes=[("ntmp", tag, b)])
                P.op("act", lambda e, c=c, b=b: e.activation(
                    out=out_fn(c), in_=T["tmp"][b][:, :n], func=AF.Identity,
                    bias=modv(layer, jshift, c, which), scale=1.0),
                    reads=[("ntmp", tag, b), "MOD"], writes=out_keys(c))

        def ffn(layer, groups):
            HW2 = 1156
            GT = 1152
            cw = lambda c, tap: CONST[:, CS("convw", (layer * 44 + c) * 3 + tap, 1)]
            cb = lambda c: CONST[:, CS("convb", layer * 44 + c, 1)]
            h2 = rv("HT", 0, KC * HW2, BF16,
                    keys=[(("h2", c), c * HW2, HW2) for c in range(KC)]).rearrange("p (k t) -> p k t", k=KC)
            o1 = KC * HW2 * 2
            ctmp = [[rv("HT", o1 + (i * 2 + j) * 1920, 480, F32, key=("ctmp", i, j))
                     for j in range(2)] for i in range(2)]
            T = nrm_temps("HT", o1 + 4 * 1920, "H")
            stash = rv("HT", o1 + 4 * 1920 + 10240, KC * 8, BF16,
                       keys=[(("stash", c), c * 8, 8) for c in range(KC)]).rearrange("p (k t) -> p k t", k=KC)
            gbuf = rv("SCR", 0, NPAIR * GT, BF16,
                      keys=[(("g", c), c * GT, GT) for c in range(NPAIR)]
                      ).rearrange("p (c t) -> p c t", c=NPAIR)
            h2keys = [("h2", c) for c in range(KC)]
            si = 0
            stash_idx = {}
            for gi, segs in enumerate(groups):
                for (slo, shi, t0, G, which) in segs:
                    if t0 - 1 >= slo:
                        stash_idx[(gi, t0)] = si
                        norm_mod(T, layer, 3, 4, t0 - 2, 2, which,
                                 lambda c, si=si: stash[:, c, 2 * si:2 * si + 2],
                                 lambda c: [("stash", c)])
                        si += 1
            for gi, segs in enumerate(groups):
                lay = []
                hoff = goff = 0
                for (slo, shi, t0, G, which) in segs:
                    npiece = (G + 479) // 480
                    base = G // npiece
                    pieces = []
                    o = 0
                    for i in range(npiece):
                        n = base + (1 if i < G % npiece else 0)
                        pieces.append((o, n))
                        o += n
                    lay.append((slo, shi, t0, G, which, hoff, goff, pieces))
                    hoff += G + 2
                    goff += G
                assert hoff <= HW2 and goff <= GT
                for (slo, shi, t0, G, which, hoff, goff, pieces) in lay:
                    hi = min(t0 + G + 1, shi)
                    if t0 - 1 >= slo:
                        sidx = stash_idx[(gi, t0)]
                        for c in range(KC):
                            P.op("dve", lambda e, c=c, sidx=sidx, hoff=hoff: e.tensor_copy(
                                out=h2[:, c, hoff:hoff + 1], in_=stash[:, c, 2 * sidx + 1:2 * sidx + 2]),
                                reads=[("stash", c)], writes=[("h2", c)])
                    else:
                        P.op("dve", lambda e, hoff=hoff: e.memset(h2[:, :, hoff:hoff + 1], 0.0), writes=h2keys)
                    if hi < t0 + G + 1:
                        P.op("dve", lambda e, G=G, hoff=hoff: e.memset(h2[:, :, hoff + G + 1:hoff + G + 2], 0.0),
                             writes=h2keys)
                    for (b0, bn) in tok_blocks(t0, hi):
                        j0 = hoff + b0 - (t0 - 1)
                        norm_mod(T, layer, 3, 4, b0, bn, which,
                                 lambda c, j0=j0, bn=bn: h2[:, c, j0:j0 + bn],
                                 lambda c: [("h2", c)])
                it = 0
                for c in range(NPAIR):
                    s = wload(wffn_d[layer, c], KC * 256)
                    for (slo, shi, t0, G, which, hoff, goff, pieces) in lay:
                        for (o, n) in pieces:
                            bsel = it % 2
                            it += 1
                            pa, pg = 2 + bsel * 2, 3 + bsel * 2
                            ho = hoff + o
                            for half, pb in ((0, pa), (1, pg)):
                                for kc in range(KC):
                                    P.op("pe", lambda e, s=s, kc=kc, half=half, pb=pb, ho=ho, n=n: e.matmul(
                                        PS[pb][:, 0:n + 2],
                                        lhsT=WS[s][:, kc * 256 + half * 128: kc * 256 + half * 128 + 128],
                                        rhs=h2[:, kc, ho:ho + n + 2],
                                        start=(kc == 0), stop=(kc == KC - 1)),
                                        reads=[("WS", s), ("h2", kc)], writes=[("ps", pb)])
                            for half, pb in ((0, pa), (1, pg)):
                                cc = c + half * NPAIR
                                tb = ctmp[bsel][half]
                                P.op("act", lambda e, pb=pb, tb=tb, cc=cc, n=n: e.activation(
                                    out=tb[:, 0:n], in_=PS[pb][:, 1:n + 1], func=AF.Identity,
                                    scale=cw(cc, 1), bias=cb(cc)),
                                    reads=[("ps", pb), "CONST"], writes=[("ctmp", bsel, half)])
                                P.op("dve", lambda e, pb=pb, tb=tb, cc=cc, n=n: e.scalar_tensor_tensor(
                                    out=tb[:, 0:n], in0=PS[pb][:, 0:n], scalar=cw(cc, 0), in1=tb[:, 0:n],
                                    op0=ALU.mult, op1=ALU.add),
                                    reads=[("ps", pb), "CONST", ("ctmp", bsel, half)], writes=[("ctmp", bsel, half)])
                                P.op("dve", lambda e, pb=pb, tb=tb, cc=cc, n=n: e.scalar_tensor_tensor(
                                    out=tb[:, 0:n], in0=PS[pb][:, 2:n + 2], scalar=cw(cc, 2), in1=tb[:, 0:n],
                                    op0=ALU.mult, op1=ALU.add),
                                    reads=[("ps", pb), "CONST", ("ctmp", bsel, half)], writes=[("ctmp", bsel, half)])
                            P.op("act", lambda e, bsel=bsel, n=n: e.activation(
                                out=ctmp[bsel][1][:, 0:n], in_=ctmp[bsel][1][:, 0:n], func=AF.Silu),
                                reads=[("ctmp", bsel, 1)], writes=[("ctmp", bsel, 1)])
                            go = goff + o
                            P.op("dve", lambda e, bsel=bsel, n=n, c=c, go=go: e.tensor_tensor(
                                out=gbuf[:, c, go:go + n], in0=ctmp[bsel][0][:, 0:n], in1=ctmp[bsel][1][:, 0:n],
                                op=ALU.mult),
                                reads=[("ctmp", bsel, 0), ("ctmp", bsel, 1)], writes=[("g", c)])
                it = 0
                for m in range(KC):
                    s = wload(wdn_d[layer, m], NPAIR * 128)
                    for (slo, shi, t0, G, which, hoff, goff, pieces) in lay:
                        for (b0, bn) in tok_blocks(0, G):
                            pb = 2 + (it % 4)
                            it += 1
                            gb = goff + b0
                            for c in range(NPAIR):
                                P.op("pe", lambda e, s=s, c=c, pb=pb, gb=gb, bn=bn: e.matmul(
                                    PS[pb][:, 0:bn], lhsT=WS[s][:, c * 128:(c + 1) * 128],
                                    rhs=gbuf[:, c, gb:gb + bn], start=(c == 0), stop=(c == NPAIR - 1)),
                                    reads=[("WS", s), ("g", c)], writes=[("ps", pb)])
                            tt = t0 + b0
                            P.op("dve", lambda e, pb=pb, m=m, tt=tt, bn=bn, which=which: e.scalar_tensor_tensor(
                                out=xT[:, m, tt:tt + bn], in0=PS[pb][:, 0:bn], scalar=modv(layer, 5, m, which),
                                in1=xT[:, m, tt:tt + bn], op0=ALU.mult, op1=ALU.add),
                                reads=[("ps", pb), "MOD"] + xk(m, tt, bn), writes=xk(m, tt, bn))

        def hT_view():
            return rv("HT", 0, KC * NT, BF16,
                      keys=[(("hT", c), c * NT, NT) for c in range(KC)]).rearrange("p (k t) -> p k t", k=KC)

        ALLBLK = [(0, NCTX, 1)] + [(b0, bn, 0) for (b0, bn) in tok_blocks(NCTX, NT)]

        def compute_hT(layer):
            hT = hT_view()
            T = nrm_temps("SCR", SCR_BYTES - 10240, "S")
            for (b0, bn, which) in ALLBLK:
                norm_mod(T, layer, 0, 1, b0, bn, which,
                         lambda c, b0=b0, bn=bn: hT[:, c, b0:b0 + bn], lambda c: [("hT", c)])
            return hT

        hkeys = [("hT", c) for c in range(KC)]
        psrr = {"i": 0}

        def nextps(lo=0, hi=8):
            i = lo + psrr["i"] % (hi - lo)
            psrr["i"] += 1
            return i

        def resid_add(layer, ysrc_fn, ykeys, s, wcol_fn, nk, jgate, blocks=None):
            for m in range(KC):
                for (b0, bn, which) in (blocks or ALLBLK):
                    pb = nextps()
                    for k in range(nk):
                        P.op("pe", lambda e, m=m, k=k, pb=pb, b0=b0, bn=bn: e.matmul(
                            PS[pb][:, 0:bn], lhsT=wcol_fn(k, m), rhs=ysrc_fn(k, b0, bn),
                            start=(k == 0), stop=(k == nk - 1)),
                            reads=[("WS", s)] + ykeys(k), writes=[("ps", pb)])
                    P.op("dve", lambda e, m=m, pb=pb, b0=b0, bn=bn, which=which: e.scalar_tensor_tensor(
                        out=xT[:, m, b0:b0 + bn], in0=PS[pb][:, 0:bn], scalar=modv(layer, jgate, m, which),
                        in1=xT[:, m, b0:b0 + bn], op0=ALU.mult, op1=ALU.add),
                        reads=[("ps", pb), "MOD"] + xk(m, b0, bn), writes=xk(m, b0, bn))

        def mixer0():
            layer = 0
            hT = compute_hT(layer)
            WP = 2336

            def col(t):
                return t + 8 if t < NCTX else t + 24
            U = rv("SCR", 0, WP, F32, key="U")
            T1 = rv("SCR", 9344, WP, F32, key="T1")
            dbuf = rv("SCR", 18688, WP, BF16, key="dbuf")
            ypool = rv("SCR", 23360, 4 * NT, BF16,
                       keys=[(("ypool", g), g * NT, NT) for g in range(4)]).rearrange("p (g t) -> p g t", g=4)
            T2 = rv("ROPE", 0, WP, F32, key="T2")
            s_u = wload(wpool_d, KC * 512)
            for (a, b) in ((0, 8), (264, 280), (2328, 2336)):
                P.op("dve", lambda e, a=a, b=b: e.memset(U[:, a:b], 0.0), writes=["U"])
            tmpE = MISC[:, 8:16]
            for g, w in enumerate((2, 4, 8, 16)):
                for (b0, bn, which) in ALLBLK:
                    pb = nextps()
                    for kc in range(KC):
                        P.op("pe", lambda e, kc=kc, pb=pb, b0=b0, bn=bn, g=g: e.matmul(
                            PS[pb][:, 0:bn], lhsT=WS[s_u][:, kc * 512 + g * 128: kc * 512 + g * 128 + 128],
                            rhs=hT[:, kc, b0:b0 + bn], start=(kc == 0), stop=(kc == KC - 1)),
                            reads=[("WS", s_u), ("hT", kc)], writes=[("ps", pb)])
                    P.op("act", lambda e, pb=pb, b0=b0, bn=bn: e.activation(
                        out=U[:, col(b0):col(b0) + bn], in_=PS[pb][:, 0:bn], func=AF.Identity),
                        reads=[("ps", pb)], writes=["U"])
                P.op("dve", lambda e: e.tensor_tensor(out=T1[:, 1:WP], in0=U[:, 0:WP - 1], in1=U[:, 1:WP], op=ALU.add),
                     reads=["U"], writes=["T1"])
                A, Akey = T1, "T1"
                if w >= 4:
                    P.op("dve", lambda e: e.tensor_tensor(out=T2[:, 2:WP - 1], in0=T1[:, 1:WP - 2], in1=T1[:, 3:WP],
                                                          op=ALU.add), reads=["T1"], writes=["T2"])
                    A, Akey = T2, "T2"
                if w >= 8:
                    P.op("dve", lambda e: e.tensor_tensor(out=T1[:, 4:WP - 3], in0=T2[:, 2:WP - 5], in1=T2[:, 6:WP - 1],
                                                          op=ALU.add), reads=["T2"], writes=["T1"])
                    A, Akey = T1, "T1"
                if w >= 16:
                    P.op("dve", lambda e: e.tensor_tensor(out=T2[:, 8:WP - 7], in0=T1[:, 4:WP - 11], in1=T1[:, 12:WP - 3],
                                                          op=ALU.add), reads=["T1"], writes=["T2"])
                    A, Akey = T2, "T2"
                P.op("dve", lambda e, A=A, w=w: e.scalar_tensor_tensor(
                    out=dbuf[:, 8:2328], in0=A[:, 8:2328], scalar=1.0 / w, in1=U[:, 8:2328],
                    op0=ALU.mult, op1=ALU.subtract), reads=[Akey, "U"], writes=["dbuf"])
                hw = w // 2
                for ei, c0 in enumerate((col(0), col(NCTX - 1) + 1 - hw, col(NCTX), col(NT - 1) + 1 - hw)):
                    tbl = CONST[:, CS("pedge", (g * 4 + ei) * 8, hw)]
                    P.op("dve", lambda e, A=A, c0=c0, tbl=tbl, hw=hw: e.tensor_tensor(
                        out=tmpE[:, 0:hw], in0=A[:, c0:c0 + hw], in1=tbl, op=ALU.mult),
                        reads=[Akey, "CONST"], writes=["tmpE"])
                    P.op("dve", lambda e, c0=c0, hw=hw: e.tensor_tensor(
                        out=dbuf[:, c0:c0 + hw], in0=tmpE[:, 0:hw], in1=U[:, c0:c0 + hw], op=ALU.subtract),
                        reads=["tmpE", "U", "dbuf"], writes=["dbuf"])
                for (b0, bn, which) in ALLBLK:
                    pb = nextps()
                    P.op("pe", lambda e, pb=pb, b0=b0, bn=bn, g=g: e.matmul(
                        PS[pb][:, 0:bn], lhsT=CB[:, 256 + g * 128: 256 + (g + 1) * 128],
                        rhs=dbuf[:, col(b0):col(b0) + bn], start=True, stop=True),
                        reads=["CB", "dbuf"], writes=[("ps", pb)])
                    P.op("act", lambda e, pb=pb, b0=b0, bn=bn, g=g: e.activation(
                        out=ypool[:, g, b0:b0 + bn], in_=PS[pb][:, 0:bn], func=AF.Identity,
                        scale=CONST[:, CS("pscale", g, 1)]),
                        reads=[("ps", pb), "CONST"], writes=[("ypool", g)])
            s_w = state["slot"] % NSLOT
            state["slot"] += 1
            P.dma("pool", WS[s_w][:, 0:4096].rearrange("p (k n) -> p k n", k=4),
                  wout0_d[4:8].rearrange("k p n -> p k n"), writes=[("WS", s_w)])
            resid_add(layer, lambda k, b0, bn: ypool[:, k, b0:b0 + bn], lambda k: [("ypool", k)], s_w,
                      lambda k, m: WS[s_w][:, k * 1024 + m * 128: k * 1024 + m * 128 + 128], 4, 2)

            ropeT = rv("ROPE", 0, 2 * NLAT, F32, key="rope")
            P.dma("sp", ropeT, rope_d, writes=["rope"])
            CTt, STt = ropeT[:, 0:NLAT], ropeT[:, NLAT:2 * NLAT]
            P.op("dve", lambda e: e.tensor_tensor(out=MISC[:, 16:80], in0=CONST[:, CS("lqk", 0, 64)],
                                                  in1=CONST[:, CS("lqk", 64, 64)], op=ALU.mult),
                 reads=["CONST"], writes=["lam_p1"])
            P.op("dve", lambda e: e.tensor_tensor(out=MISC[:, 80:144], in0=CONST[:, CS("lqk", 128, 64)],
                                                  in1=CONST[:, CS("lqk", 192, 64)], op=ALU.mult),
                 reads=["CONST"], writes=["lam_p2"])
            P.op("dve", lambda e: e.reduce_sum(out=MISC[:, 1:2], in_=MISC[:, 16:80], axis=mybir.AxisListType.X),
                 reads=["lam_p1"], writes=["lam_s1"])
            P.op("dve", lambda e: e.reduce_sum(out=MISC[:, 2:3], in_=MISC[:, 80:144], axis=mybir.AxisListType.X),
                 reads=["lam_p2"], writes=["lam_s2"])
            P.op("act", lambda e: e.activation(out=MISC[:, 1:3], in_=MISC[:, 1:3], func=AF.Exp),
                 reads=["lam_s1", "lam_s2"], writes=["lam_e"])
            P.op("dve", lambda e: e.tensor_tensor(out=MISC[:, 3:4], in0=MISC[:, 2:3], in1=MISC[:, 1:2],
                                                  op=ALU.subtract), reads=["lam_e"], writes=["neglam0"])
            lambda_init = 0.8 - 0.6 * math.exp(-0.3 * 0)
            P.op("dve", lambda e: e.tensor_scalar_add(out=MISC[:, 3:4], in0=MISC[:, 3:4], scalar1=-lambda_init),
                 reads=["neglam0"], writes=["neglam"])
            P.op("dve", lambda e: e.tensor_scalar_mul(out=MISC[:, 4:8], in0=CONST[:, CS("subln")],
                                                      scalar1=1.0 - lambda_init),
                 reads=["CONST"], writes=["G08"])
            neglam = MISC[:, 3:4]

            qT = rv("SCR", 0, NT, BF16, key="qT")
            kT = rv("SCR", 4608, NT, BF16, key="kT")
            Vh = rv("SCR", 9216, 18 * 128, BF16, key="Vh").rearrange("p (i d) -> p i d", i=18)
            yh = rv("SCR", 13824, NT, BF16, key="yh")
            E = [[rv("SCR", 18432 + (m * 2 + b) * 1024, 512, BF16, key=("E", m, b)) for b in range(2)]
                 for m in range(2)]
            rt = [rv("SCR", 22528 + i * 2048, 512, F32, key=("rt", i)) for i in range(2)]
            r0 = rv("SCR", 26624, 512, F32, key="r0")
            r1 = rv("SCR", 28672, 512, F32, key="r1")
            t0b = rv("SCR", 30720, 512, F32, key="t0b")
            t1b = rv("SCR", 32768, 512, F32, key="t1b")
            sqh = rv("SCR", 34816, 512, BF16, key="sqh")
            rsh = rv("SCR", 35840, 512, F32, key="rsh")
            rrh = rv("SCR", 37888, 512, F32, key="rrh")
            for h in range(4):
                s = wload(whead_d[h], KC * 640)
                W = lambda kc, c0: WS[s][:, kc * 640 + c0: kc * 640 + c0 + 128]
                for (dst, dkey, c0) in ((qT, "qT", 0), (kT, "kT", 256)):
                    for (b0, bn, which) in ALLBLK:
                        pa = nextps()
                        for kc in range(KC):
                            P.op("pe", lambda e, kc=kc, pa=pa, b0=b0, bn=bn, c0=c0: e.matmul(
                                PS[pa][:, 0:bn], lhsT=W(kc, c0), rhs=hT[:, kc, b0:b0 + bn],
                                start=(kc == 0), stop=(kc == KC - 1)),
                                reads=[("WS", s), ("hT", kc)], writes=[("ps", pa)])
                        if which == 1:
                            P.op("act", lambda e, pa=pa, b0=b0, bn=bn, dst=dst: e.activation(
                                out=dst[:, b0:b0 + bn], in_=PS[pa][:, 0:bn], func=AF.Identity),
                                reads=[("ps", pa)], writes=[dkey])
                            continue
                        pb = nextps()
                        for kc in range(KC):
                            P.op("pe", lambda e, kc=kc, pb=pb, b0=b0, bn=bn, c0=c0: e.matmul(
                                PS[pb][:, 0:bn], lhsT=W(kc, c0 + 128), rhs=hT[:, kc, b0:b0 + bn],
                                start=(kc == 0), stop=(kc == KC - 1)),
                                reads=[("WS", s), ("hT", kc)], writes=[("ps", pb)])
                        l0 = b0 - NCTX
                        P.op("dve", lambda e, pa=pa, bn=bn, l0=l0: e.tensor_tensor(
                            out=rt[0][:, 0:bn], in0=PS[pa][:, 0:bn], in1=CTt[:, l0:l0 + bn], op=ALU.mult),
                            reads=[("ps", pa), "rope"], writes=[("rt", 0)])
                        P.op("dve", lambda e, pb=pb, bn=bn, l0=l0: e.tensor_tensor(
                            out=rt[1][:, 0:bn], in0=PS[pb][:, 0:bn], in1=STt[:, l0:l0 + bn], op=ALU.mult),
                            reads=[("ps", pb), "rope"], writes=[("rt", 1)])
                        P.op("dve", lambda e, b0=b0, bn=bn, dst=dst: e.tensor_tensor(
                            out=dst[:, b0:b0 + bn], in0=rt[0][:, 0:bn], in1=rt[1][:, 0:bn], op=ALU.add),
                            reads=[("rt", 0), ("rt", 1)], writes=[dkey])
                for i4 in range(0, 18, 4):
                    nt_ = min(4, 18 - i4)
                    pb = nextps()
                    for j in range(nt_):
                        i = i4 + j
                        for kc in range(KC):
                            P.op("pe", lambda e, kc=kc, pb=pb, i=i, j=j: e.matmul(
                                PS[pb][:, j * 128:(j + 1) * 128], lhsT=hT[:, kc, i * 128:(i + 1) * 128],
                                rhs=W(kc, 512), start=(kc == 0), stop=(kc == KC - 1)),
                                reads=[("WS", s), ("hT", kc)], writes=[("ps", pb)])
                    P.op("act", lambda e, pb=pb, i4=i4, nt_=nt_: e.activation(
                        out=Vh[:, i4:i4 + nt_, :], in_=PS[pb][:, 0:nt_ * 128].rearrange("p (i d) -> p i d", i=nt_),
                        func=AF.Identity), reads=[("ps", pb)], writes=["Vh"])
                for (q0, n, which) in ALLBLK:
                    nkt = 2 if which == 1 else 18

                    def qk(i):
                        b = i % 2
                        for m in range(2):
                            P.op("pe", lambda e, m=m, b=b, i=i: e.matmul(
                                PS[4 + m * 2 + b][:, 0:n], lhsT=kT[64 * m:64 * m + 64, i * 128:(i + 1) * 128],
                                rhs=qT[64 * m:64 * m + 64, q0:q0 + n], start=True, stop=True),
                                reads=["kT", "qT"], writes=[("ps", 4 + m * 2 + b)])
                    qk(0)
                    for i in range(nkt):
                        b = i % 2
                        if i + 1 < nkt:
                            qk(i + 1)
                        for m in range(2):
                            P.op("act", lambda e, m=m, b=b, n=n: e.activation(
                                out=E[m][b][:, 0:n], in_=PS[4 + m * 2 + b][:, 0:n], func=AF.Exp, scale=0.125),
                                reads=[("ps", 4 + m * 2 + b)], writes=[("E", m, b)])
                        for m in range(2):
                            P.op("pe", lambda e, m=m, b=b, i=i, n=n: e.matmul(
                                PS[m][:, 0:n], lhsT=Vh[:, i, :], rhs=E[m][b][:, 0:n],
                                start=(i == 0), stop=(i == nkt - 1)),
                                reads=["Vh", ("E", m, b)], writes=[("ps", m)])
                            P.op("pe", lambda e, m=m, b=b, i=i, n=n: e.matmul(
                                PS[2 + m][:, 0:n], lhsT=ones, rhs=E[m][b][:, 0:n],
                                start=(i == 0), stop=(i == nkt - 1)),
                                reads=["CB", ("E", m, b)], writes=[("ps", 2 + m)])
                    P.op("act", lambda e, n=n: e.activation(out=r0[:, 0:n], in_=PS[2][:, 0:n], func=AF.Ln),
                         reads=[("ps", 2)], writes=["r0"])
                    P.op("act", lambda e, n=n: e.activation(out=r1[:, 0:n], in_=PS[3][:, 0:n], func=AF.Ln),
                         reads=[("ps", 3)], writes=["r1"])
                    P.op("act", lambda e, n=n: e.activation(out=r0[:, 0:n], in_=r0[:, 0:n], func=AF.Exp, scale=-1.0),
                         reads=["r0"], writes=["r0"])
                    P.op("act", lambda e, n=n: e.activation(out=r1[:, 0:n], in_=r1[:, 0:n], func=AF.Exp, scale=-1.0),
                         reads=["r1"], writes=["r1"])
                    P.op("dve", lambda e, n=n: e.tensor_tensor(out=t0b[:, 0:n], in0=PS[0][:, 0:n], in1=r0[:, 0:n],
                                                               op=ALU.mult),
                         reads=[("ps", 0), "r0"], writes=["t0b"])
                    P.op("dve", lambda e, n=n: e.tensor_tensor(out=t1b[:, 0:n], in0=PS[1][:, 0:n], in1=r1[:, 0:n],
                                                               op=ALU.mult),
                         reads=[("ps", 1), "r1"], writes=["t1b"])
                    P.op("dve", lambda e, n=n: e.scalar_tensor_tensor(
                        out=t0b[:, 0:n], in0=t1b[:, 0:n], scalar=neglam, in1=t0b[:, 0:n],
                        op0=ALU.mult, op1=ALU.add), reads=["t1b", "t0b", "neglam"], writes=["t0b"])
                    P.op("act", lambda e, n=n: e.activation(out=sqh[:, 0:n], in_=t0b[:, 0:n], func=AF.Square),
                         reads=["t0b"], writes=["sqh"])
                    P.op("pe", lambda e, n=n: e.matmul(PS[2][:, 0:n], lhsT=ones, rhs=sqh[:, 0:n], start=True, stop=True),
                         reads=["CB", "sqh"], writes=[("ps", 2)])
                    P.op("act", lambda e, n=n: e.activation(out=rsh[:, 0:n], in_=PS[2][:, 0:n], func=AF.Ln,
                                                            scale=1.0 / 128.0, bias=EPSB),
                         reads=[("ps", 2), "EPSB"], writes=["rsh"])
                    P.op("act", lambda e, n=n: e.activation(out=rrh[:, 0:n], in_=rsh[:, 0:n], func=AF.Exp, scale=-0.5),
                         reads=["rsh"], writes=["rrh"])
                    P.op("dve", lambda e, n=n, q0=q0, h=h: e.scalar_tensor_tensor(
                        out=yh[:, q0:q0 + n], in0=t0b[:, 0:n], scalar=MISC[:, 4 + h:5 + h], in1=rrh[:, 0:n],
                        op0=ALU.mult, op1=ALU.mult), reads=["t0b", "G08", "rrh"], writes=["yh"])
                s_w = wload(wout0_d[h], D)
                resid_add(layer, lambda k, b0, bn: yh[:, b0:b0 + bn], lambda k: ["yh"], s_w,
                          lambda k, m, s_w=s_w: WS[s_w][:, m * 128:(m + 1) * 128], 1, 2)

        HS = sb("HG_S", [128, 2 * 128], F32)
        HSm = sb("HG_Sm", [128, 2 * 128], BF16)
        ATm = sb("HG_ATm", [128, 2 * 64], BF16)
        HSC = sb("HG_SC", [128, 6 * 36 + 16 + 32], F32)
        LATBLK = [(b0, bn, 0) for (b0, bn) in tok_blocks(NCTX, NT)]

        def mixer1():
            stages = stop.split("+")
            layer = 1
            hT = compute_hT(layer)
            LB = HSC[:, 232:248]
            OML = HSC[:, 248:264]
            lbl = CONST[:, CS("lblog")].rearrange("p (d l h) -> p d l h", d=2, l=2)
            LB3 = LB.rearrange("p (d h) -> p d h", d=2)
            P.op("dve", lambda e: e.tensor_tensor(out=LB3, in0=lbl[:, :, 1, :], in1=lbl[:, :, 0, :], op=ALU.subtract),
                 reads=["CONST"], writes=["LB"])
            P.op("act", lambda e: e.activation(out=LB, in_=LB, func=AF.Sigmoid), reads=["LB"], writes=["LB"])
            P.op("dve", lambda e: e.tensor_scalar(out=OML, in0=LB, scalar1=-1.0, scalar2=1.0, op0=ALU.mult,
                                                  op1=ALU.add), reads=["LB"], writes=["OML"])

            Qd = [rv("SCR", d * 4096, NLAT, BF16, key=("Qd", d)) for d in range(2)]
            Kd = [rv("SCR", 8192 + d * 4608, NT, BF16, key=("Kd", d)) for d in range(2)]
            Ktok = [rv("SCR", 17408 + d * 4608, 18 * 128, BF16, key=("Ktok", d)).rearrange("p (i c) -> p i c", i=18)
                    for d in range(2)]
            Vh = rv("SCR", 26624, 18 * 128, BF16, key="Vh1").rearrange("p (i d) -> p i d", i=18)
            TO = 31232
            OF = rv("ROPE", 0, NLAT, F32, keys=[(("OF", b), b * 512, 512) for b in range(4)])
            SG = rv("ROPE", 8192, NLAT, BF16, key="SG")
            YH = rv("ROPE", 12288, NLAT, BF16, key="YH")
            AL = [HSC[:, d * 108 + 0: d * 108 + 36] for d in range(2)]
            BE = [HSC[:, d * 108 + 36: d * 108 + 72] for d in range(2)]
            GA = [HSC[:, d * 108 + 72: d * 108 + 108] for d in range(2)]
            RP = HSC[:, 216:224]
            PIECES = tok_blocks(0, NT)
            for h in range(8):
                Qp = rv("SCR", TO, 512, F32, key="Qp")
                Fb = [rv("SCR", TO + 2048 + d * 2048, 512, F32, key=("Fb", d)) for d in range(2)]
                Kk = rv("SCR", TO + 6144, 512, F32, key="Kk")
                Bc = rv("SCR", TO + 8192, 512, F32, key="Bc")
                Gl = rv("SCR", TO + 10240, 512, F32, key="Gl")
                s = wload(whg_d[h], KC * 640)
                W = lambda kc, c0: WS[s][:, kc * 640 + c0: kc * 640 + c0 + 128]

                def proj(c0, p0, n, pb):
                    for kc in range(KC):
                        P.op("pe", lambda e, kc=kc: e.matmul(
                            PS[pb][:, 0:n], lhsT=W(kc, c0), rhs=hT[:, kc, p0:p0 + n],
                            start=(kc == 0), stop=(kc == KC - 1)),
                            reads=[("WS", s), ("hT", kc)], writes=[("ps", pb)])

                for i4 in range(0, 18, 4):
                    nt_ = min(4, 18 - i4)
                    pb = nextps()
                    for j in range(nt_):
                        i = i4 + j
                        for kc in range(KC):
                            P.op("pe", lambda e, kc=kc, pb=pb, i=i, j=j: e.matmul(
                                PS[pb][:, j * 128:(j + 1) * 128], lhsT=hT[:, kc, i * 128:(i + 1) * 128],
                                rhs=W(kc, 128), start=(kc == 0), stop=(kc == KC - 1)),
                                reads=[("WS", s), ("hT", kc)], writes=[("ps", pb)])
                    P.op("act", lambda e, pb=pb, i4=i4, nt_=nt_: e.activation(
                        out=Vh[:, i4:i4 + nt_, :], in_=PS[pb][:, 0:nt_ * 128].rearrange("p (i d) -> p i d", i=nt_),
                        func=AF.Identity), reads=[("ps", pb)], writes=["Vh1"])
                for (b0, bn, _) in LATBLK:
                    pb = nextps()
                    proj(512, b0, bn, pb)
                    P.op("act", lambda e, pb=pb, b0=b0, bn=bn: e.activation(
                        out=SG[:, b0 - NCTX:b0 - NCTX + bn], in_=PS[pb][:, 0:bn], func=AF.Silu),
                        reads=[("ps", pb)], writes=["SG"])
                for (p0, n) in PIECES:
                    nch = n // 64
                    c0 = p0 // 64
                    l0 = max(p0, NCTX)
                    ln = p0 + n - l0
                    lo = l0 - p0
                    v3 = lambda t, n=n: t[:, 0:n].rearrange("p (c t) -> p c t", t=64)
                    sc = lambda t, nch=nch, c0=c0: t[:, c0:c0 + nch].rearrange("p (c o) -> p c o", o=1)
                    pb = nextps()
                    proj(0, p0, n, pb)
                    P.op("act", lambda e, pb=pb, n=n: e.activation(out=Qp[:, 0:n], in_=PS[pb][:, 0:n], func=AF.Silu),
                         reads=[("ps", pb)], writes=["Qp"])
                    for d in range(2):
                        pb = nextps()
                        proj(256 + d * 128, p0, n, pb)
                        P.op("act", lambda e, pb=pb, n=n, d=d: e.activation(out=Fb[d][:, 0:n], in_=PS[pb][:, 0:n],
                                                                            func=AF.Sigmoid),
                             reads=[("ps", pb)], writes=[("Fb", d)])
                    for d in range(2):
                        F = Fb[d]
                        P.op("dve", lambda e, n=n, d=d, h=h, F=F: e.tensor_scalar(
                            out=F[:, 0:n], in0=F[:, 0:n], scalar1=OML[:, d * 8 + h:d * 8 + h + 1],
                            scalar2=LB[:, d * 8 + h:d * 8 + h + 1], op0=ALU.mult, op1=ALU.add),
                            reads=[("Fb", d), "OML", "LB"], writes=[("Fb", d)])
                        P.op("dve", lambda e, n=n, F=F: e.tensor_scalar(
                            out=Kk[:, 0:n], in0=F[:, 0:n], scalar1=-1.0, scalar2=1.0, op0=ALU.mult, op1=ALU.add),
                            reads=[("Fb", d)], writes=["Kk"])
                        P.op("act", lambda e, n=n, F=F: e.activation(out=F[:, 0:n], in_=F[:, 0:n], func=AF.Ln),
                             reads=[("Fb", d)], writes=[("Fb", d)])
                        P.op("dve", lambda e, n=n, F=F: e.tensor_tensor_scan(
                            out=Bc[:, 0:n], data0=CONST[:, CS("m01", 0, n)], data1=F[:, 0:n], initial=0.0,
                            op0=ALU.mult, op1=ALU.add), reads=[("Fb", d), "CONST"], writes=["Bc"])
                        B3 = v3(Bc)
                        G3 = v3(Gl)
                        P.op("dve", lambda e, nch=nch, B3=B3, G3=G3: e.tensor_tensor(
                            out=G3, in0=B3, in1=B3[:, :, 31:32].to_broadcast([128, nch, 64]), op=ALU.subtract),
                            reads=["Bc"], writes=["Gl"])
                        P.op("dve", lambda e, d=d, B3=B3: e.tensor_copy(out=sc(AL[d]), in_=B3[:, :, 63:64]),
                             reads=["Bc"], writes=[("AL", d)])
                        bsrc = G3[:, :, 63:64] if d == 0 else B3[:, :, 31:32]
                        gsrc = B3[:, :, 31:32] if d == 0 else G3[:, :, 63:64]
                        P.op("dve", lambda e, d=d, bsrc=bsrc: e.tensor_copy(out=sc(BE[d]), in_=bsrc),
                             reads=["Bc", "Gl"], writes=[("BE", d)])
                        P.op("dve", lambda e, d=d, gsrc=gsrc: e.tensor_copy(out=sc(GA[d]), in_=gsrc),
                             reads=["Bc", "Gl"], writes=[("GA", d)])
                        if d == 0:
                            P.op("act", lambda e, n=n: e.activation(out=Bc[:, 0:n], in_=Gl[:, 0:n], func=AF.Exp),
                                 reads=["Gl", "Bc"], writes=["Bc"])
                            P.op("act", lambda e, n=n, F=F: e.activation(out=F[:, 0:n], in_=Gl[:, 0:n], func=AF.Exp,
                                                                         scale=-1.0),
                                 reads=["Gl", ("Fb", d)], writes=[("Fb", d)])
                            if ln > 0:
                                P.op("dve", lambda e, lo=lo, ln=ln, l0=l0: e.tensor_tensor(
                                    out=Qd[0][:, l0 - NCTX:l0 - NCTX + ln], in0=Qp[:, lo:lo + ln],
                                    in1=Bc[:, lo:lo + ln], op=ALU.mult), reads=["Qp", "Bc"], writes=[("Qd", 0)])
                            P.op("dve", lambda e, n=n, p0=p0, F=F: e.tensor_tensor(
                                out=Kd[0][:, p0:p0 + n], in0=Kk[:, 0:n], in1=F[:, 0:n], op=ALU.mult),
                                reads=["Kk", ("Fb", d)], writes=[("Kd", 0)])
                        else:
                            F3 = v3(F)
                            P.op("dve", lambda e, F3=F3, G3=G3: e.tensor_copy(out=F3[:, :, 1:64], in_=G3[:, :, 0:63]),
                                 reads=["Gl", ("Fb", d)], writes=[("Fb", d)])
                            P.op("dve", lambda e, F3=F3, B3=B3: e.tensor_scalar(
                                out=F3[:, :, 0:1], in0=B3[:, :, 31:32], scalar1=-1.0, scalar2=None, op0=ALU.mult),
                                reads=["Bc", ("Fb", d)], writes=[("Fb", d)])
                            P.op("act", lambda e, n=n, F=F: e.activation(out=Bc[:, 0:n], in_=F[:, 0:n], func=AF.Exp),
                                 reads=[("Fb", d), "Bc"], writes=["Bc"])
                            P.op("dve", lambda e, n=n, p0=p0: e.tensor_tensor(
                                out=Kd[1][:, p0:p0 + n], in0=Kk[:, 0:n], in1=Bc[:, 0:n], op=ALU.mult),
                                reads=["Kk", "Bc"], writes=[("Kd", 1)])
                            if ln > 0:
                                P.op("act", lambda e, lo=lo, ln=ln, F=F: e.activation(
                                    out=Gl[:, lo:lo + ln], in_=F[:, lo:lo + ln], func=AF.Exp, scale=-1.0),
                                    reads=[("Fb", d), "Gl"], writes=["Gl"])
                                P.op("dve", lambda e, lo=lo, ln=ln, l0=l0: e.tensor_tensor(
                                    out=Qd[1][:, l0 - NCTX:l0 - NCTX + ln], in0=Qp[:, lo:lo + ln],
                                    in1=Gl[:, lo:lo + ln], op=ALU.mult), reads=["Qp", "Gl"], writes=[("Qd", 1)])
                for d in range(2):
                    P.op("act", lambda e, d=d: e.activation(out=HSC[:, d * 108:(d + 1) * 108],
                                                            in_=HSC[:, d * 108:(d + 1) * 108], func=AF.Exp),
                         reads=[("AL", d), ("BE", d), ("GA", d)], writes=[("AL", d), ("BE", d), ("GA", d)])
                for d in range(2):
                    for i4 in range(0, 18, 4):
                        nt_ = min(4, 18 - i4)
                        pb = nextps()
                        psb = PS[pb][:, 0:256].bitcast(BF16)
                        for j in range(nt_):
                            i = i4 + j
                            P.op("pe", lambda e, d=d, i=i, j=j, psb=psb: e.transpose(
                                psb[:, j * 128:(j + 1) * 128], Kd[d][:, i * 128:(i + 1) * 128], ident),
                                reads=[("Kd", d), "CB"], writes=[("ps", pb)])
                        P.op("act", lambda e, d=d, i4=i4, nt_=nt_, psb=psb: e.activation(
                            out=Ktok[d][:, i4:i4 + nt_, :],
                            in_=psb[:, 0:nt_ * 128].rearrange("p (i c) -> p i c", i=nt_), func=AF.Identity),
                            reads=[("ps", pb)], writes=[("Ktok", d)])
                osum = rv("SCR", TO, 512, F32, key="osum")
                rsb = rv("SCR", TO + 2048, 512, F32, key="rsb")
                rrb = rv("SCR", TO + 4096, 512, F32, key="rrb")
                ytmp = rv("SCR", TO + 6144, 512, F32, key="ytmp")
                sqb = rv("SCR", TO + 8192, 512, BF16, key="sqb")
                order = [list(range(36)), [3, 2, 1, 0] + list(range(35, 3, -1))]
                maskc = [CONST[:, CS("maskf")], CONST[:, CS("maskb")]]
                obank = {}
                def chunk_ctx(step, d):
                    n = order[d][step]
                    i, half = n // 2, n % 2
                    c = dict(n=n, i=i, pbase=half * 64, t0=n * 64, is_lat=(n >= 4), first=(step == 0),
                             last=(step == 35), Sd=HS[:, d * 128:(d + 1) * 128], Smd=HSm[:, d * 128:(d + 1) * 128],
                             kvb=6 + d, atb=4 + d, d=d, step=step)
                    if c["is_lat"]:
                        lt = c["t0"] - NCTX
                        c.update(lt=lt, blk=lt // 512, ob=d * 2 + ((lt // 512) % 2), oc=lt % 512)
                    return c

                def partA2(c):
                    d, i, pbase, kvb = c["d"], c["i"], c["pbase"], c["kvb"]
                    if not c["last"]:
                        P.op("pe", lambda e: e.matmul(
                            PS[kvb][:, 0:128], lhsT=Ktok[d][pbase:pbase + 64, i, :],
                            rhs=Vh[pbase:pbase + 64, i, :], start=True, stop=True),
                            reads=[("Ktok", d), "Vh1"], writes=[("ps", kvb)])

                def partA1(c):
                    d, pbase = c["d"], c["pbase"]
                    if c["is_lat"]:
                        t0, lt, atb = c["t0"], c["lt"], c["atb"]
                        P.op("pe", lambda e: e.matmul(
                            PS[atb][pbase:pbase + 64, 0:64], lhsT=Kd[d][:, t0:t0 + 64],
                            rhs=Qd[d][:, lt:lt + 64], start=True, stop=True),
                            reads=[("Kd", d), ("Qd", d)], writes=[("ps", atb)])

                def partB(c):
                    if not c["is_lat"]:
                        return
                    d, pbase, atb = c["d"], c["pbase"], c["atb"]
                    P.op("dve", lambda e: e.tensor_tensor(
                        out=ATm[pbase:pbase + 64, d * 64:(d + 1) * 64], in0=PS[atb][pbase:pbase + 64, 0:64],
                        in1=maskc[d][pbase:pbase + 64, :], op=ALU.mult),
                        reads=[("ps", atb), "CONST"], writes=[("ATm", d)])

                def partC1(c):
                    if not c["is_lat"]:
                        return
                    d, i, pbase, lt, ob, oc, blk, Smd = (c["d"], c["i"], c["pbase"], c["lt"], c["ob"], c["oc"],
                                                       c["blk"], c["Smd"])
                    P.op("pe", lambda e: e.matmul(
                        PS[ob][:, oc:oc + 64], lhsT=Vh[pbase:pbase + 64, i, :],
                        rhs=ATm[pbase:pbase + 64, d * 64:(d + 1) * 64], start=True, stop=False),
                        reads=["Vh1", ("ATm", d)], writes=[("ps", ob)])

                def partC2(c):
                    if not c["is_lat"]:
                        return
                    d, i, pbase, lt, ob, oc, blk, Smd = (c["d"], c["i"], c["pbase"], c["lt"], c["ob"], c["oc"],
                                                       c["blk"], c["Smd"])
                    P.op("pe", lambda e: e.matmul(
                        PS[ob][:, oc:oc + 64], lhsT=Smd, rhs=Qd[d][:, lt:lt + 64], start=False, stop=True),
                        reads=[("Sm", d), ("Qd", d)], writes=[("ps", ob)])
                    done = (oc == 448) if d == 0 else (oc == 0)
                    if not done:
                        return
                    b0 = blk * 512
                    if blk not in obank:
                        obank[blk] = d
                        P.op("act", lambda e: e.activation(
                            out=OF[:, b0:b0 + 512], in_=PS[ob][:, 0:512], func=AF.Identity),
                            reads=[("ps", ob)], writes=[("OF", blk)])
                        return
                    P.op("dve", lambda e: e.tensor_tensor(
                        out=osum[:, :], in0=PS[ob][:, 0:512], in1=OF[:, b0:b0 + 512], op=ALU.add),
                        reads=[("ps", ob), ("OF", blk)], writes=["osum"])
                    P.op("act", lambda e: e.activation(out=sqb[:, :], in_=osum[:, :], func=AF.Square),
                         reads=["osum"], writes=["sqb"])
                    P.op("pe", lambda e: e.matmul(PS[ob][:, 0:512], lhsT=ones, rhs=sqb[:, :], start=True, stop=True),
                         reads=["CB", "sqb"], writes=[("ps", ob)])
                    P.op("act", lambda e: e.activation(
                        out=rsb[:, :], in_=PS[ob][:, 0:512], func=AF.Ln, scale=1.0 / 128.0, bias=EPSB),
                        reads=[("ps", ob), "EPSB"], writes=["rsb"])
                    P.op("act", lambda e: e.activation(out=rrb[:, :], in_=rsb[:, :], func=AF.Exp, scale=-0.5),
                         reads=["rsb"], writes=["rrb"])
                    P.op("dve", lambda e: e.scalar_tensor_tensor(
                        out=ytmp[:, :], in0=osum[:, :], scalar=CONST[:, CS("hgnorm", h, 1)],
                        in1=rrb[:, :], op0=ALU.mult, op1=ALU.mult),
                        reads=["osum", "CONST", "rrb"], writes=["ytmp"])
                    P.op("dve", lambda e: e.tensor_tensor(
                        out=YH[:, b0:b0 + 512], in0=ytmp[:, :], in1=SG[:, b0:b0 + 512], op=ALU.mult),
                        reads=["ytmp", "SG"], writes=["YH"])

                def partD(c):
                    if c["last"]:
                        return
                    d, n, kvb, Sd, Smd, step = c["d"], c["n"], c["kvb"], c["Sd"], c["Smd"], c["step"]
                    if c["first"]:
                        P.op("dve", lambda e: e.tensor_scalar(
                            out=Sd, in0=PS[kvb][:, 0:128], scalar1=BE[d][:, n:n + 1], scalar2=None, op0=ALU.mult),
                            reads=[("ps", kvb), ("BE", d)], writes=[("S", d)])
                    else:
                        P.op("dve", lambda e: e.tensor_scalar(
                            out=Sd, in0=Sd, scalar1=AL[d][:, n:n + 1], scalar2=None, op0=ALU.mult),
                            reads=[("S", d), ("AL", d)], writes=[("S", d)])
                        P.op("dve", lambda e: e.scalar_tensor_tensor(
                            out=Sd, in0=PS[kvb][:, 0:128], scalar=BE[d][:, n:n + 1], in1=Sd,
                            op0=ALU.mult, op1=ALU.add),
                            reads=[("ps", kvb), ("BE", d), ("S", d)], writes=[("S", d)])
                    nn = order[d][step + 1]
                    if nn >= 4:
                        P.op("act", lambda e: e.activation(
                            out=Smd, in_=Sd, func=AF.Identity, scale=GA[d][:, nn:nn + 1]),
                            reads=[("S", d), ("GA", d)], writes=[("Sm", d)])

                for step in range(36 if "nochunk" not in stages else 0):
                    cs = [chunk_ctx(step, d) for d in range(2)]
                    for part in (partA1, partB, partA2, partC1, partC2, partD):
                        for c in cs:
                            part(c)
                if "nochunk" in stages:
                    P.op("act", lambda e: e.activation(out=YH[:, :], in_=SG[:, :], func=AF.Identity),
                         reads=["SG"], writes=["YH"])
                s_w = wload(wout1_d[h], D)
                resid_add(layer, lambda k, b0, bn: YH[:, b0 - NCTX:b0 - NCTX + bn], lambda k: ["YH"], s_w,
                          lambda k, m, s_w=s_w: WS[s_w][:, m * 128:(m + 1) * 128], 1, 2, blocks=LATBLK)

        stages = stop.split("+")

        if "mix0" in stages:
            mixer0()
        if "ffn0" in stages:
            ffn(0, [[(0, NCTX, 0, NCTX, 1), (NCTX, NT, NCTX, 896, 0)],
                    [(NCTX, NT, NCTX + 896, 1152, 0)]])
        if "mix1" in stages:
            mixer1()
        if "ffn1" in stages:
            ffn(1, [[(NCTX, NT, NCTX, 1024, 0)],
                    [(NCTX, NT, NCTX + 1024, 1024, 0)]])

        if "rawout" in stages:
            for c in range(KC):
                P.out_tokens.append(P.dma("sp", out_d[:, c, :], xT[:, c, NCTX:NT], reads=xk(c, NCTX, NLAT)))
        else:
            OUTB = [rv("SCR", i * 2048, 512, F32, key=("outb", i)) for i in range(2)]
            T = nrm_temps("SCR", SCR_BYTES - 10240, "S")
            it = 0
            for (b0, bn) in tok_blocks(NCTX, NT):
                rms_rstd(T, lambda c, b0=b0, bn=bn: xT[:, c, b0:b0 + bn],
                         lambda c, b0=b0, bn=bn: xk(c, b0, bn), KC, bn, float(D), 1)
                for c in range(KC):
                    b = it % 2
                    it += 1
                    P.op("dve", lambda e, c=c, b=b, b0=b0, bn=bn: e.scalar_tensor_tensor(
                        out=OUTB[b][:, :bn], in0=xT[:, c, b0:b0 + bn], scalar=CONST[:, CS("fnorm", c, 1)],
                        in1=T["rstd"][:, :bn], op0=ALU.mult, op1=ALU.mult),
                        reads=xk(c, b0, bn) + ["CONST", ("rstd", "S")], writes=[("outb", b)])
                    P.out_tokens.append(P.dma("sp", out_d[:, c, b0 - NCTX:b0 - NCTX + bn], OUTB[b][:, :bn],
                                              reads=[("outb", b)]))
        P.finish("sp", P.out_tokens)
    return nc


def _fm(v):
    v = np.asarray(v, np.float32)
    n = v.shape[-1] // 128
    return np.moveaxis(v.reshape(v.shape[:-1] + (n, 128)), -1, 0)


def _wk(w):
    K, N = w.shape
    return np.ascontiguousarray(w.reshape(K // 128, 128, N).transpose(1, 0, 2).reshape(128, (K // 128) * N))


def _rope_tables():
    rows_n = NLAT // GRID_W
    row = np.repeat(np.arange(rows_n, dtype=np.float32), GRID_W)
    col = np.tile(np.arange(GRID_W, dtype=np.float32), rows_n)
    n_freq = 16
    inv = (np.float32(10000.0) ** (-np.arange(n_freq, dtype=np.float32) / n_freq)).astype(np.float32)
    ang = np.stack([row[:, None] * inv, col[:, None] * inv], axis=1).astype(np.float32)
    cos, sin = np.cos(ang).astype(np.float32), np.sin(ang).astype(np.float32)
    CT = np.zeros((128, NLAT), np.float32)
    ST = np.zeros((128, NLAT), np.float32)
    for p in range(128):
        q = p % 64
        axis, half, f = q // 32, (q % 32) // 16, q % 16
        CT[p] = cos[:, axis, f]
        ST[p] = sin[:, axis, f] * (-1.0 if half == 0 else 1.0)
    return np.concatenate([CT, ST], axis=1)


def _swap_cols(w):
    N = w.shape[1]
    idx = np.arange(N)
    blk = idx // 32
    r = idx % 32
    return w[:, blk * 32 + (r + 16) % 32]


def prepare_inputs(inp):
    f32 = np.float32
    g = {k: np.asarray(v, f32) for k, v in inp.items()}
    shared = {}
    wm = g["w_mod"]
    shared["wmod"] = np.stack([np.stack([_wk(wm[l][:, p * 512:(p + 1) * 512]) for p in range(12)]) for l in range(2)])
    shared["rope"] = _rope_tables()
    wi = g["ev_w_in"][0]
    qw, kw, vw, uw = wi[:, 0:512], wi[:, 512:1024], wi[:, 1024:1536], wi[:, 1536:2048]
    qs, ks = _swap_cols(qw), _swap_cols(kw)
    heads = []
    for h in range(4):
        sl = slice(h * 128, (h + 1) * 128)
        heads.append(_wk(np.concatenate([qw[:, sl], qs[:, sl], kw[:, sl], ks[:, sl], vw[:, sl]], axis=1)))
    shared["whead"] = np.stack(heads)
    shared["wpool"] = _wk(uw)
    shared["wout0"] = np.ascontiguousarray(g["ev_w_out"][0].reshape(KC, 128, D))
    wu = g["ffn_w_up"]
    shared["wffn"] = np.stack([np.stack([
        _wk(np.concatenate([wu[l][:, c * 128:(c + 1) * 128], wu[l][:, DFF + c * 128: DFF + (c + 1) * 128]], axis=1))
        for c in range(NPAIR)]) for l in range(2)])
    wd = g["ffn_w_down"]
    shared["wdn"] = np.stack([np.stack([
        np.ascontiguousarray(wd[l][:, m * 128:(m + 1) * 128].reshape(NPAIR, 128, 128).transpose(1, 0, 2)
                             .reshape(128, NPAIR * 128)) for m in range(KC)]) for l in range(2)])
    hw = g["hg_w_in"][0]
    hh = []
    for h in range(8):
        cols = [hw[:, part * 1024 + h * 128: part * 1024 + (h + 1) * 128] for part in range(5)]
        hh.append(_wk(np.concatenate(cols, axis=1)))
    shared["whg"] = np.stack(hh)
    shared["wout1"] = np.ascontiguousarray(g["hg_w_out"][0].reshape(KC, 128, D))
    cb = np.zeros((128, 256 + 512), f32)
    cb[:, 0:128] = np.eye(128, dtype=f32)
    cb[:, 128:256] = 1.0
    cb[:, 256:768] = g["pool_w"][0].transpose(1, 0, 2).reshape(128, 512)
    shared["constb"] = cb

    const = np.zeros((128, NCONST), f32)
    bm = g["b_mod"]
    bmT = np.stack([_fm(bm[l]) for l in range(2)], axis=1)
    const[:, CS("bmod")] = np.repeat(bmT.reshape(128, 96), 2, axis=1).reshape(128, 192)
    lq = np.concatenate([g["da_lq1"][0], g["da_lk1"][0], g["da_lq2"][0], g["da_lk2"][0]])
    const[:, CS("lqk")] = lq[None, :]
    const[:, CS("subln")] = _fm(g["da_subln"][0])
    const[:, CS("pscale")] = _fm(g["pool_scale"][0])
    const[:, CS("hgnorm")] = _fm(g["hg_norm"][0])
    const[:, CS("fnorm")] = _fm(g["final_norm"])
    cwT = _fm(g["ffn_conv_w"])
    const[:, CS("convw")] = cwT.transpose(0, 1, 3, 2).reshape(128, -1)
    const[:, CS("convb")] = _fm(g["ffn_conv_b"]).reshape(128, -1)
    const[:, CS("lblog")] = _fm(g["hg_lb_logits"]).reshape(128, -1)
    pe = np.zeros((4, 4, 8), f32)
    for gi, w in enumerate((2, 4, 8, 16)):
        hw_ = w // 2
        for ei, T in enumerate((NCTX, NCTX, NLAT, NLAT)):
            for i in range(hw_):
                t = i if ei % 2 == 0 else T - hw_ + i
                cnt = min(t + w - hw_, T) - max(t - hw_, 0)
                pe[gi, ei, i] = 1.0 / cnt
    const[:, CS("pedge")] = pe.reshape(1, -1)
    s_idx = np.arange(128)[:, None] % 64
    t_idx = np.arange(64)[None, :]
    const[:, CS("maskf")] = (t_idx >= s_idx).astype(f32)
    const[:, CS("maskb")] = (t_idx <= s_idx).astype(f32)
    const[:, CS("m01")] = (np.arange(512) % 64 != 0).astype(f32)[None, :]

    cctx = _fm(g["c_ctx"])
    in_maps = []
    for b in range(NCORES):
        cst = const.copy()
        cst[:, CS("cT", 0, 8)] = _fm(g["c"][b])
        cst[:, CS("cT", 8, 8)] = cctx
        tok = np.concatenate([g["ctx"][b], g["x"][b]], axis=0)
        xTh = np.ascontiguousarray(tok.T.reshape(KC, 128, NT).transpose(1, 0, 2))
        m = dict(shared)
        m["const"] = cst
        m["xT"] = xTh
        in_maps.append(m)
    return in_maps


_CACHE = {}


def run(inputs, stop="full", trace=False, cores=None):
    in_maps = prepare_inputs(inputs)
    if cores is not None:
        in_maps = [in_maps[b] for b in cores]
    if stop not in _CACHE:
        _CACHE[stop] = build_program(stop)
    nc = _CACHE[stop]
    res = run_bass_kernel_spmd(nc, in_maps, core_ids=list(range(len(in_maps))), trace=trace)
    outs = []
    for r in res.results:
        o = np.asarray(r["outT"], np.float32)
        outs.append(o.transpose(2, 1, 0).reshape(NLAT, D))
    return np.stack(outs), res


def kernel(**inputs):
    out, _ = run(inputs, stop="mix0+ffn0+mix1+ffn1")
    return out
```

```python
import math
from contextlib import ExitStack

import numpy as np
import concourse.bass as bass
import concourse.mybir as mybir
from concourse.bass_utils import run_bass_kernel_spmd

F32 = mybir.dt.float32
BF16 = mybir.dt.bfloat16
AF = mybir.ActivationFunctionType
ALU = mybir.AluOpType

NCORES = 8
D = 1024
KC = 8
NCTX = 256
NLAT = 2048
NT = NCTX + NLAT
DFF = 2816
NPAIR = DFF // 128
EPS = 1e-6
GRID_W = 64
SLOT_ELEMS = 5120
NSLOT = 2
SCR_BYTES = 50688

_c = {}
_off = 0


def _cdef(name, n):
    global _off
    _c[name] = (_off, n)
    _off += n


_cdef("cT", 16)
_cdef("bmod", 2 * 96)
_cdef("lqk", 4 * 64)
_cdef("subln", 4)
_cdef("pscale", 4)
_cdef("hgnorm", 8)
_cdef("fnorm", 8)
_cdef("convw", 2 * 44 * 3)
_cdef("convb", 2 * 44)
_cdef("lblog", 2 * 2 * 8)
_cdef("pedge", 4 * 4 * 8)
_cdef("maskf", 64)
_cdef("maskb", 64)
_cdef("m01", 512)
NCONST = _off


def CS(name, a=0, n=None):
    o, sz = _c[name]
    if n is None:
        n = sz - a
    return slice(o + a, o + a + n)


class Prog:
    ROT = 7000

    def __init__(self, nc, es):
        self.nc = nc
        self.es = es
        self.eng = {"pe": nc.tensor, "act": nc.scalar, "dve": nc.vector,
                    "pool": nc.gpsimd, "sp": nc.sync}
        self.sem = {}
        self.cnt = {}
        self.nsem = 0
        for e in self.eng:
            self.sem[e] = self._newsem(e)
            self.cnt[e] = 0
        self.known = {e: {} for e in self.eng}
        self.res = {}
        self.dq = {}
        for q in ("sp", "pool", "act"):
            self.dq[q] = {"sems": [[self._newsem("d" + q), 0] for _ in range(8)], "rr": 0}
        self.out_tokens = []
        self.active = {}
        self.retired = set()

    def _newsem(self, tag):
        self.nsem += 1
        return self.es.enter_context(self.nc.semaphore(f"s_{tag}_{self.nsem}"))

    def _need(self, e, tok):
        sem, val, owner, snap = tok
        if owner == "pe" and e == "pe":
            return
        k = self.known[e]
        if k.get(sem.name, 0) >= val:
            return
        self.eng[e].wait_ge(sem, val)
        k[sem.name] = val
        for name, v in snap.items():
            if k.get(name, 0) < v:
                k[name] = v

    def _deps(self, e, reads, writes):
        for key in list(reads) + list(writes):
            if key in self.retired:
                raise RuntimeError(f"use of retired key {key}")
        for key in reads:
            r = self.res.get(key)
            if r and r[0] is not None:
                self._need(e, r[0])
        for key in writes:
            r = self.res.get(key)
            if r:
                if r[0] is not None:
                    self._need(e, r[0])
                for t in r[1]:
                    self._need(e, t)

    def _commit(self, tok, reads, writes):
        for key in reads:
            r = self.res.setdefault(key, [None, []])
            r[1] = [t for t in r[1] if t[0].name != tok[0].name] + [tok]
        for key in writes:
            self.res[key] = [tok, []]

    def op(self, e, fn, reads=(), writes=()):
        self._deps(e, reads, writes)
        if self.cnt[e] >= self.ROT:
            self.sem[e] = self._newsem(e)
            self.cnt[e] = 0
        inst = fn(self.eng[e])
        self.cnt[e] += 1
        inst.then_inc(self.sem[e], 1)
        tok = (self.sem[e], self.cnt[e], e, dict(self.known[e]))
        self._commit(tok, reads, writes)
        return tok

    def dma(self, q, out, in_, reads=(), writes=()):
        self._deps(q, reads, writes)
        dq = self.dq[q]
        slot = dq["sems"][dq["rr"] % len(dq["sems"])]
        dq["rr"] += 1
        if slot[1] > 0:
            self._need(q, (slot[0], slot[1], "dma", {}))
        inst = self.eng[q].dma_start(out=out, in_=in_)
        slot[1] += 16
        inst.then_inc(slot[0], 16)
        tok = (slot[0], slot[1], "dma", dict(self.known[q]))
        self._commit(tok, reads, writes)
        return tok

    def alias(self, new_keys, old_keys):
        toks = []
        for k in old_keys:
            r = self.res.get(k)
            if r:
                if r[0] is not None:
                    toks.append(r[0])
                toks.extend(r[1])
        for nk in new_keys:
            r = self.res.setdefault(nk, [None, []])
            r[1] = r[1] + toks

    def claim(self, region, lo, hi, key):
        act = self.active.setdefault(region, {})
        if key in act and act[key] == (lo, hi):
            return
        for k, (l, h) in list(act.items()):
            if l < hi and lo < h:
                self.alias([key], [k])
                del act[k]
                self.retired.add(k)
        act[key] = (lo, hi)
        self.retired.discard(key)

    def finish(self, e, toks):
        for t in toks:
            self._need(e, t)


def tok_blocks(lo, hi, step=512):
    out = []
    t = lo
    while t < hi:
        n = min(step, hi - t)
        out.append((t, n))
        t += n
    return out


def build_program(stop="full", debug=False):
    nc = bass.Bass("TRN2", target_bir_lowering=False)

    def din(name, shape, dt=F32):
        return nc.dram_tensor(name, list(shape), dt, kind="ExternalInput").ap()

    xT_d = din("xT", [128, KC, NT])
    const_d = din("const", [128, NCONST])
    constb_d = din("constb", [128, 256 + 512])
    wmod_d = din("wmod", [2, 12, 128, KC * 512])
    rope_d = din("rope", [128, 2 * NLAT])
    whead_d = din("whead", [4, 128, KC * 640])
    wpool_d = din("wpool", [128, KC * 512])
    wout0_d = din("wout0", [KC, 128, D])
    wffn_d = din("wffn", [2, NPAIR, 128, KC * 256])
    wdn_d = din("wdn", [2, KC, 128, NPAIR * 128])
    whg_d = din("whg", [8, 128, KC * 640])
    wout1_d = din("wout1", [KC, 128, D])
    out_d = nc.dram_tensor("outT", [128, KC, NLAT], F32, kind="ExternalOutput").ap()

    with ExitStack() as es:
        P = Prog(nc, es)

        def sb(name, shape, dt):
            return es.enter_context(nc.sbuf_tensor(name, list(shape), dt))

        xT = sb("xT_sb", [128, KC, NT], F32)
        REG = {"HT": sb("HT", [128, 18432], BF16),
               "SCR": sb("SCR", [128, SCR_BYTES // 2], BF16),
               "ROPE": sb("ROPE", [128, 8192], BF16)}
        WS = [sb(f"WS{i}", [128, SLOT_ELEMS], BF16) for i in range(NSLOT)]
        CONST = sb("CONST", [128, NCONST], F32)
        CB = sb("CB", [128, 256 + 512], BF16)
        MOD = sb("MOD", [128, 2 * 96], F32)
        MOD1 = sb("MOD1", [128, 2 * 96], F32)
        SC = sb("SC", [128, KC * 2], BF16)
        MISC = sb("MISC", [128, 160], F32)
        PS = [es.enter_context(nc.psum_tensor(f"ps{i}", [128, 512], F32)) for i in range(8)]

        ident = CB[:, 0:128]
        ones = CB[:, 128:256]

        def rv(region, off, nelem, dt, key=None, keys=None):
            esz = 2 if dt == BF16 else 4
            assert off % 4 == 0
            v = REG[region][:, off // 2: off // 2 + nelem * esz // 2]
            if dt != BF16:
                v = v.bitcast(dt)
            if key is not None:
                P.claim(region, off, off + nelem * esz, key)
            if keys is not None:
                for (k, so, n) in keys:
                    P.claim(region, off + so * esz, off + (so + n) * esz, k)
            return v

        def xk(c, t0, n):
            return [("xT", c, b) for b in range(t0 // 256, (t0 + n + 255) // 256)]

        P.dma("sp", CONST[:], const_d, writes=["CONST"])
        P.dma("pool", CB[:], constb_d, writes=["CB"])
        for c in range(KC):
            P.dma("sp", xT[:, c, :], xT_d[:, c, :], writes=xk(c, 0, NT))
        P.op("dve", lambda e: e.memset(MISC[:, 0:1], EPS), writes=["EPSB"])
        EPSB = MISC[:, 0:1]

        state = {"slot": 0}

        def wload(src_ap, nelem):
            s = state["slot"] % NSLOT
            state["slot"] += 1
            P.dma("pool", WS[s][:, 0:nelem], src_ap, writes=[("WS", s)])
            return s

        SC3 = SC[:, :].rearrange("p (k two) -> p k two", two=2)
        P.op("act", lambda e: e.activation(out=SC3[:, :, 0], in_=CONST[:, CS("cT", 0, 8)], func=AF.Silu),
             reads=["CONST"], writes=["SC0"])
        P.op("act", lambda e: e.activation(out=SC3[:, :, 1], in_=CONST[:, CS("cT", 8, 8)], func=AF.Silu),
             reads=["CONST"], writes=["SC1"])
        for layer in range(2):
            for piece in range(12):
                s = wload(wmod_d[layer, piece], KC * 512)
                for sub in range(4):
                    jc = piece * 4 + sub
                    col = layer * 96 + jc * 2
                    for kc in range(KC):
                        P.op("pe", lambda e, s=s, kc=kc, sub=sub, col=col: e.matmul(
                            PS[0][:, col:col + 2],
                            lhsT=WS[s][:, kc * 512 + sub * 128: kc * 512 + sub * 128 + 128],
                            rhs=SC[:, kc * 2: kc * 2 + 2],
                            start=(kc == 0), stop=(kc == KC - 1)),
                            reads=[("WS", s), "SC0", "SC1"], writes=[("ps", 0)])
        P.op("dve", lambda e: e.tensor_tensor(out=MOD[:, :], in0=PS[0][:, 0:192], in1=CONST[:, CS("bmod")],
                                               op=ALU.add),
             reads=[("ps", 0), "CONST"], writes=["MOD"])
        P.op("dve", lambda e: e.tensor_scalar_add(out=MOD1[:, :], in0=MOD[:, :], scalar1=1.0),
             reads=["MOD"], writes=["MOD1"])

        def modv(layer, j, chunk, which, plus1=False):
            t = MOD1 if plus1 else MOD
            col = layer * 96 + (j * 8 + chunk) * 2 + which
            return t[:, col:col + 1]

        def nrm_temps(region, off, tag):
            t = {"tag": tag}
            t["sq"] = [rv(region, off + i * 1024, 512, BF16, key=("sq", tag, i)) for i in range(2)]
            t["rs"] = rv(region, off + 2048, 512, F32, key=("rs", tag))
            t["rstd"] = rv(region, off + 4096, 512, F32, key=("rstd", tag))
            t["tmp"] = [rv(region, off + 6144 + i * 2048, 512, F32, key=("ntmp", tag, i)) for i in range(2)]
            return t

        def rms_rstd(T, src_fn, rkeys, nchunk, n, denom, psb):
            tag = T["tag"]
            for c in range(nchunk):
                b = c % 2
                P.op("act", lambda e, c=c, b=b: e.activation(out=T["sq"][b][:, :n], in_=src_fn(c), func=AF.Square),
                     reads=rkeys(c), writes=[("sq", tag, b)])
                P.op("pe", lambda e, c=c, b=b: e.matmul(PS[psb][:, :n], lhsT=ones, rhs=T["sq"][b][:, :n],
                                                        start=(c == 0), stop=(c == nchunk - 1)),
                     reads=[("sq", tag, b), "CB"], writes=[("ps", psb)])
            P.op("act", lambda e: e.activation(out=T["rs"][:, :n], in_=PS[psb][:, :n], func=AF.Ln,
                                               scale=1.0 / denom, bias=EPSB),
                 reads=[("ps", psb), "EPSB"], writes=[("rs", tag)])
            P.op("act", lambda e: e.activation(out=T["rstd"][:, :n], in_=T["rs"][:, :n], func=AF.Exp, scale=-0.5),
                 reads=[("rs", tag)], writes=[("rstd", tag)])

        def norm_mod(T, layer, jshift, jscale, t0, n, which, out_fn, out_keys, psb=1):
            tag = T["tag"]
            rms_rstd(T, lambda c: xT[:, c, t0:t0 + n], lambda c: xk(c, t0, n), KC, n, float(D), psb)
            for c in range(KC):
                b = c % 2
                P.op("dve", lambda e, c=c, b=b: e.scalar_tensor_tensor(
                    out=T["tmp"][b][:, :n], in0=xT[:, c, t0:t0 + n], scalar=modv(layer, jscale, c, which, True),
                    in1=T["rstd"][:, :n], op0=ALU.mult, op1=ALU.mult),
                    reads=xk(c, t0, n) + ["MOD1", ("rstd", tag)], writes=[("ntmp", tag, b)])
                P.op("act", lambda e, c=c, b=b: e.activation(
                    out=out_fn(c), in_=T["tmp"][b][:, :n], func=AF.Identity,
                    bias=modv(layer, jshift, c, which), scale=1.0),
                    reads=[("ntmp", tag, b), "MOD"], writes=out_keys(c))

        def ffn(layer, groups):
            HW2 = 1156
            GT = 1152
            cw = lambda c, tap: CONST[:, CS("convw", (layer * 44 + c) * 3 + tap, 1)]
            cb = lambda c: CONST[:, CS("convb", layer * 44 + c, 1)]
            h2 = rv("HT", 0, KC * HW2, BF16,
                    keys=[(("h2", c), c * HW2, HW2) for c in range(KC)]).rearrange("p (k t) -> p k t", k=KC)
            o1 = KC * HW2 * 2
            ctmp = [[rv("HT", o1 + (i * 2 + j) * 1920, 480, F32, key=("ctmp", i, j))
                     for j in range(2)] for i in range(2)]
            T = nrm_temps("HT", o1 + 4 * 1920, "H")
            stash = rv("HT", o1 + 4 * 1920 + 10240, KC * 8, BF16,
                       keys=[(("stash", c), c * 8, 8) for c in range(KC)]).rearrange("p (k t) -> p k t", k=KC)
            gbuf = rv("SCR", 0, NPAIR * GT, BF16,
                      keys=[(("g", c), c * GT, GT) for c in range(NPAIR)]
                      ).rearrange("p (c t) -> p c t", c=NPAIR)
            h2keys = [("h2", c) for c in range(KC)]
            si = 0
            stash_idx = {}
            for gi, segs in enumerate(groups):
                for (slo, shi, t0, G, which) in segs:
                    if t0 - 1 >= slo:
                        stash_idx[(gi, t0)] = si
                        norm_mod(T, layer, 3, 4, t0 - 2, 2, which,
                                 lambda c, si=si: stash[:, c, 2 * si:2 * si + 2],
                                 lambda c: [("stash", c)])
                        si += 1
            def layout(segs):
                lay = []
                hoff = goff = 0
                for (slo, shi, t0, G, which) in segs:
                    npiece = (G + 479) // 480
                    base = G // npiece
                    pieces = []
                    o = 0
                    for i in range(npiece):
                        n = base + (1 if i < G % npiece else 0)
                        pieces.append((o, n))
                        o += n
                    lay.append((slo, shi, t0, G, which, hoff, goff, pieces))
                    hoff += G + 2
                    goff += G
                assert hoff <= HW2 and goff <= GT
                return lay

            def prep_items(gi, lay):
                items = []
                for (slo, shi, t0, G, which, hoff, goff, pieces) in lay:
                    hi = min(t0 + G + 1, shi)

                    def halo(slo=slo, t0=t0, G=G, hoff=hoff, hi=hi):
                        if t0 - 1 >= slo:
                            sidx = stash_idx[(gi, t0)]
                            for c in range(KC):
                                P.op("dve", lambda e, c=c, sidx=sidx, hoff=hoff: e.tensor_copy(
                                    out=h2[:, c, hoff:hoff + 1], in_=stash[:, c, 2 * sidx + 1:2 * sidx + 2]),
                                    reads=[("stash", c)], writes=[("h2", c)])
                        else:
                            P.op("dve", lambda e, hoff=hoff: e.memset(h2[:, :, hoff:hoff + 1], 0.0), writes=h2keys)
                        if hi < t0 + G + 1:
                            P.op("dve", lambda e, G=G, hoff=hoff: e.memset(h2[:, :, hoff + G + 1:hoff + G + 2], 0.0),
                                 writes=h2keys)
                    items.append(halo)
                    for (b0, bn) in tok_blocks(t0, hi):
                        j0 = hoff + b0 - (t0 - 1)
                        items.append(lambda b0=b0, bn=bn, j0=j0, which=which: norm_mod(
                            T, layer, 3, 4, b0, bn, which,
                            lambda c, j0=j0, bn=bn: h2[:, c, j0:j0 + bn],
                            lambda c: [("h2", c)]))
                return items

            lays = [layout(segs) for segs in groups]
            for it_ in prep_items(0, lays[0]):
                it_()
            for gi, segs in enumerate(groups):
                lay = lays[gi]
                pending = prep_items(gi + 1, lays[gi + 1]) if gi + 1 < len(groups) else []
                it = 0
                for c in range(NPAIR):
                    s = wload(wffn_d[layer, c], KC * 256)
                    for (slo, shi, t0, G, which, hoff, goff, pieces) in lay:
                        for (o, n) in pieces:
                            bsel = it % 2
                            it += 1
                            pa, pg = 2 + bsel * 2, 3 + bsel * 2
                            ho = hoff + o
                            for half, pb in ((0, pa), (1, pg)):
                                for kc in range(KC):
                                    P.op("pe", lambda e, s=s, kc=kc, half=half, pb=pb, ho=ho, n=n: e.matmul(
                                        PS[pb][:, 0:n + 2],
                                        lhsT=WS[s][:, kc * 256 + half * 128: kc * 256 + half * 128 + 128],
                                        rhs=h2[:, kc, ho:ho + n + 2],
                                        start=(kc == 0), stop=(kc == KC - 1)),
                                        reads=[("WS", s), ("h2", kc)], writes=[("ps", pb)])
                            for half, pb in ((0, pa), (1, pg)):
                                cc = c + half * NPAIR
                                tb = ctmp[bsel][half]
                                P.op("act", lambda e, pb=pb, tb=tb, cc=cc, n=n: e.activation(
                                    out=tb[:, 0:n], in_=PS[pb][:, 1:n + 1], func=AF.Identity,
                                    scale=cw(cc, 1), bias=cb(cc)),
                                    reads=[("ps", pb), "CONST"], writes=[("ctmp", bsel, half)])
                                P.op("dve", lambda e, pb=pb, tb=tb, cc=cc, n=n: e.scalar_tensor_tensor(
                                    out=tb[:, 0:n], in0=PS[pb][:, 0:n], scalar=cw(cc, 0), in1=tb[:, 0:n],
                                    op0=ALU.mult, op1=ALU.add),
                                    reads=[("ps", pb), "CONST", ("ctmp", bsel, half)], writes=[("ctmp", bsel, half)])
                                P.op("dve", lambda e, pb=pb, tb=tb, cc=cc, n=n: e.scalar_tensor_tensor(
                                    out=tb[:, 0:n], in0=PS[pb][:, 2:n + 2], scalar=cw(cc, 2), in1=tb[:, 0:n],
                                    op0=ALU.mult, op1=ALU.add),
                                    reads=[("ps", pb), "CONST", ("ctmp", bsel, half)], writes=[("ctmp", bsel, half)])
                            P.op("act", lambda e, bsel=bsel, n=n: e.activation(
                                out=ctmp[bsel][1][:, 0:n], in_=ctmp[bsel][1][:, 0:n], func=AF.Silu),
                                reads=[("ctmp", bsel, 1)], writes=[("ctmp", bsel, 1)])
                            go = goff + o
                            P.op("dve", lambda e, bsel=bsel, n=n, c=c, go=go: e.tensor_tensor(
                                out=gbuf[:, c, go:go + n], in0=ctmp[bsel][0][:, 0:n], in1=ctmp[bsel][1][:, 0:n],
                                op=ALU.mult),
                                reads=[("ctmp", bsel, 0), ("ctmp", bsel, 1)], writes=[("g", c)])
                it = 0
                for m in range(KC):
                    s = wload(wdn_d[layer, m], NPAIR * 128)
                    for (slo, shi, t0, G, which, hoff, goff, pieces) in lay:
                        for (b0, bn) in tok_blocks(0, G):
                            pb = 2 + (it % 4)
                            it += 1
                            gb = goff + b0
                            for c in range(NPAIR):
                                P.op("pe", lambda e, s=s, c=c, pb=pb, gb=gb, bn=bn: e.matmul(
                                    PS[pb][:, 0:bn], lhsT=WS[s][:, c * 128:(c + 1) * 128],
                                    rhs=gbuf[:, c, gb:gb + bn], start=(c == 0), stop=(c == NPAIR - 1)),
                                    reads=[("WS", s), ("g", c)], writes=[("ps", pb)])
                            tt = t0 + b0
                            P.op("dve", lambda e, pb=pb, m=m, tt=tt, bn=bn, which=which: e.scalar_tensor_tensor(
                                out=xT[:, m, tt:tt + bn], in0=PS[pb][:, 0:bn], scalar=modv(layer, 5, m, which),
                                in1=xT[:, m, tt:tt + bn], op0=ALU.mult, op1=ALU.add),
                                reads=[("ps", pb), "MOD"] + xk(m, tt, bn), writes=xk(m, tt, bn))
                    if pending:
                        pending.pop(0)()
                while pending:
                    pending.pop(0)()

        def hT_view():
            return rv("HT", 0, KC * NT, BF16,
                      keys=[(("hT", c), c * NT, NT) for c in range(KC)]).rearrange("p (k t) -> p k t", k=KC)

        ALLBLK = [(0, NCTX, 1)] + [(b0, bn, 0) for (b0, bn) in tok_blocks(NCTX, NT)]

        def compute_hT(layer):
            hT = hT_view()
            T = nrm_temps("SCR", SCR_BYTES - 10240, "S")
            for (b0, bn, which) in ALLBLK:
                norm_mod(T, layer, 0, 1, b0, bn, which,
                         lambda c, b0=b0, bn=bn: hT[:, c, b0:b0 + bn], lambda c: [("hT", c)])
            return hT

        hkeys = [("hT", c) for c in range(KC)]
        psrr = {"i": 0}

        def nextps(lo=0, hi=8):
            i = lo + psrr["i"] % (hi - lo)
            psrr["i"] += 1
            return i

        def resid_add(layer, ysrc_fn, ykeys, s, wcol_fn, nk, jgate, blocks=None):
            for m in range(KC):
                for (b0, bn, which) in (blocks or ALLBLK):
                    pb = nextps()
                    for k in range(nk):
                        P.op("pe", lambda e, m=m, k=k, pb=pb, b0=b0, bn=bn: e.matmul(
                            PS[pb][:, 0:bn], lhsT=wcol_fn(k, m), rhs=ysrc_fn(k, b0, bn),
                            start=(k == 0), stop=(k == nk - 1)),
                            reads=[("WS", s)] + ykeys(k), writes=[("ps", pb)])
                    P.op("dve", lambda e, m=m, pb=pb, b0=b0, bn=bn, which=which: e.scalar_tensor_tensor(
                        out=xT[:, m, b0:b0 + bn], in0=PS[pb][:, 0:bn], scalar=modv(layer, jgate, m, which),
                        in1=xT[:, m, b0:b0 + bn], op0=ALU.mult, op1=ALU.add),
                        reads=[("ps", pb), "MOD"] + xk(m, b0, bn), writes=xk(m, b0, bn))

        def mixer0():
            layer = 0
            hT = compute_hT(layer)
            WP = 2336

            def col(t):
                return t + 8 if t < NCTX else t + 24
            U = rv("SCR", 0, WP, F32, key="U")
            T1 = rv("SCR", 9344, WP, F32, key="T1")
            dbuf = rv("SCR", 18688, WP, BF16, key="dbuf")
            ypool = rv("SCR", 23360, 4 * NT, BF16,
                       keys=[(("ypool", g), g * NT, NT) for g in range(4)]).rearrange("p (g t) -> p g t", g=4)
            T2 = rv("ROPE", 0, WP, F32, key="T2")
            s_u = wload(wpool_d, KC * 512)
            for (a, b) in ((0, 8), (264, 280), (2328, 2336)):
                P.op("dve", lambda e, a=a, b=b: e.memset(U[:, a:b], 0.0), writes=["U"])
            tmpE = MISC[:, 8:16]
            for g, w in enumerate((2, 4, 8, 16)):
                for (b0, bn, which) in ALLBLK:
                    pb = nextps()
                    for kc in range(KC):
                        P.op("pe", lambda e, kc=kc, pb=pb, b0=b0, bn=bn, g=g: e.matmul(
                            PS[pb][:, 0:bn], lhsT=WS[s_u][:, kc * 512 + g * 128: kc * 512 + g * 128 + 128],
                            rhs=hT[:, kc, b0:b0 + bn], start=(kc == 0), stop=(kc == KC - 1)),
                            reads=[("WS", s_u), ("hT", kc)], writes=[("ps", pb)])
                    P.op("act", lambda e, pb=pb, b0=b0, bn=bn: e.activation(
                        out=U[:, col(b0):col(b0) + bn], in_=PS[pb][:, 0:bn], func=AF.Identity),
                        reads=[("ps", pb)], writes=["U"])
                P.op("dve", lambda e: e.tensor_tensor(out=T1[:, 1:WP], in0=U[:, 0:WP - 1], in1=U[:, 1:WP], op=ALU.add),
                     reads=["U"], writes=["T1"])
                A, Akey = T1, "T1"
                if w >= 4:
                    P.op("dve", lambda e: e.tensor_tensor(out=T2[:, 2:WP - 1], in0=T1[:, 1:WP - 2], in1=T1[:, 3:WP],
                                                          op=ALU.add), reads=["T1"], writes=["T2"])
                    A, Akey = T2, "T2"
                if w >= 8:
                    P.op("dve", lambda e: e.tensor_tensor(out=T1[:, 4:WP - 3], in0=T2[:, 2:WP - 5], in1=T2[:, 6:WP - 1],
                                                          op=ALU.add), reads=["T2"], writes=["T1"])
                    A, Akey = T1, "T1"
                if w >= 16:
                    P.op("dve", lambda e: e.tensor_tensor(out=T2[:, 8:WP - 7], in0=T1[:, 4:WP - 11], in1=T1[:, 12:WP - 3],
                                                          op=ALU.add), reads=["T1"], writes=["T2"])
                    A, Akey = T2, "T2"
                P.op("dve", lambda e, A=A, w=w: e.scalar_tensor_tensor(
                    out=dbuf[:, 8:2328], in0=A[:, 8:2328], scalar=1.0 / w, in1=U[:, 8:2328],
                    op0=ALU.mult, op1=ALU.subtract), reads=[Akey, "U"], writes=["dbuf"])
                hw = w // 2
                for ei, c0 in enumerate((col(0), col(NCTX - 1) + 1 - hw, col(NCTX), col(NT - 1) + 1 - hw)):
                    tbl = CONST[:, CS("pedge", (g * 4 + ei) * 8, hw)]
                    P.op("dve", lambda e, A=A, c0=c0, tbl=tbl, hw=hw: e.tensor_tensor(
                        out=tmpE[:, 0:hw], in0=A[:, c0:c0 + hw], in1=tbl, op=ALU.mult),
                        reads=[Akey, "CONST"], writes=["tmpE"])
                    P.op("dve", lambda e, c0=c0, hw=hw: e.tensor_tensor(
                        out=dbuf[:, c0:c0 + hw], in0=tmpE[:, 0:hw], in1=U[:, c0:c0 + hw], op=ALU.subtract),
                        reads=["tmpE", "U", "dbuf"], writes=["dbuf"])
                for (b0, bn, which) in ALLBLK:
                    pb = nextps()
                    P.op("pe", lambda e, pb=pb, b0=b0, bn=bn, g=g: e.matmul(
                        PS[pb][:, 0:bn], lhsT=CB[:, 256 + g * 128: 256 + (g + 1) * 128],
                        rhs=dbuf[:, col(b0):col(b0) + bn], start=True, stop=True),
                        reads=["CB", "dbuf"], writes=[("ps", pb)])
                    P.op("act", lambda e, pb=pb, b0=b0, bn=bn, g=g: e.activation(
                        out=ypool[:, g, b0:b0 + bn], in_=PS[pb][:, 0:bn], func=AF.Identity,
                        scale=CONST[:, CS("pscale", g, 1)]),
                        reads=[("ps", pb), "CONST"], writes=[("ypool", g)])
            s_w = state["slot"] % NSLOT
            state["slot"] += 1
            P.dma("pool", WS[s_w][:, 0:4096].rearrange("p (k n) -> p k n", k=4),
                  wout0_d[4:8].rearrange("k p n -> p k n"), writes=[("WS", s_w)])
            resid_add(layer, lambda k, b0, bn: ypool[:, k, b0:b0 + bn], lambda k: [("ypool", k)], s_w,
                      lambda k, m: WS[s_w][:, k * 1024 + m * 128: k * 1024 + m * 128 + 128], 4, 2)

            ropeT = rv("ROPE", 0, 2 * NLAT, F32, key="rope")
            P.dma("sp", ropeT, rope_d, writes=["rope"])
            CTt, STt = ropeT[:, 0:NLAT], ropeT[:, NLAT:2 * NLAT]
            P.op("dve", lambda e: e.tensor_tensor(out=MISC[:, 16:80], in0=CONST[:, CS("lqk", 0, 64)],
                                                  in1=CONST[:, CS("lqk", 64, 64)], op=ALU.mult),
                 reads=["CONST"], writes=["lam_p1"])
            P.op("dve", lambda e: e.tensor_tensor(out=MISC[:, 80:144], in0=CONST[:, CS("lqk", 128, 64)],
                                                  in1=CONST[:, CS("lqk", 192, 64)], op=ALU.mult),
                 reads=["CONST"], writes=["lam_p2"])
            P.op("dve", lambda e: e.reduce_sum(out=MISC[:, 1:2], in_=MISC[:, 16:80], axis=mybir.AxisListType.X),
                 reads=["lam_p1"], writes=["lam_s1"])
            P.op("dve", lambda e: e.reduce_sum(out=MISC[:, 2:3], in_=MISC[:, 80:144], axis=mybir.AxisListType.X),
                 reads=["lam_p2"], writes=["lam_s2"])
            P.op("act", lambda e: e.activation(out=MISC[:, 1:3], in_=MISC[:, 1:3], func=AF.Exp),
                 reads=["lam_s1", "lam_s2"], writes=["lam_e"])
            P.op("dve", lambda e: e.tensor_tensor(out=MISC[:, 3:4], in0=MISC[:, 2:3], in1=MISC[:, 1:2],
                                                  op=ALU.subtract), reads=["lam_e"], writes=["neglam0"])
            lambda_init = 0.8 - 0.6 * math.exp(-0.3 * 0)
            P.op("dve", lambda e: e.tensor_scalar_add(out=MISC[:, 3:4], in0=MISC[:, 3:4], scalar1=-lambda_init),
                 reads=["neglam0"], writes=["neglam"])
            P.op("dve", lambda e: e.tensor_scalar_mul(out=MISC[:, 4:8], in0=CONST[:, CS("subln")],
                                                      scalar1=1.0 - lambda_init),
                 reads=["CONST"], writes=["G08"])
            neglam = MISC[:, 3:4]

            qT = rv("SCR", 0, NT, BF16, key="qT")
            kT = rv("SCR", 4608, NT, BF16, key="kT")
            Vh = rv("SCR", 9216, 18 * 128, BF16, key="Vh").rearrange("p (i d) -> p i d", i=18)
            yh = rv("SCR", 13824, NT, BF16, key="yh")
            E = [[rv("SCR", 18432 + (m * 2 + b) * 1024, 512, BF16, key=("E", m, b)) for b in range(2)]
                 for m in range(2)]
            rt = [rv("SCR", 22528 + i * 2048, 512, F32, key=("rt", i)) for i in range(2)]
            r0 = rv("SCR", 26624, 512, F32, key="r0")
            r1 = rv("SCR", 28672, 512, F32, key="r1")
            t0b = rv("SCR", 30720, 512, F32, key="t0b")
            t1b = rv("SCR", 32768, 512, F32, key="t1b")
            sqh = rv("SCR", 34816, 512, BF16, key="sqh")
            rsh = rv("SCR", 35840, 512, F32, key="rsh")
            rrh = rv("SCR", 37888, 512, F32, key="rrh")
            for h in range(4):
                s = wload(whead_d[h], KC * 640)
                W = lambda kc, c0: WS[s][:, kc * 640 + c0: kc * 640 + c0 + 128]
                for (dst, dkey, c0) in ((qT, "qT", 0), (kT, "kT", 256)):
                    for (b0, bn, which) in ALLBLK:
                        pa = nextps()
                        for kc in range(KC):
                            P.op("pe", lambda e, kc=kc, pa=pa, b0=b0, bn=bn, c0=c0: e.matmul(
                                PS[pa][:, 0:bn], lhsT=W(kc, c0), rhs=hT[:, kc, b0:b0 + bn],
                                start=(kc == 0), stop=(kc == KC - 1)),
                                reads=[("WS", s), ("hT", kc)], writes=[("ps", pa)])
                        if which == 1:
                            P.op("act", lambda e, pa=pa, b0=b0, bn=bn, dst=dst: e.activation(
                                out=dst[:, b0:b0 + bn], in_=PS[pa][:, 0:bn], func=AF.Identity),
                                reads=[("ps", pa)], writes=[dkey])
                            continue
                        pb = nextps()
                        for kc in range(KC):
                            P.op("pe", lambda e, kc=kc, pb=pb, b0=b0, bn=bn, c0=c0: e.matmul(
                                PS[pb][:, 0:bn], lhsT=W(kc, c0 + 128), rhs=hT[:, kc, b0:b0 + bn],
                                start=(kc == 0), stop=(kc == KC - 1)),
                                reads=[("WS", s), ("hT", kc)], writes=[("ps", pb)])
                        l0 = b0 - NCTX
                        P.op("dve", lambda e, pa=pa, bn=bn, l0=l0: e.tensor_tensor(
                            out=rt[0][:, 0:bn], in0=PS[pa][:, 0:bn], in1=CTt[:, l0:l0 + bn], op=ALU.mult),
                            reads=[("ps", pa), "rope"], writes=[("rt", 0)])
                        P.op("dve", lambda e, pb=pb, bn=bn, l0=l0: e.tensor_tensor(
                            out=rt[1][:, 0:bn], in0=PS[pb][:, 0:bn], in1=STt[:, l0:l0 + bn], op=ALU.mult),
                            reads=[("ps", pb), "rope"], writes=[("rt", 1)])
                        P.op("dve", lambda e, b0=b0, bn=bn, dst=dst: e.tensor_tensor(
                            out=dst[:, b0:b0 + bn], in0=rt[0][:, 0:bn], in1=rt[1][:, 0:bn], op=ALU.add),
                            reads=[("rt", 0), ("rt", 1)], writes=[dkey])
                for i4 in range(0, 18, 4):
                    nt_ = min(4, 18 - i4)
                    pb = nextps()
                    for j in range(nt_):
                        i = i4 + j
                        for kc in range(KC):
                            P.op("pe", lambda e, kc=kc, pb=pb, i=i, j=j: e.matmul(
                                PS[pb][:, j * 128:(j + 1) * 128], lhsT=hT[:, kc, i * 128:(i + 1) * 128],
                                rhs=W(kc, 512), start=(kc == 0), stop=(kc == KC - 1)),
                                reads=[("WS", s), ("hT", kc)], writes=[("ps", pb)])
                    P.op("act", lambda e, pb=pb, i4=i4, nt_=nt_: e.activation(
                        out=Vh[:, i4:i4 + nt_, :], in_=PS[pb][:, 0:nt_ * 128].rearrange("p (i d) -> p i d", i=nt_),
                        func=AF.Identity), reads=[("ps", pb)], writes=["Vh"])
                for (q0, n, which) in ALLBLK:
                    nkt = 2 if which == 1 else 18

                    def qk(i):
                        b = i % 2
                        for m in range(2):
                            P.op("pe", lambda e, m=m, b=b, i=i: e.matmul(
                                PS[4 + m * 2 + b][:, 0:n], lhsT=kT[64 * m:64 * m + 64, i * 128:(i + 1) * 128],
                                rhs=qT[64 * m:64 * m + 64, q0:q0 + n], start=True, stop=True),
                                reads=["kT", "qT"], writes=[("ps", 4 + m * 2 + b)])
                    qk(0)
                    for i in range(nkt):
                        b = i % 2
                        if i + 1 < nkt:
                            qk(i + 1)
                        for m in range(2):
                            P.op("act", lambda e, m=m, b=b, n=n: e.activation(
                                out=E[m][b][:, 0:n], in_=PS[4 + m * 2 + b][:, 0:n], func=AF.Exp, scale=0.125),
                                reads=[("ps", 4 + m * 2 + b)], writes=[("E", m, b)])
                        for m in range(2):
                            P.op("pe", lambda e, m=m, b=b, i=i, n=n: e.matmul(
                                PS[m][:, 0:n], lhsT=Vh[:, i, :], rhs=E[m][b][:, 0:n],
                                start=(i == 0), stop=(i == nkt - 1)),
                                reads=["Vh", ("E", m, b)], writes=[("ps", m)])
                            P.op("pe", lambda e, m=m, b=b, i=i, n=n: e.matmul(
                                PS[2 + m][:, 0:n], lhsT=ones, rhs=E[m][b][:, 0:n],
                                start=(i == 0), stop=(i == nkt - 1)),
                                reads=["CB", ("E", m, b)], writes=[("ps", 2 + m)])
                    P.op("act", lambda e, n=n: e.activation(out=r0[:, 0:n], in_=PS[2][:, 0:n], func=AF.Ln),
                         reads=[("ps", 2)], writes=["r0"])
                    P.op("act", lambda e, n=n: e.activation(out=r1[:, 0:n], in_=PS[3][:, 0:n], func=AF.Ln),
                         reads=[("ps", 3)], writes=["r1"])
                    P.op("act", lambda e, n=n: e.activation(out=r0[:, 0:n], in_=r0[:, 0:n], func=AF.Exp, scale=-1.0),
                         reads=["r0"], writes=["r0"])
                    P.op("act", lambda e, n=n: e.activation(out=r1[:, 0:n], in_=r1[:, 0:n], func=AF.Exp, scale=-1.0),
                         reads=["r1"], writes=["r1"])
                    P.op("dve", lambda e, n=n: e.tensor_tensor(out=t0b[:, 0:n], in0=PS[0][:, 0:n], in1=r0[:, 0:n],
                                                               op=ALU.mult),
                         reads=[("ps", 0), "r0"], writes=["t0b"])
                    P.op("dve", lambda e, n=n: e.tensor_tensor(out=t1b[:, 0:n], in0=PS[1][:, 0:n], in1=r1[:, 0:n],
                                                               op=ALU.mult),
                         reads=[("ps", 1), "r1"], writes=["t1b"])
                    P.op("dve", lambda e, n=n: e.scalar_tensor_tensor(
                        out=t0b[:, 0:n], in0=t1b[:, 0:n], scalar=neglam, in1=t0b[:, 0:n],
                        op0=ALU.mult, op1=ALU.add), reads=["t1b", "t0b", "neglam"], writes=["t0b"])
                    P.op("act", lambda e, n=n: e.activation(out=sqh[:, 0:n], in_=t0b[:, 0:n], func=AF.Square),
                         reads=["t0b"], writes=["sqh"])
                    P.op("pe", lambda e, n=n: e.matmul(PS[2][:, 0:n], lhsT=ones, rhs=sqh[:, 0:n], start=True, stop=True),
                         reads=["CB", "sqh"], writes=[("ps", 2)])
                    P.op("act", lambda e, n=n: e.activation(out=rsh[:, 0:n], in_=PS[2][:, 0:n], func=AF.Ln,
                                                            scale=1.0 / 128.0, bias=EPSB),
                         reads=[("ps", 2), "EPSB"], writes=["rsh"])
                    P.op("act", lambda e, n=n: e.activation(out=rrh[:, 0:n], in_=rsh[:, 0:n], func=AF.Exp, scale=-0.5),
                         reads=["rsh"], writes=["rrh"])
                    P.op("dve", lambda e, n=n, q0=q0, h=h: e.scalar_tensor_tensor(
                        out=yh[:, q0:q0 + n], in0=t0b[:, 0:n], scalar=MISC[:, 4 + h:5 + h], in1=rrh[:, 0:n],
                        op0=ALU.mult, op1=ALU.mult), reads=["t0b", "G08", "rrh"], writes=["yh"])
                s_w = wload(wout0_d[h], D)
                resid_add(layer, lambda k, b0, bn: yh[:, b0:b0 + bn], lambda k: ["yh"], s_w,
                          lambda k, m, s_w=s_w: WS[s_w][:, m * 128:(m + 1) * 128], 1, 2)

        HS = sb("HG_S", [128, 2 * 128], F32)
        HSm = sb("HG_Sm", [128, 2 * 128], BF16)
        ATm = sb("HG_ATm", [128, 2 * 64], BF16)
        HSC = sb("HG_SC", [128, 6 * 36 + 16 + 32], F32)
        LATBLK = [(b0, bn, 0) for (b0, bn) in tok_blocks(NCTX, NT)]

        def mixer1():
            stages = stop.split("+")
            layer = 1
            hT = compute_hT(layer)
            LB = HSC[:, 232:248]
            OML = HSC[:, 248:264]
            lbl = CONST[:, CS("lblog")].rearrange("p (d l h) -> p d l h", d=2, l=2)
            LB3 = LB.rearrange("p (d h) -> p d h", d=2)
            P.op("dve", lambda e: e.tensor_tensor(out=LB3, in0=lbl[:, :, 1, :], in1=lbl[:, :, 0, :], op=ALU.subtract),
                 reads=["CONST"], writes=["LB"])
            P.op("act", lambda e: e.activation(out=LB, in_=LB, func=AF.Sigmoid), reads=["LB"], writes=["LB"])
            P.op("dve", lambda e: e.tensor_scalar(out=OML, in0=LB, scalar1=-1.0, scalar2=1.0, op0=ALU.mult,
                                                  op1=ALU.add), reads=["LB"], writes=["OML"])

            Qd = [rv("SCR", d * 4096, NLAT, BF16, key=("Qd", d)) for d in range(2)]
            Kd = [rv("SCR", 8192 + d * 4608, NT, BF16, key=("Kd", d)) for d in range(2)]
            Ktok = [rv("SCR", 17408 + d * 4608, 18 * 128, BF16, key=("Ktok", d)).rearrange("p (i c) -> p i c", i=18)
                    for d in range(2)]
            Vh = rv("SCR", 26624, 18 * 128, BF16, key="Vh1").rearrange("p (i d) -> p i d", i=18)
            TO = 31232
            OF = rv("ROPE", 0, NLAT, F32, keys=[(("OF", b), b * 512, 512) for b in range(4)])
            SG = rv("ROPE", 8192, NLAT, BF16, key="SG")
            YH = rv("ROPE", 12288, NLAT, BF16, key="YH")
            AL = [HSC[:, d * 108 + 0: d * 108 + 36] for d in range(2)]
            BE = [HSC[:, d * 108 + 36: d * 108 + 72] for d in range(2)]
            GA = [HSC[:, d * 108 + 72: d * 108 + 108] for d in range(2)]
            RP = HSC[:, 216:224]
            PIECES = tok_blocks(0, NT)
            for h in range(8):
                Qp = rv("SCR", TO, 512, F32, key="Qp")
                Fb = [rv("SCR", TO + 2048 + d * 2048, 512, F32, key=("Fb", d)) for d in range(2)]
                Kk = rv("SCR", TO + 6144, 512, F32, key="Kk")
                Bc = rv("SCR", TO + 8192, 512, F32, key="Bc")
                Gl = rv("SCR", TO + 10240, 512, F32, key="Gl")
                s = wload(whg_d[h], KC * 640)
                W = lambda kc, c0: WS[s][:, kc * 640 + c0: kc * 640 + c0 + 128]

                def proj(c0, p0, n, pb):
                    for kc in range(KC):
                        P.op("pe", lambda e, kc=kc: e.matmul(
                            PS[pb][:, 0:n], lhsT=W(kc, c0), rhs=hT[:, kc, p0:p0 + n],
                            start=(kc == 0), stop=(kc == KC - 1)),
                            reads=[("WS", s), ("hT", kc)], writes=[("ps", pb)])

                for i4 in range(0, 18, 4):
                    nt_ = min(4, 18 - i4)
                    pb = nextps()
                    for j in range(nt_):
                        i = i4 + j
                        for kc in range(KC):
                            P.op("pe", lambda e, kc=kc, pb=pb, i=i, j=j: e.matmul(
                                PS[pb][:, j * 128:(j + 1) * 128], lhsT=hT[:, kc, i * 128:(i + 1) * 128],
                                rhs=W(kc, 128), start=(kc == 0), stop=(kc == KC - 1)),
                                reads=[("WS", s), ("hT", kc)], writes=[("ps", pb)])
                    P.op("act", lambda e, pb=pb, i4=i4, nt_=nt_: e.activation(
                        out=Vh[:, i4:i4 + nt_, :], in_=PS[pb][:, 0:nt_ * 128].rearrange("p (i d) -> p i d", i=nt_),
                        func=AF.Identity), reads=[("ps", pb)], writes=["Vh1"])
                for (b0, bn, _) in LATBLK:
                    pb = nextps()
                    proj(512, b0, bn, pb)
                    P.op("act", lambda e, pb=pb, b0=b0, bn=bn: e.activation(
                        out=SG[:, b0 - NCTX:b0 - NCTX + bn], in_=PS[pb][:, 0:bn], func=AF.Silu),
                        reads=[("ps", pb)], writes=["SG"])
                for (p0, n) in PIECES:
                    nch = n // 64
                    c0 = p0 // 64
                    l0 = max(p0, NCTX)
                    ln = p0 + n - l0
                    lo = l0 - p0
                    v3 = lambda t, n=n: t[:, 0:n].rearrange("p (c t) -> p c t", t=64)
                    sc = lambda t, nch=nch, c0=c0: t[:, c0:c0 + nch].rearrange("p (c o) -> p c o", o=1)
                    pb = nextps()
                    proj(0, p0, n, pb)
                    P.op("act", lambda e, pb=pb, n=n: e.activation(out=Qp[:, 0:n], in_=PS[pb][:, 0:n], func=AF.Silu),
                         reads=[("ps", pb)], writes=["Qp"])
                    for d in range(2):
                        pb = nextps()
                        proj(256 + d * 128, p0, n, pb)
                        P.op("act", lambda e, pb=pb, n=n, d=d: e.activation(out=Fb[d][:, 0:n], in_=PS[pb][:, 0:n],
                                                                            func=AF.Sigmoid),
                             reads=[("ps", pb)], writes=[("Fb", d)])
                    for d in range(2):
                        F = Fb[d]
                        P.op("dve", lambda e, n=n, d=d, h=h, F=F: e.tensor_scalar(
                            out=F[:, 0:n], in0=F[:, 0:n], scalar1=OML[:, d * 8 + h:d * 8 + h + 1],
                            scalar2=LB[:, d * 8 + h:d * 8 + h + 1], op0=ALU.mult, op1=ALU.add),
                            reads=[("Fb", d), "OML", "LB"], writes=[("Fb", d)])
                        P.op("dve", lambda e, n=n, F=F: e.tensor_scalar(
                            out=Kk[:, 0:n], in0=F[:, 0:n], scalar1=-1.0, scalar2=1.0, op0=ALU.mult, op1=ALU.add),
                            reads=[("Fb", d)], writes=["Kk"])
                        P.op("act", lambda e, n=n, F=F: e.activation(out=F[:, 0:n], in_=F[:, 0:n], func=AF.Ln),
                             reads=[("Fb", d)], writes=[("Fb", d)])
                        P.op("dve", lambda e, n=n, F=F: e.tensor_tensor_scan(
                            out=Bc[:, 0:n], data0=CONST[:, CS("m01", 0, n)], data1=F[:, 0:n], initial=0.0,
                            op0=ALU.mult, op1=ALU.add), reads=[("Fb", d), "CONST"], writes=["Bc"])
                        B3 = v3(Bc)
                        G3 = v3(Gl)
                        P.op("dve", lambda e, nch=nch, B3=B3, G3=G3: e.tensor_tensor(
                            out=G3, in0=B3, in1=B3[:, :, 31:32].to_broadcast([128, nch, 64]), op=ALU.subtract),
                            reads=["Bc"], writes=["Gl"])
                        P.op("dve", lambda e, d=d, B3=B3: e.tensor_copy(out=sc(AL[d]), in_=B3[:, :, 63:64]),
                             reads=["Bc"], writes=[("AL", d)])
                        bsrc = G3[:, :, 63:64] if d == 0 else B3[:, :, 31:32]
                        gsrc = B3[:, :, 31:32] if d == 0 else G3[:, :, 63:64]
                        P.op("dve", lambda e, d=d, bsrc=bsrc: e.tensor_copy(out=sc(BE[d]), in_=bsrc),
                             reads=["Bc", "Gl"], writes=[("BE", d)])
                        P.op("dve", lambda e, d=d, gsrc=gsrc: e.tensor_copy(out=sc(GA[d]), in_=gsrc),
                             reads=["Bc", "Gl"], writes=[("GA", d)])
                        if d == 0:
                            P.op("act", lambda e, n=n: e.activation(out=Bc[:, 0:n], in_=Gl[:, 0:n], func=AF.Exp),
                                 reads=["Gl", "Bc"], writes=["Bc"])
                            P.op("act", lambda e, n=n, F=F: e.activation(out=F[:, 0:n], in_=Gl[:, 0:n], func=AF.Exp,
                                                                         scale=-1.0),
                                 reads=["Gl", ("Fb", d)], writes=[("Fb", d)])
                            if ln > 0:
                                P.op("dve", lambda e, lo=lo, ln=ln, l0=l0: e.tensor_tensor(
                                    out=Qd[0][:, l0 - NCTX:l0 - NCTX + ln], in0=Qp[:, lo:lo + ln],
                                    in1=Bc[:, lo:lo + ln], op=ALU.mult), reads=["Qp", "Bc"], writes=[("Qd", 0)])
                            P.op("dve", lambda e, n=n, p0=p0, F=F: e.tensor_tensor(
                                out=Kd[0][:, p0:p0 + n], in0=Kk[:, 0:n], in1=F[:, 0:n], op=ALU.mult),
                                reads=["Kk", ("Fb", d)], writes=[("Kd", 0)])
                        else:
                            F3 = v3(F)
                            P.op("dve", lambda e, F3=F3, G3=G3: e.tensor_copy(out=F3[:, :, 1:64], in_=G3[:, :, 0:63]),
                                 reads=["Gl", ("Fb", d)], writes=[("Fb", d)])
                            P.op("dve", lambda e, F3=F3, B3=B3: e.tensor_scalar(
                                out=F3[:, :, 0:1], in0=B3[:, :, 31:32], scalar1=-1.0, scalar2=None, op0=ALU.mult),
                                reads=["Bc", ("Fb", d)], writes=[("Fb", d)])
                            P.op("act", lambda e, n=n, F=F: e.activation(out=Bc[:, 0:n], in_=F[:, 0:n], func=AF.Exp),
                                 reads=[("Fb", d), "Bc"], writes=["Bc"])
                            P.op("dve", lambda e, n=n, p0=p0: e.tensor_tensor(
                                out=Kd[1][:, p0:p0 + n], in0=Kk[:, 0:n], in1=Bc[:, 0:n], op=ALU.mult),
                                reads=["Kk", "Bc"], writes=[("Kd", 1)])
                            if ln > 0:
                                P.op("act", lambda e, lo=lo, ln=ln, F=F: e.activation(
                                    out=Gl[:, lo:lo + ln], in_=F[:, lo:lo + ln], func=AF.Exp, scale=-1.0),
                                    reads=[("Fb", d), "Gl"], writes=["Gl"])
                                P.op("dve", lambda e, lo=lo, ln=ln, l0=l0: e.tensor_tensor(
                                    out=Qd[1][:, l0 - NCTX:l0 - NCTX + ln], in0=Qp[:, lo:lo + ln],
                                    in1=Gl[:, lo:lo + ln], op=ALU.mult), reads=["Qp", "Gl"], writes=[("Qd", 1)])
                for d in range(2):
                    P.op("act", lambda e, d=d: e.activation(out=HSC[:, d * 108:(d + 1) * 108],
                                                            in_=HSC[:, d * 108:(d + 1) * 108], func=AF.Exp),
                         reads=[("AL", d), ("BE", d), ("GA", d)], writes=[("AL", d), ("BE", d), ("GA", d)])
                for d in range(2):
                    for i4 in range(0, 18, 4):
                        nt_ = min(4, 18 - i4)
                        pb = nextps()
                        psb = PS[pb][:, 0:256].bitcast(BF16)
                        for j in range(nt_):
                            i = i4 + j
                            P.op("pe", lambda e, d=d, i=i, j=j, psb=psb: e.transpose(
                                psb[:, j * 128:(j + 1) * 128], Kd[d][:, i * 128:(i + 1) * 128], ident),
                                reads=[("Kd", d), "CB"], writes=[("ps", pb)])
                        P.op("act", lambda e, d=d, i4=i4, nt_=nt_, psb=psb: e.activation(
                            out=Ktok[d][:, i4:i4 + nt_, :],
                            in_=psb[:, 0:nt_ * 128].rearrange("p (i c) -> p i c", i=nt_), func=AF.Identity),
                            reads=[("ps", pb)], writes=[("Ktok", d)])
                osum = rv("SCR", TO, 512, F32, key="osum")
                rsb = rv("SCR", TO + 2048, 512, F32, key="rsb")
                rrb = rv("SCR", TO + 4096, 512, F32, key="rrb")
                ytmp = rv("SCR", TO + 6144, 512, F32, key="ytmp")
                sqb = rv("SCR", TO + 8192, 512, BF16, key="sqb")
                order = [list(range(36)), [3, 2, 1, 0] + list(range(35, 3, -1))]
                maskc = [CONST[:, CS("maskf")], CONST[:, CS("maskb")]]
                obank = {}
                def chunk_ctx(step, d):
                    n = order[d][step]
                    i, half = n // 2, n % 2
                    c = dict(n=n, i=i, pbase=half * 64, t0=n * 64, is_lat=(n >= 4), first=(step == 0),
                             last=(step == 35), Sd=HS[:, d * 128:(d + 1) * 128], Smd=HSm[:, d * 128:(d + 1) * 128],
                             kvb=6 + d, atb=4 + d, d=d, step=step)
                    if c["is_lat"]:
                        lt = c["t0"] - NCTX
                        c.update(lt=lt, blk=lt // 512, ob=d * 2 + ((lt // 512) % 2), oc=lt % 512)
                    return c

                def partA2(c):
                    d, i, pbase, kvb = c["d"], c["i"], c["pbase"], c["kvb"]
                    if not c["last"]:
                        P.op("pe", lambda e: e.matmul(
                            PS[kvb][:, 0:128], lhsT=Ktok[d][pbase:pbase + 64, i, :],
                            rhs=Vh[pbase:pbase + 64, i, :], start=True, stop=True),
                            reads=[("Ktok", d), "Vh1"], writes=[("ps", kvb)])

                def partA1(c):
                    d, pbase = c["d"], c["pbase"]
                    if c["is_lat"]:
                        t0, lt, atb = c["t0"], c["lt"], c["atb"]
                        P.op("pe", lambda e: e.matmul(
                            PS[atb][pbase:pbase + 64, 0:64], lhsT=Kd[d][:, t0:t0 + 64],
                            rhs=Qd[d][:, lt:lt + 64], start=True, stop=True),
                            reads=[("Kd", d), ("Qd", d)], writes=[("ps", atb)])

                def partB(c):
                    if not c["is_lat"]:
                        return
                    d, pbase, atb = c["d"], c["pbase"], c["atb"]
                    P.op("dve", lambda e: e.tensor_tensor(
                        out=ATm[pbase:pbase + 64, d * 64:(d + 1) * 64], in0=PS[atb][pbase:pbase + 64, 0:64],
                        in1=maskc[d][pbase:pbase + 64, :], op=ALU.mult),
                        reads=[("ps", atb), "CONST"], writes=[("ATm", d)])

                def partC1(c):
                    if not c["is_lat"]:
                        return
                    d, i, pbase, lt, ob, oc, blk, Smd = (c["d"], c["i"], c["pbase"], c["lt"], c["ob"], c["oc"],
                                                       c["blk"], c["Smd"])
                    P.op("pe", lambda e: e.matmul(
                        PS[ob][:, oc:oc + 64], lhsT=Vh[pbase:pbase + 64, i, :],
                        rhs=ATm[pbase:pbase + 64, d * 64:(d + 1) * 64], start=True, stop=False),
                        reads=["Vh1", ("ATm", d)], writes=[("ps", ob)])

                def partC2(c):
                    if not c["is_lat"]:
                        return
                    d, i, pbase, lt, ob, oc, blk, Smd = (c["d"], c["i"], c["pbase"], c["lt"], c["ob"], c["oc"],
                                                       c["blk"], c["Smd"])
                    P.op("pe", lambda e: e.matmul(
                        PS[ob][:, oc:oc + 64], lhsT=Smd, rhs=Qd[d][:, lt:lt + 64], start=False, stop=True),
                        reads=[("Sm", d), ("Qd", d)], writes=[("ps", ob)])
                    done = (oc == 448) if d == 0 else (oc == 0)
                    if not done:
                        return
                    b0 = blk * 512
                    if blk not in obank:
                        obank[blk] = d
                        P.op("act", lambda e: e.activation(
                            out=OF[:, b0:b0 + 512], in_=PS[ob][:, 0:512], func=AF.Identity),
                            reads=[("ps", ob)], writes=[("OF", blk)])
                        return
                    P.op("dve", lambda e: e.tensor_tensor(
                        out=osum[:, :], in0=PS[ob][:, 0:512], in1=OF[:, b0:b0 + 512], op=ALU.add),
                        reads=[("ps", ob), ("OF", blk)], writes=["osum"])
                    P.op("act", lambda e: e.activation(out=sqb[:, :], in_=osum[:, :], func=AF.Square),
                         reads=["osum"], writes=["sqb"])
                    P.op("pe", lambda e: e.matmul(PS[ob][:, 0:512], lhsT=ones, rhs=sqb[:, :], start=True, stop=True),
                         reads=["CB", "sqb"], writes=[("ps", ob)])
                    P.op("act", lambda e: e.activation(
                        out=rsb[:, :], in_=PS[ob][:, 0:512], func=AF.Ln, scale=1.0 / 128.0, bias=EPSB),
                        reads=[("ps", ob), "EPSB"], writes=["rsb"])
                    P.op("act", lambda e: e.activation(out=rrb[:, :], in_=rsb[:, :], func=AF.Exp, scale=-0.5),
                         reads=["rsb"], writes=["rrb"])
                    P.op("dve", lambda e: e.scalar_tensor_tensor(
                        out=ytmp[:, :], in0=osum[:, :], scalar=CONST[:, CS("hgnorm", h, 1)],
                        in1=rrb[:, :], op0=ALU.mult, op1=ALU.mult),
                        reads=["osum", "CONST", "rrb"], writes=["ytmp"])
                    P.op("dve", lambda e: e.tensor_tensor(
                        out=YH[:, b0:b0 + 512], in0=ytmp[:, :], in1=SG[:, b0:b0 + 512], op=ALU.mult),
                        reads=["ytmp", "SG"], writes=["YH"])

                def partD(c):
                    if c["last"]:
                        return
                    d, n, kvb, Sd, Smd, step = c["d"], c["n"], c["kvb"], c["Sd"], c["Smd"], c["step"]
                    if c["first"]:
                        P.op("dve", lambda e: e.tensor_scalar(
                            out=Sd, in0=PS[kvb][:, 0:128], scalar1=BE[d][:, n:n + 1], scalar2=None, op0=ALU.mult),
                            reads=[("ps", kvb), ("BE", d)], writes=[("S", d)])
                    else:
                        P.op("dve", lambda e: e.tensor_scalar(
                            out=Sd, in0=Sd, scalar1=AL[d][:, n:n + 1], scalar2=None, op0=ALU.mult),
                            reads=[("S", d), ("AL", d)], writes=[("S", d)])
                        P.op("dve", lambda e: e.scalar_tensor_tensor(
                            out=Sd, in0=PS[kvb][:, 0:128], scalar=BE[d][:, n:n + 1], in1=Sd,
                            op0=ALU.mult, op1=ALU.add),
                            reads=[("ps", kvb), ("BE", d), ("S", d)], writes=[("S", d)])
                    nn = order[d][step + 1]
                    if nn >= 4:
                        P.op("act", lambda e: e.activation(
                            out=Smd, in_=Sd, func=AF.Identity, scale=GA[d][:, nn:nn + 1]),
                            reads=[("S", d), ("GA", d)], writes=[("Sm", d)])

                for step in range(36 if "nochunk" not in stages else 0):
                    cs = [chunk_ctx(step, d) for d in range(2)]
                    for part in (partA1, partB, partA2, partC1, partC2, partD):
                        for c in cs:
                            part(c)
                if "nochunk" in stages:
                    P.op("act", lambda e: e.activation(out=YH[:, :], in_=SG[:, :], func=AF.Identity),
                         reads=["SG"], writes=["YH"])
                s_w = wload(wout1_d[h], D)
                resid_add(layer, lambda k, b0, bn: YH[:, b0 - NCTX:b0 - NCTX + bn], lambda k: ["YH"], s_w,
                          lambda k, m, s_w=s_w: WS[s_w][:, m * 128:(m + 1) * 128], 1, 2, blocks=LATBLK)

        stages = stop.split("+")

        if "mix0" in stages:
            mixer0()
        if "ffn0" in stages:
            ffn(0, [[(0, NCTX, 0, NCTX, 1), (NCTX, NT, NCTX, 896, 0)],
                    [(NCTX, NT, NCTX + 896, 1152, 0)]])
        if "mix1" in stages:
            mixer1()
        if "ffn1" in stages:
            ffn(1, [[(NCTX, NT, NCTX, 1024, 0)],
                    [(NCTX, NT, NCTX + 1024, 1024, 0)]])

        if "rawout" in stages:
            for c in range(KC):
                P.out_tokens.append(P.dma("sp", out_d[:, c, :], xT[:, c, NCTX:NT], reads=xk(c, NCTX, NLAT)))
        else:
            OUTB = [rv("SCR", i * 2048, 512, F32, key=("outb", i)) for i in range(2)]
            T = nrm_temps("SCR", SCR_BYTES - 10240, "S")
            it = 0
            for (b0, bn) in tok_blocks(NCTX, NT):
                rms_rstd(T, lambda c, b0=b0, bn=bn: xT[:, c, b0:b0 + bn],
                         lambda c, b0=b0, bn=bn: xk(c, b0, bn), KC, bn, float(D), 1)
                for c in range(KC):
                    b = it % 2
                    it += 1
                    P.op("dve", lambda e, c=c, b=b, b0=b0, bn=bn: e.scalar_tensor_tensor(
                        out=OUTB[b][:, :bn], in0=xT[:, c, b0:b0 + bn], scalar=CONST[:, CS("fnorm", c, 1)],
                        in1=T["rstd"][:, :bn], op0=ALU.mult, op1=ALU.mult),
                        reads=xk(c, b0, bn) + ["CONST", ("rstd", "S")], writes=[("outb", b)])
                    P.out_tokens.append(P.dma("sp", out_d[:, c, b0 - NCTX:b0 - NCTX + bn], OUTB[b][:, :bn],
                                              reads=[("outb", b)]))
        P.finish("sp", P.out_tokens)
    return nc


def _fm(v):
    v = np.asarray(v, np.float32)
    n = v.shape[-1] // 128
    return np.moveaxis(v.reshape(v.shape[:-1] + (n, 128)), -1, 0)


def _wk(w):
    K, N = w.shape
    return np.ascontiguousarray(w.reshape(K // 128, 128, N).transpose(1, 0, 2).reshape(128, (K // 128) * N))


def _rope_tables():
    rows_n = NLAT // GRID_W
    row = np.repeat(np.arange(rows_n, dtype=np.float32), GRID_W)
    col = np.tile(np.arange(GRID_W, dtype=np.float32), rows_n)
    n_freq = 16
    inv = (np.float32(10000.0) ** (-np.arange(n_freq, dtype=np.float32) / n_freq)).astype(np.float32)
    ang = np.stack([row[:, None] * inv, col[:, None] * inv], axis=1).astype(np.float32)
    cos, sin = np.cos(ang).astype(np.float32), np.sin(ang).astype(np.float32)
    CT = np.zeros((128, NLAT), np.float32)
    ST = np.zeros((128, NLAT), np.float32)
    for p in range(128):
        q = p % 64
        axis, half, f = q // 32, (q % 32) // 16, q % 16
        CT[p] = cos[:, axis, f]
        ST[p] = sin[:, axis, f] * (-1.0 if half == 0 else 1.0)
    return np.concatenate([CT, ST], axis=1)


def _swap_cols(w):
    N = w.shape[1]
    idx = np.arange(N)
    blk = idx // 32
    r = idx % 32
    return w[:, blk * 32 + (r + 16) % 32]


def prepare_inputs(inp):
    f32 = np.float32
    g = {k: np.asarray(v, f32) for k, v in inp.items()}
    shared = {}
    wm = g["w_mod"]
    shared["wmod"] = np.stack([np.stack([_wk(wm[l][:, p * 512:(p + 1) * 512]) for p in range(12)]) for l in range(2)])
    shared["rope"] = _rope_tables()
    wi = g["ev_w_in"][0]
    qw, kw, vw, uw = wi[:, 0:512], wi[:, 512:1024], wi[:, 1024:1536], wi[:, 1536:2048]
    qs, ks = _swap_cols(qw), _swap_cols(kw)
    heads = []
    for h in range(4):
        sl = slice(h * 128, (h + 1) * 128)
        heads.append(_wk(np.concatenate([qw[:, sl], qs[:, sl], kw[:, sl], ks[:, sl], vw[:, sl]], axis=1)))
    shared["whead"] = np.stack(heads)
    shared["wpool"] = _wk(uw)
    shared["wout0"] = np.ascontiguousarray(g["ev_w_out"][0].reshape(KC, 128, D))
    wu = g["ffn_w_up"]
    shared["wffn"] = np.stack([np.stack([
        _wk(np.concatenate([wu[l][:, c * 128:(c + 1) * 128], wu[l][:, DFF + c * 128: DFF + (c + 1) * 128]], axis=1))
        for c in range(NPAIR)]) for l in range(2)])
    wd = g["ffn_w_down"]
    shared["wdn"] = np.stack([np.stack([
        np.ascontiguousarray(wd[l][:, m * 128:(m + 1) * 128].reshape(NPAIR, 128, 128).transpose(1, 0, 2)
                             .reshape(128, NPAIR * 128)) for m in range(KC)]) for l in range(2)])
    hw = g["hg_w_in"][0]
    hh = []
    for h in range(8):
        cols = [hw[:, part * 1024 + h * 128: part * 1024 + (h + 1) * 128] for part in range(5)]
        hh.append(_wk(np.concatenate(cols, axis=1)))
    shared["whg"] = np.stack(hh)
    shared["wout1"] = np.ascontiguousarray(g["hg_w_out"][0].reshape(KC, 128, D))
    cb = np.zeros((128, 256 + 512), f32)
    cb[:, 0:128] = np.eye(128, dtype=f32)
    cb[:, 128:256] = 1.0
    cb[:, 256:768] = g["pool_w"][0].transpose(1, 0, 2).reshape(128, 512)
    shared["constb"] = cb

    const = np.zeros((128, NCONST), f32)
    bm = g["b_mod"]
    bmT = np.stack([_fm(bm[l]) for l in range(2)], axis=1)
    const[:, CS("bmod")] = np.repeat(bmT.reshape(128, 96), 2, axis=1).reshape(128, 192)
    lq = np.concatenate([g["da_lq1"][0], g["da_lk1"][0], g["da_lq2"][0], g["da_lk2"][0]])
    const[:, CS("lqk")] = lq[None, :]
    const[:, CS("subln")] = _fm(g["da_subln"][0])
    const[:, CS("pscale")] = _fm(g["pool_scale"][0])
    const[:, CS("hgnorm")] = _fm(g["hg_norm"][0])
    const[:, CS("fnorm")] = _fm(g["final_norm"])
    cwT = _fm(g["ffn_conv_w"])
    const[:, CS("convw")] = cwT.transpose(0, 1, 3, 2).reshape(128, -1)
    const[:, CS("convb")] = _fm(g["ffn_conv_b"]).reshape(128, -1)
    const[:, CS("lblog")] = _fm(g["hg_lb_logits"]).reshape(128, -1)
    pe = np.zeros((4, 4, 8), f32)
    for gi, w in enumerate((2, 4, 8, 16)):
        hw_ = w // 2
        for ei, T in enumerate((NCTX, NCTX, NLAT, NLAT)):
            for i in range(hw_):
                t = i if ei % 2 == 0 else T - hw_ + i
                cnt = min(t + w - hw_, T) - max(t - hw_, 0)
                pe[gi, ei, i] = 1.0 / cnt
    const[:, CS("pedge")] = pe.reshape(1, -1)
    s_idx = np.arange(128)[:, None] % 64
    t_idx = np.arange(64)[None, :]
    const[:, CS("maskf")] = (t_idx >= s_idx).astype(f32)
    const[:, CS("maskb")] = (t_idx <= s_idx).astype(f32)
    const[:, CS("m01")] = (np.arange(512) % 64 != 0).astype(f32)[None, :]

    cctx = _fm(g["c_ctx"])
    in_maps = []
    for b in range(NCORES):
        cst = const.copy()
        cst[:, CS("cT", 0, 8)] = _fm(g["c"][b])
        cst[:, CS("cT", 8, 8)] = cctx
        tok = np.concatenate([g["ctx"][b], g["x"][b]], axis=0)
        xTh = np.ascontiguousarray(tok.T.reshape(KC, 128, NT).transpose(1, 0, 2))
        m = dict(shared)
        m["const"] = cst
        m["xT"] = xTh
        in_maps.append(m)
    return in_maps


_CACHE = {}


def run(inputs, stop="full", trace=False, cores=None):
    in_maps = prepare_inputs(inputs)
    if cores is not None:
        in_maps = [in_maps[b] for b in cores]
    if stop not in _CACHE:
        _CACHE[stop] = build_program(stop)
    nc = _CACHE[stop]
    res = run_bass_kernel_spmd(nc, in_maps, core_ids=list(range(len(in_maps))), trace=trace)
    outs = []
    for r in res.results:
        o = np.asarray(r["outT"], np.float32)
        outs.append(o.transpose(2, 1, 0).reshape(NLAT, D))
    return np.stack(outs), res


def kernel(**inputs):
    out, _ = run(inputs, stop="mix0+ffn0+mix1+ffn1")
    return out
```
